# Optimizing a Trainium2 kernel written in Bass

```python
import jax, jax.numpy as jnp
from jax import lax
import numpy as np

D_MODEL = 1024
BATCH = 8
SEQ = 8192
DEPTH = 2

CTX_LEN = 256
GRID_W = 64
W_BRANCH = 512
N_BRANCH = 3
CHUNK = 128
A_GROUPS = 4
A_GW = W_BRANCH // A_GROUPS
B_BLOCKS = 8
B_BW = W_BRANCH // B_BLOCKS
CONV_W = 4
CONV_PAD_L = 2
LRU_C = 8.0
C_HEADS = 8
C_HD = W_BRANCH // C_HEADS
WIN_R = 8
WIN_C = 16
ROPE_BASE = 10000.0
IN_SPLITS = (W_BRANCH,) * 9 + (N_BRANCH * D_MODEL,)
N_IN = 9 * W_BRANCH + N_BRANCH * D_MODEL
ALPHA = (2 * DEPTH) ** 0.25
BETA = (8 * DEPTH) ** -0.25
LN_EPS = 1e-5

kernel_name = "hybrid_gmlp_rglru_natten_deepnorm"


def layer_norm(x, g, b):
    xf = x.astype(jnp.float32)
    mu = jnp.mean(xf, -1, keepdims=True)
    var = jnp.mean(jnp.square(xf - mu), -1, keepdims=True)
    y = (xf - mu) * lax.rsqrt(var + LN_EPS) * g.astype(jnp.float32) + b.astype(jnp.float32)
    return y.astype(x.dtype)


def split_cols(z):
    idx = [int(i) for i in np.cumsum(IN_SPLITS)[:-1]]
    return jnp.split(z, idx, axis=-1)


def heads(z):
    return z.reshape(*z.shape[:-1], C_HEADS, C_HD)


def chunk_sgu(u, v, ln_g, ln_b, w_s, b_s):
    bsz, L, _ = v.shape
    v = layer_norm(v, ln_g, ln_b).reshape(bsz, L // CHUNK, CHUNK, A_GROUPS, A_GW)
    v = jnp.einsum("gpq,bnqgc->bnpgc", w_s, v) + b_s.T[:, :, None]
    return u * v.reshape(bsz, L, W_BRANCH)


def depthwise_conv_centred(x, w, b):
    L = x.shape[1]
    xp = jnp.pad(x, ((0, 0), (CONV_PAD_L, CONV_W - 1 - CONV_PAD_L), (0, 0)))
    return b + sum(xp[:, j:j + L] * w[j] for j in range(CONV_W))


def rglru_coeffs(x, wa, ba, wx, bx, lam):
    bsz, L, _ = x.shape
    xb = x.reshape(bsz, L, B_BLOCKS, B_BW)
    r = jax.nn.sigmoid(jnp.einsum("blhi,hij->blhj", xb, wa).reshape(bsz, L, W_BRANCH) + ba)
    i = jax.nn.sigmoid(jnp.einsum("blhi,hij->blhj", xb, wx).reshape(bsz, L, W_BRANCH) + bx)
    log_a = -LRU_C * r.astype(jnp.float32) * jax.nn.softplus(-lam.astype(jnp.float32))
    a = jnp.exp(log_a)
    b = jnp.sqrt(-jnp.expm1(2.0 * log_a)) * (i * x).astype(jnp.float32)
    return a, b


def linear_scan(a, b, h0, reverse):
    idx = -1 if reverse else 0
    b = b.at[:, idx].add(a[:, idx] * h0)

    def combine(left, right):
        al, bl = left
        ar, br = right
        return ar * al, ar * bl + br

    _, h = lax.associative_scan(combine, (a, b), reverse=reverse, axis=1)
    return h


def rglru_bidir(x_lat, x_ctx, conv_w, conv_b, wa, ba, wx, bx, lam, with_ctx):
    xl = depthwise_conv_centred(x_lat, conv_w, conv_b)
    xc = depthwise_conv_centred(x_ctx, conv_w, conv_b)
    h0 = jnp.zeros((x_lat.shape[0], W_BRANCH), jnp.float32)
    ys_lat, ys_ctx = [], []
    for d, rev in enumerate((False, True)):
        a_c, b_c = rglru_coeffs(xc, wa[d], ba[d], wx[d], bx[d], lam[d])
        h_c = linear_scan(a_c, b_c, h0, rev)
        h_fin = h_c[:, 0] if rev else h_c[:, -1]
        a_l, b_l = rglru_coeffs(xl, wa[d], ba[d], wx[d], bx[d], lam[d])
        ys_lat.append(linear_scan(a_l, b_l, h_fin, rev))
        if with_ctx:
            ys_ctx.append(h_c)
    y_lat = (ys_lat[0] + ys_lat[1]).astype(x_lat.dtype)
    y_ctx = (ys_ctx[0] + ys_ctx[1]).astype(x_ctx.dtype) if with_ctx else None
    return y_lat, y_ctx


def rope_2d(x, rows, cols):
    half = C_HD // 2
    quarter = half // 2
    inv_freq = ROPE_BASE ** (-jnp.arange(quarter, dtype=jnp.float32) / quarter)

    def rotate(xa, p):
        ang = p.astype(jnp.float32)[:, None] * inv_freq
        cos = jnp.cos(ang)[None, :, None, :]
        sin = jnp.sin(ang)[None, :, None, :]
        xa = xa.astype(jnp.float32)
        x1, x2 = xa[..., :quarter], xa[..., quarter:]
        return jnp.concatenate([x1 * cos - x2 * sin, x1 * sin + x2 * cos], -1)

    out = jnp.concatenate([rotate(x[..., :half], rows), rotate(x[..., half:], cols)], -1)
    return out.astype(x.dtype)


def neighbourhood_attention(q, k, v, k_ctx, v_ctx, rpb):
    bsz, L, nh, hd = q.shape
    rows = L // GRID_W
    wr = min(WIN_R, rows)
    scale = hd ** -0.5
    qg = q.reshape(bsz, rows, GRID_W, nh, hd)
    kg = k.reshape(bsz, rows, GRID_W, nh, hd)
    vg = v.reshape(bsz, rows, GRID_W, nh, hd)
    qc = jnp.arange(GRID_W)
    c0 = jnp.clip(qc - WIN_C // 2, 0, GRID_W - WIN_C)
    col_idx = c0[:, None] + jnp.arange(WIN_C)[None, :]
    col_bias_idx = col_idx - qc[:, None] + (WIN_C - 1)

    def row_block(r):
        r0 = jnp.clip(r - wr // 2, 0, rows - wr)
        k_rows = lax.dynamic_slice_in_dim(kg, r0, wr, axis=1)
        v_rows = lax.dynamic_slice_in_dim(vg, r0, wr, axis=1)
        k_win = k_rows[:, :, col_idx]
        v_win = v_rows[:, :, col_idx]
        q_r = lax.dynamic_index_in_dim(qg, r, axis=1, keepdims=False)
        s_loc = jnp.einsum("bqhd,bsqchd->bhqsc", q_r, k_win).astype(jnp.float32) * scale
        row_off = r0 + jnp.arange(wr) - r + (WIN_R - 1)
        bias = rpb[:, row_off[None, :, None], col_bias_idx[:, None, :]]
        s_loc = s_loc + bias[None].astype(jnp.float32)
        s_ctx = jnp.einsum("bqhd,bkhd->bhqk", q_r, k_ctx).astype(jnp.float32) * scale
        s = jnp.concatenate([s_loc.reshape(bsz, nh, GRID_W, wr * WIN_C), s_ctx], -1)
        p = jax.nn.softmax(s, -1).astype(v.dtype)
        p_loc = p[..., :wr * WIN_C].reshape(bsz, nh, GRID_W, wr, WIN_C)
        p_ctx = p[..., wr * WIN_C:]
        return (jnp.einsum("bhqsc,bsqchd->bqhd", p_loc, v_win)
                + jnp.einsum("bhqk,bkhd->bqhd", p_ctx, v_ctx))

    out = lax.map(row_block, jnp.arange(rows))
    return out.transpose(1, 0, 2, 3, 4).reshape(bsz, L, nh * hd)


def context_attention(q, k, v):
    s = jnp.einsum("bqhd,bkhd->bhqk", q, k).astype(jnp.float32) * C_HD ** -0.5
    p = jax.nn.softmax(s, -1).astype(v.dtype)
    o = jnp.einsum("bhqk,bkhd->bqhd", p, v)
    return o.reshape(*o.shape[:2], W_BRANCH)


def merge_project(x, ys, g_m, gate, w_br, w_out, ln_g, ln_b):
    g = jax.nn.sigmoid(g_m)
    m = sum(g[..., n * D_MODEL:(n + 1) * D_MODEL] * (ys[n] @ w_br[n]) for n in range(N_BRANCH))
    return layer_norm(ALPHA * x + gate * (m @ w_out), ln_g, ln_b)


def setup_inputs(seed: int = 0) -> dict:
    key = jax.random.key(seed)
    ks = jax.random.split(key, 32)
    f32 = jnp.float32
    nrm = lambda k, shape, s: (jax.random.normal(k, shape, f32) * s).astype(f32)
    a0 = jax.random.uniform(ks[20], (DEPTH, 2, W_BRANCH), f32, minval=0.9, maxval=0.999)
    a_base = a0 ** (1.0 / LRU_C)
    lam = jnp.log(a_base) - jnp.log1p(-a_base)
    return {
        "x": nrm(ks[0], (BATCH, SEQ, D_MODEL), 1.0),
        "c": nrm(ks[1], (BATCH, D_MODEL), 1.0),
        "ctx": nrm(ks[2], (BATCH, CTX_LEN, D_MODEL), 1.0),
        "c_ctx": nrm(ks[3], (D_MODEL,), 1.0),
        "w_ada": nrm(ks[4], (DEPTH, D_MODEL, 3 * D_MODEL), 0.3 * D_MODEL ** -0.5),
        "b_ada": nrm(ks[5], (DEPTH, 3 * D_MODEL), 0.02),
        "w_in": nrm(ks[6], (DEPTH, D_MODEL, N_IN), D_MODEL ** -0.5),
        "b_in": nrm(ks[7], (DEPTH, N_IN), 0.02),
        "sgu_ln_g": 1.0 + nrm(ks[8], (DEPTH, W_BRANCH), 0.02),
        "sgu_ln_b": nrm(ks[9], (DEPTH, W_BRANCH), 0.02),
        "w_s": nrm(ks[10], (DEPTH, A_GROUPS, CHUNK, CHUNK), CHUNK ** -0.5),
        "b_s": 1.0 + nrm(ks[11], (DEPTH, A_GROUPS, CHUNK), 0.1),
        "conv_w": nrm(ks[12], (DEPTH, CONV_W, W_BRANCH), CONV_W ** -0.5),
        "conv_b": nrm(ks[13], (DEPTH, W_BRANCH), 0.02),
        "lru_wa": nrm(ks[14], (DEPTH, 2, B_BLOCKS, B_BW, B_BW), B_BW ** -0.5),
        "lru_ba": nrm(ks[15], (DEPTH, 2, W_BRANCH), 0.02),
        "lru_wx": nrm(ks[16], (DEPTH, 2, B_BLOCKS, B_BW, B_BW), B_BW ** -0.5),
        "lru_bx": nrm(ks[17], (DEPTH, 2, W_BRANCH), 0.02),
        "lru_lam": lam.astype(f32),
        "rpb": nrm(ks[18], (DEPTH, C_HEADS, 2 * WIN_R - 1, 2 * WIN_C - 1), 0.1),
        "w_br": nrm(ks[19], (DEPTH, N_BRANCH, W_BRANCH, D_MODEL), BETA * W_BRANCH ** -0.5),
        "w_out": nrm(ks[21], (DEPTH, D_MODEL, D_MODEL), BETA * D_MODEL ** -0.5),
        "ln_g": 1.0 + nrm(ks[22], (DEPTH, D_MODEL), 0.02),
        "ln_b": nrm(ks[23], (DEPTH, D_MODEL), 0.02),
    }


def reference(x, c, ctx, c_ctx, w_ada, b_ada, w_in, b_in, sgu_ln_g, sgu_ln_b, w_s, b_s,
              conv_w, conv_b, lru_wa, lru_ba, lru_wx, lru_bx, lru_lam, rpb, w_br, w_out,
              ln_g, ln_b):
    L = x.shape[1]
    pos = jnp.arange(L)
    rows_pos, cols_pos = pos // GRID_W, pos % GRID_W
    xc = ctx
    sc = jax.nn.silu(c)
    scc = jax.nn.silu(c_ctx)
    gelu = lambda t: jax.nn.gelu(t, approximate=False)
    for l in range(DEPTH):
        with_ctx = l < DEPTH - 1
        shift, scale, gate = jnp.split(sc @ w_ada[l] + b_ada[l], 3, axis=-1)
        shift_c, scale_c, gate_c = jnp.split(scc @ w_ada[l] + b_ada[l], 3, axis=-1)
        u = x * (1.0 + scale[:, None]) + shift[:, None]
        uc = xc * (1.0 + scale_c) + shift_c
        a_u, a_v, a_g, b_x, b_g, c_q, c_k, c_v, c_g, g_m = split_cols(u @ w_in[l] + b_in[l])
        a_uc, a_vc, a_gc, b_xc, b_gc, c_qc, c_kc, c_vc, c_gc, g_mc = split_cols(uc @ w_in[l] + b_in[l])

        y_a = chunk_sgu(gelu(a_u), gelu(a_v), sgu_ln_g[l], sgu_ln_b[l], w_s[l], b_s[l]) * jax.nn.silu(a_g)
        y_b, y_bc = rglru_bidir(b_x, b_xc, conv_w[l], conv_b[l], lru_wa[l], lru_ba[l],
                                lru_wx[l], lru_bx[l], lru_lam[l], with_ctx)
        y_b = y_b * jax.nn.silu(b_g)
        q = rope_2d(heads(c_q), rows_pos, cols_pos)
        k = rope_2d(heads(c_k), rows_pos, cols_pos)
        k_ctx, v_ctx = heads(c_kc), heads(c_vc)
        y_c = neighbourhood_attention(q, k, heads(c_v), k_ctx, v_ctx, rpb[l]) * jax.nn.silu(c_g)

        x_new = merge_project(x, (y_a, y_b, y_c), g_m, gate[:, None], w_br[l], w_out[l], ln_g[l], ln_b[l])
        if with_ctx:
            y_ac = chunk_sgu(gelu(a_uc), gelu(a_vc), sgu_ln_g[l], sgu_ln_b[l], w_s[l], b_s[l]) * jax.nn.silu(a_gc)
            y_bc = y_bc * jax.nn.silu(b_gc)
            y_cc = context_attention(heads(c_qc), k_ctx, v_ctx) * jax.nn.silu(c_gc)
            xc = merge_project(xc, (y_ac, y_bc, y_cc), g_mc, gate_c, w_br[l], w_out[l], ln_g[l], ln_b[l])
        x = x_new
    return x
```

```python
import numpy as np
import concourse.bass as bass
import concourse.mybir as mybir
from concourse.bass_utils import run_bass_kernel_spmd

F32 = mybir.dt.float32
BF16 = mybir.dt.bfloat16
AF = mybir.ActivationFunctionType
ALU = mybir.AluOpType

D = 1024
SEQ = 8192
CTX = 256
NTOK = SEQ + CTX
DEPTH = 2
NB = 8
ALPHA = (2 * DEPTH) ** 0.25
LN_EPS = 1e-5
NEG = -30000.0
NPIECE = 25
PSZ = 4096

ENGS = ("pe", "act", "dve", "pool", "sp")


class V:
    def __init__(self, ap, keys):
        self.ap = ap
        self.keys = list(keys)


def _keys(items):
    out = []
    for it in items:
        if isinstance(it, V):
            out.extend(it.keys)
        else:
            out.append(it)
    return out


class Prog:
    def __init__(self, nc):
        self.nc = nc
        self.q = {e: [] for e in ENGS}
        self.esem = {e: nc.alloc_semaphore("s_" + e) for e in ("pe", "act", "dve", "pool")}
        self.ecnt = {e: 0 for e in self.esem}
        self.dsem = {}
        self.last_w = {}
        self.readers = {}
        self.waited = {e: {} for e in ENGS}
        self.n_ops = 0
        self.ps_i = 0

    def _deps(self, eng, reads, writes):
        toks = {}

        def add(t):
            if t is None:
                return
            s, v = t
            k = id(s)
            if k not in toks or toks[k][1] < v:
                toks[k] = (s, v)

        for r in reads:
            add(self.last_w.get(r))
        for w in writes:
            add(self.last_w.get(w))
            for t in self.readers.get(w, {}).values():
                add(t)
        out = []
        wd = self.waited[eng]
        for k, (s, v) in toks.items():
            if wd.get(k, 0) >= v:
                continue
            wd[k] = v
            out.append((s, v))
        return out

    def _commit(self, tok, reads, writes):
        for w in writes:
            self.last_w[w] = tok
            self.readers[w] = {}
        for r in reads:
            self.readers.setdefault(r, {})[id(tok[0])] = tok

    def op(self, eng, fn, r=(), w=()):
        reads, writes = _keys(r), _keys(w)
        waits = self._deps(eng, reads, writes)
        sem = self.esem[eng]
        self.ecnt[eng] += 1
        tok = (sem, self.ecnt[eng])

        def run(e, waits=waits, fn=fn, sem=sem):
            for s, v in waits:
                e.wait_ge(s, v)
            fn(e).then_inc(sem, 1)

        self.q[eng].append(run)
        self._commit(tok, reads, writes)
        self.n_ops += 1
        return tok

    def dma(self, eng, out, in_, r=(), w=(), semkey=None):
        reads, writes = _keys(r), _keys(w)
        if semkey is None:
            semkey = next(k for k in (writes + reads) if isinstance(k, str) or k[0] == "pg")
        if semkey not in self.dsem:
            self.dsem[semkey] = [self.nc.alloc_semaphore("d%d" % len(self.dsem)), 0]
        ent = self.dsem[semkey]
        waits = self._deps(eng, reads, writes)
        ent[1] += 16
        tok = (ent[0], ent[1])

        def run(e, waits=waits, sem=ent[0], out=out, in_=in_):
            for s, v in waits:
                e.wait_ge(s, v)
            e.dma_start(out=out, in_=in_).then_inc(sem, 16)

        self.q[eng].append(run)
        self._commit(tok, reads, writes)
        self.n_ops += 1
        return tok

    def finish(self, eng="sp"):
        ents = [tuple(v) for v in self.dsem.values()]
        ecs = [(self.esem[e], self.ecnt[e]) for e in self.esem if self.ecnt[e] > 0]

        def run(e):
            for sem, cnt in ents:
                e.wait_ge(sem, cnt)
            for sem, cnt in ecs:
                e.wait_ge(sem, cnt)

        self.q[eng].append(run)

    def emit(self):
        with self.nc.Block() as block:
            @block.tensor
            def _(e):
                for f in self.q["pe"]:
                    f(e)

            @block.scalar
            def _(e):
                for f in self.q["act"]:
                    f(e)

            @block.vector
            def _(e):
                for f in self.q["dve"]:
                    f(e)

            @block.gpsimd
            def _(e):
                for f in self.q["pool"]:
                    f(e)

            @block.sync
            def _(e):
                for f in self.q["sp"]:
                    f(e)


def _partner():
    d = np.arange(64)
    return np.where(d < 16, d + 16, np.where(d < 32, d - 16, np.where(d < 48, d + 16, d - 16)))


def _piece_cols():
    ar = np.arange(512)
    permcols = (np.arange(8)[:, None] * 64 + _partner()[None, :]).reshape(-1)
    pcs = [(1536 + ar, "fm"), (3072 + ar, "fm"), (3072 + permcols, "fm"), (3584 + ar, "wide"),
           (0 + ar, "fm"), (512 + ar, "wide"), (1024 + ar, "fm"), (2048 + ar, "fm"),
           (2560 + ar, "fm"), (2560 + permcols, "fm"), (4096 + ar, "fm")]
    for hh in range(2):
        for n in range(3):
            pcs.append((4608 + n * 1024 + hh * 512 + ar, "fm"))
    return pcs


P_BX, P_K, P_KP, P_V, P_AU, P_AV, P_AG, P_BG, P_Q, P_QP, P_CG = range(11)
P_GM = 11
P_WBR = 17
P_WOUT = 23


def _fm(Wc):
    return np.ascontiguousarray(Wc.reshape(8, 128, 4, 128).transpose(1, 2, 0, 3)).reshape(128, PSZ)


def _wide(Wc):
    return np.ascontiguousarray(Wc.reshape(8, 128, 512).transpose(1, 0, 2)).reshape(128, PSZ)


def _col(v, n):
    return np.ascontiguousarray(np.asarray(v).reshape(n, 128).T)


def prep_shared(inp):
    f32 = np.float32
    pcs = _piece_cols()
    sh = {}
    wcat = np.zeros((DEPTH, NPIECE, 128, PSZ), f32)
    wada = np.zeros((DEPTH, 6, 128, PSZ), f32)
    bcol = np.zeros((DEPTH, 128, 17 * 4), f32)
    rows = np.zeros((DEPTH, 9, 1024), f32)
    wsT = np.zeros((DEPTH, 128, 4, 128), f32)
    lruW = np.zeros((DEPTH, 128, 2, 4, 2, 128), f32)
    lcol = np.zeros((DEPTH, 128, 2, 4, 3), f32)
    ccol = np.zeros((DEPTH, 128, 4, 5), f32)
    adac = np.zeros((DEPTH, 128, 16), f32)
    btab = np.full((DEPTH, 128, 14, 512), NEG, f32)
    qc = np.arange(64)
    c0 = np.clip(qc - 8, 0, 48)
    kc = np.arange(64)
    valid = (kc[:, None] >= c0[None, :]) & (kc[:, None] < c0[None, :] + 16)
    bidx = np.clip(kc[:, None] - qc[None, :] + 15, 0, 30)
    for l in range(DEPTH):
        w_in, b_in = inp["w_in"][l], inp["b_in"][l]
        for i, (cols, kind) in enumerate(pcs):
            wcat[l, i] = _fm(w_in[:, cols]) if kind == "fm" else _wide(w_in[:, cols])
            if kind == "fm":
                bcol[l, :, i * 4:(i + 1) * 4] = _col(b_in[cols], 4)
        for hh in range(2):
            for n in range(3):
                wb = inp["w_br"][l][n][:, hh * 512:(hh + 1) * 512]
                wcat[l, P_WBR + hh * 3 + n, :, :2048] = np.ascontiguousarray(
                    wb.reshape(4, 128, 512).transpose(1, 0, 2)).reshape(128, 2048)
        for half in range(2):
            wcat[l, P_WOUT + half] = _wide(inp["w_out"][l][:, half * 512:(half + 1) * 512])
        wa = inp["w_ada"][l]
        for i in range(4):
            wada[l, i] = _fm(wa[:, i * 512:(i + 1) * 512])
        for i in range(2):
            wada[l, 4 + i] = _wide(wa[:, 2048 + i * 512:2048 + (i + 1) * 512])
        adac[l] = _col(inp["b_ada"][l][:2048], 16)
        rows[l, 0] = inp["b_ada"][l][2048:]
        rows[l, 1] = inp["ln_g"][l]
        rows[l, 2] = inp["ln_b"][l]
        rows[l, 3, :512] = b_in[512:1024]
        rows[l, 4, :512] = b_in[3584:4096]
        rows[l, 5, :512] = inp["sgu_ln_g"][l]
        rows[l, 6, :512] = inp["sgu_ln_b"][l]
        rows[l, 7, :512] = inp["b_s"][l].reshape(-1)
        wsT[l] = inp["w_s"][l].transpose(2, 0, 1)
        for d in range(2):
            for ct in range(4):
                for wi, nm in enumerate(("lru_wa", "lru_wx")):
                    for e in range(2):
                        lruW[l, e * 64:(e + 1) * 64, d, ct, wi, e * 64:(e + 1) * 64] = inp[nm][l][d][2 * ct + e]
            lcol[l, :, d, :, 0] = _col(inp["lru_ba"][l][d], 4)
            lcol[l, :, d, :, 1] = _col(inp["lru_bx"][l][d], 4)
            lcol[l, :, d, :, 2] = _col(inp["lru_lam"][l][d], 4)
        for j in range(4):
            ccol[l, :, :, j] = _col(inp["conv_w"][l][j], 4)
        ccol[l, :, :, 4] = _col(inp["conv_b"][l], 4)
        rpb = inp["rpb"][l]
        for m in range(14):
            for e in range(2):
                g = rpb[:, m + e][:, bidx]
                g = np.where(valid[None], g, f32(NEG))
                g = g.reshape(4, 2, 64, 64).transpose(2, 1, 0, 3)
                btab[l, e * 64:(e + 1) * 64, m] = g.reshape(64, 512)
    inv_freq = (f32(10000.0) ** (-np.arange(16, dtype=f32) / f32(16))).astype(f32)
    pos = np.arange(SEQ)
    rw, cl = (pos // 64).astype(f32), (pos % 64).astype(f32)
    cosT = np.ones((128, NTOK), f32)
    sinT = np.zeros((128, NTOK), f32)
    for d in range(64):
        p = rw if d < 32 else cl
        ang = (p * inv_freq[d % 16]).astype(f32)
        sgn = -1.0 if (d % 32) < 16 else 1.0
        for e in range(2):
            cosT[e * 64 + d, :SEQ] = np.cos(ang)
            sinT[e * 64 + d, :SEQ] = sgn * np.sin(ang)
    sh.update(wcat=wcat, wada=wada, bcol=bcol, rows=rows, wsT=wsT, lruW=lruW, lcol=lcol, ccol=ccol,
              adac=adac, btab=btab, cosT=cosT, sinT=sinT, ident=np.eye(128, dtype=f32))
    return sh


def prep_core(inp, b):
    cvec = np.zeros((128, 8, 2), np.float32)
    cvec[:, :, 0] = _col(inp["c"][b], 8)
    cvec[:, :, 1] = _col(inp["c_ctx"], 8)
    return {"x": np.ascontiguousarray(inp["x"][b]), "ctx": np.ascontiguousarray(inp["ctx"][b]), "cvec": cvec}


ARF = 14336
PG = 512


class _Stop(Exception):
    pass


BF = True
NWB = 3


def build(layers=(0, 1), dbg=(), stop=None):
    WDT = BF16 if BF else F32
    nc = bass.Bass("TRN2", target_bir_lowering=False)
    P = Prog(nc)

    def din(name, shape, dt=F32):
        return nc.dram_tensor(name, list(shape), dt, kind="ExternalInput").ap()

    def dscr(name, shape, dt, out=False):
        kind = "ExternalOutput" if (out or name in dbg) else "Internal"
        return nc.dram_tensor(name, list(shape), dt, kind=kind).ap()

    x_in = din("x", [SEQ, D]); ctx_in = din("ctx", [CTX, D]); cvec_in = din("cvec", [128, 8, 2])
    wcat = din("wcat", [DEPTH, NPIECE, 128, PSZ]); wada = din("wada", [DEPTH, 6, 128, PSZ])
    bcol_in = din("bcol", [DEPTH, 128, 68]); rows_in = din("rows", [DEPTH, 9, 1024])
    wsT_in = din("wsT", [DEPTH, 128, 512]); lruW_in = din("lruW", [DEPTH, 128, 2048])
    lcol_in = din("lcol", [DEPTH, 128, 24]); ccol_in = din("ccol", [DEPTH, 128, 20])
    adac_in = din("adac", [DEPTH, 128, 16]); btab_in = din("btab", [DEPTH, 128, 14 * 512])
    cosT_in = din("cosT", [128, NTOK]); sinT_in = din("sinT", [128, NTOK]); ident_in = din("ident", [128, 128])

    out_d = dscr("out", [SEQ, D], F32, out=True)
    wbf = dscr("wbf", [DEPTH, NPIECE, 128, PSZ], BF16) if BF else None
    bxT = dscr("bxT", [512, NTOK], F32)
    KT = dscr("KT", [512, NTOK], BF16)
    Vs = dscr("Vs", [NTOK, 512], BF16)
    hfs = dscr("hfs", [512, SEQ], F32)
    ybT = dscr("ybT", [512, NTOK], BF16)
    x1 = dscr("x1", [SEQ, D], F32)
    xc1 = dscr("xc1", [CTX, D], F32)

    def S(name, shape, dt=F32):
        return nc.alloc_sbuf_tensor("sb_" + name, list(shape), dt)

    AR = S("AR", [128, ARF])
    bcol = S("bcol", [128, 68]); adac = S("adac", [128, 16]); lcol = S("lcol", [128, 2, 4, 3]); ccol = S("ccol", [128, 4, 5])
    wsT = S("wsT", [128, 4, 128]); lruW = S("lruW", [128, 2, 4, 2, 128]); ident = S("ident", [128, 128])
    ones_bf = S("ones_bf", [128, 128], BF16)
    gate_bc = S("gate_bc", [128, 2, 1024]); lng_bc = S("lng_bc", [128, 1024]); lnb_bc = S("lnb_bc", [128, 1024])
    bav_bc = S("bav_bc", [128, 512]); bv_bc = S("bv_bc", [128, 512]); sg_bc = S("sg_bc", [128, 512])
    sb_bc = S("sb_bc", [128, 512]); bs_bc = S("bs_bc", [128, 512])
    btab = S("btab", [128, 14, 512], BF16)
    kctx = S("kctx", [128, 4, 256], BF16); vctx = S("vctx", [128, 2, 512], BF16)
    cv = S("cv", [128, 8, 2]); scv = S("scv", [128, 8, 2]); modc = S("modc", [128, 16, 2]); lcc = S("lcc", [128, 2, 4])
    ltmp = S("ltmp", [128, 2, 4])
    uT = S("uT", [128, 8, 512], WDT)
    wb = [S("wb%d" % i, [128, PSZ], WDT) for i in range(NWB)]
    yaT = S("yaT", [128, 4, 512], WDT); ybs = S("ybs", [128, 4, 512], WDT); ycs = S("ycs", [128, 4, 512], WDT)
    mT = S("mT", [128, 8, 512], WDT)
    carry = S("carry", [128, 1]); hcf = S("hcf", [128, 256])
    st6 = S("st6", [128, 12]); mv2 = S("mv2", [128, 2]); rs1 = S("rs1", [128, 1])
    psum2 = [nc.alloc_psum_tensor("ps%d" % i, [128, 1024], F32) for i in range(4)]

    def ps():
        i = P.ps_i % 8
        P.ps_i += 1
        return V(psum2[i // 2][:, (i % 2) * 512:(i % 2 + 1) * 512], ["ps%d" % i])

    def psp():
        if P.ps_i % 2:
            P.ps_i += 1
        i = P.ps_i % 8
        P.ps_i += 2
        return V(psum2[i // 2][:, :], ["ps%d" % i, "ps%d" % (i + 1)])

    def av(off_pg, shape, dt=F32, off=0):
        n = int(np.prod(shape[1:]))
        nf = n if dt == F32 else n // 2
        o = off_pg * PG + off
        ap = AR[:, o:o + nf]
        if dt != F32:
            ap = ap.bitcast(BF16)
        if len(shape) == 3:
            ap = ap.rearrange("p (a b) -> p a b", a=shape[1])
        elif len(shape) == 4:
            ap = ap.rearrange("p (a b c) -> p a b c", a=shape[1], b=shape[2])
        v = V(ap, [("pg", i) for i in range(o // PG, (o + nf - 1) // PG + 1)])
        v.o = o
        v.nf = nf
        return v

    def rev_ap(v, n):
        return bass.AP(AR, v.o + n - 1, [[ARF, 128], [-1, n]])

    wslot = [0]

    def wload(l, piece, nelem=PSZ):
        i = wslot[0] % NWB
        wslot[0] += 1
        if BF:
            P.dma("sp", wb[i][:, 0:nelem], wbf[l, piece, :, 0:nelem], r=[("wbf", l, piece)], w=["wb%d" % i])
        else:
            P.dma("sp", wb[i][:, 0:nelem], wcat[l, piece, :, 0:nelem], w=["wb%d" % i])
        return V(wb[i], ["wb%d" % i])

    def fmv(w):
        return w.ap[:, :].rearrange("p (b k c) -> p b k c", b=4, k=8)

    def widev(w, k=8):
        return w.ap[:, 0:k * 512].rearrange("p (k c) -> p k c", k=k)

    def mm_fm(w, b, ntok):
        p = ps()
        wv = fmv(w)

        def f(e):
            for k in range(8):
                ins = e.matmul(p.ap[:, 0:ntok], lhsT=wv[:, b, k, :], rhs=uT[:, k, 0:ntok], start=(k == 0), stop=(k == 7))
            return ins
        P.op("pe", f, r=[w, "uT"], w=[p])
        return p

    def cast_weights(l):
        st32 = [av(0, [128, PSZ]), av(8, [128, PSZ])]
        st16 = [av(16, [128, PSZ], BF16), av(20, [128, PSZ], BF16)]
        for i in range(NPIECE):
            s_ = i % 2
            P.dma("sp", st32[s_].ap, wcat[l, i], w=[st32[s_]])
            P.op("pool", lambda e, s_=s_: e.tensor_copy(out=st16[s_].ap, in_=st32[s_].ap), r=[st32[s_]], w=[st16[s_]])
            P.dma("pool", wbf[l, i], st16[s_].ap, r=[st16[s_]], w=[("wbf", l, i)])

    if BF:
        for l in layers:
            cast_weights(l)
    P.dma("sp", ident[:], ident_in, w=["ident"])
    P.op("pool", lambda e: e.memset(ones_bf[:], 1.0), w=["ones_bf"])
    P.dma("sp", cv[:], cvec_in, w=["cv"])
    P.op("act", lambda e: e.activation(out=scv[:], in_=cv[:], func=AF.Silu), r=["cv"], w=["scv"])

    def load_uT(src, ntok, who):
        xts = [av(24, [128, 1024]), av(26, [128, 1024])]
        for t in range(ntok // 128):
            xv = xts[t % 2]
            P.dma("sp", xv.ap, src[t * 128:(t + 1) * 128, :], r=[("src", id(src.tensor), t)], w=[xv])
            pa, pb = ps(), ps()

            def f(e, xv=xv, pa=pa, pb=pb):
                for k in range(8):
                    pp = pa if k < 4 else pb
                    ins = e.transpose(pp.ap[:, (k % 4) * 128:(k % 4 + 1) * 128], xv.ap[:, k * 128:(k + 1) * 128], ident[:])
                return ins
            P.op("pe", f, r=[xv, "ident"], w=[pa, pb])
            for k in range(8):
                pp = pa if k < 4 else pb
                P.op("act", lambda e, k=k, pp=pp, t=t: e.activation(
                    out=uT[:, k, t * 128:(t + 1) * 128], in_=pp.ap[:, (k % 4) * 128:(k % 4 + 1) * 128],
                    func=AF.Identity, bias=modc[:, k, who:who + 1], scale=modc[:, 8 + k, who:who + 1]),
                    r=[pp, "modc"], w=["uT"])

    def rope_proj(l, pc, pcp, ntok, cosv, sinv, dst_ap, dstkeys, t1s, t2s):
        w1 = wload(l, pc)
        w2 = wload(l, pcp)
        for b in range(4):
            p1 = mm_fm(w1, b, ntok)
            p2 = mm_fm(w2, b, ntok)
            t1, t2 = t1s[b % len(t1s)], t2s[b % len(t2s)]
            P.op("dve", lambda e, p1=p1, t1=t1, b=b: e.scalar_tensor_tensor(
                out=t1.ap[:, 0:ntok], in0=p1.ap[:, 0:ntok], scalar=bcol[:, pc * 4 + b:pc * 4 + b + 1], in1=cosv.ap[:, 0:ntok],
                op0=ALU.add, op1=ALU.mult), r=[p1, cosv, "bcol"], w=[t1])
            P.op("dve", lambda e, p2=p2, t2=t2, b=b: e.scalar_tensor_tensor(
                out=t2.ap[:, 0:ntok], in0=p2.ap[:, 0:ntok], scalar=bcol[:, pcp * 4 + b:pcp * 4 + b + 1], in1=sinv.ap[:, 0:ntok],
                op0=ALU.add, op1=ALU.mult), r=[p2, sinv, "bcol"], w=[t2])
            P.op("pool", lambda e, t1=t1, t2=t2, b=b: e.tensor_tensor(
                out=dst_ap[:, b, 0:ntok], in0=t1.ap[:, 0:ntok], in1=t2.ap[:, 0:ntok], op=ALU.add),
                r=[t1, t2], w=dstkeys)

    def layer_setup(l):
        P.dma("sp", bcol[:], bcol_in[l], w=["bcol"])
        P.dma("sp", adac[:], adac_in[l], w=["adac"])
        P.dma("sp", lcol[:].rearrange("p a b c -> p (a b c)"), lcol_in[l], w=["lcol"])
        P.dma("sp", ccol[:].rearrange("p a b -> p (a b)"), ccol_in[l], w=["ccol"])
        P.dma("sp", wsT[:].rearrange("p a b -> p (a b)"), wsT_in[l], w=["wsT"])
        P.dma("sp", lruW[:].rearrange("p a b c d -> p (a b c d)"), lruW_in[l], w=["lruW"])
        for hf_ in range(2):
            bt32 = av(hf_ * 7, [128, 7 * 512])
            P.dma("sp", bt32.ap, btab_in[l][:, hf_ * 3584:(hf_ + 1) * 3584], w=[bt32])
            P.op("pool", lambda e, hf_=hf_, bt32=bt32: e.tensor_copy(out=btab[:, hf_ * 7:(hf_ + 1) * 7, :].rearrange("p a b -> p (a b)"), in_=bt32.ap),
                 r=[bt32], w=["btab"])
        P.dma("sp", lng_bc[:], rows_in[l, 1, :].partition_broadcast(128), w=["lng_bc"])
        P.dma("sp", lnb_bc[:], rows_in[l, 2, :].partition_broadcast(128), w=["lnb_bc"])
        for i, (t, nm) in enumerate([(bav_bc, "bav_bc"), (bv_bc, "bv_bc"), (sg_bc, "sg_bc"), (sb_bc, "sb_bc"), (bs_bc, "bs_bc")]):
            P.dma("sp", t[:], rows_in[l, 3 + i, 0:512].partition_broadcast(128), w=[nm])
        badag = av(24, [128, 1024])
        P.dma("sp", badag.ap, rows_in[l, 0, :].partition_broadcast(128), w=[badag])
        P.op("act", lambda e: e.activation(out=ltmp[:], in_=lcol[:, :, :, 2], func=AF.Exp, scale=-1.0), r=["lcol"], w=["ltmp"])
        P.op("act", lambda e: e.activation(out=ltmp[:], in_=ltmp[:], func=AF.Ln, bias=1.0), r=["ltmp"], w=["ltmp"])
        P.op("dve", lambda e: e.tensor_scalar(out=lcc[:], in0=ltmp[:], scalar1=-8.0, scalar2=None, op0=ALU.mult), r=["ltmp"], w=["lcc"])
        screp = av(20, [128, 16, 128])
        P.op("dve", lambda e: e.tensor_copy(out=screp.ap, in_=scv[:].rearrange("p a b -> p (a b)").unsqueeze(2).broadcast_to([128, 16, 128])),
             r=["scv"], w=[screp])
        for i in range(4):
            wv = av((i % 2) * 8, [128, 4, 8, 128])
            P.dma("sp", wv.ap.rearrange("p a b c -> p (a b c)"), wada[l, i], w=[wv])
            for b in range(4):
                blk = i * 4 + b
                p = ps()

                def f(e, wv=wv, b=b, p=p):
                    for k in range(8):
                        ins = e.matmul(p.ap[:, 0:2], lhsT=wv.ap[:, b, k, :], rhs=scv[:, k, :], start=(k == 0), stop=(k == 7))
                    return ins
                P.op("pe", f, r=[wv, "scv"], w=[p])
                P.op("dve", lambda e, p=p, blk=blk: e.tensor_scalar(
                    out=modc[:, blk, :], in0=p.ap[:, 0:2], scalar1=adac[:, blk:blk + 1], scalar2=(1.0 if blk >= 8 else 0.0),
                    op0=ALU.add, op1=ALU.add), r=[p, "adac"], w=["modc"])
        for i in range(2):
            wv = av((i % 2) * 8, [128, 8, 512])
            P.dma("sp", wv.ap.rearrange("p a b -> p (a b)"), wada[l, 4 + i], w=[wv])
            for who in range(2):
                p = ps()

                def f(e, wv=wv, who=who, p=p):
                    for k in range(8):
                        ins = e.matmul(p.ap[:, 0:512], lhsT=screp.ap[:, k * 2 + who, :], rhs=wv.ap[:, k, :], start=(k == 0), stop=(k == 7))
                    return ins
                P.op("pe", f, r=[wv, screp], w=[p])
                P.op("dve", lambda e, p=p, who=who, i=i: e.tensor_tensor(
                    out=gate_bc[:, who, i * 512:(i + 1) * 512], in0=p.ap[:, 0:512], in1=badag.ap[:, i * 512:(i + 1) * 512], op=ALU.add),
                    r=[p, badag], w=["gate_bc"])

    def pass1_group(l, src, tok0, ntok, who, g):
        nt = ntok // 128
        load_uT(src, ntok, who)
        if stop == "p1a":
            raise _Stop()
        cosv, sinv = av(0, [128, 512]), av(1, [128, 512])
        P.dma("sp", cosv.ap[:, 0:ntok], cosT_in[:, tok0:tok0 + ntok], w=[cosv])
        P.dma("sp", sinv.ap[:, 0:ntok], sinT_in[:, tok0:tok0 + ntok], w=[sinv])
        bxs = av(2, [128, 4, 512])
        w = wload(l, P_BX)
        for b in range(4):
            p = mm_fm(w, b, ntok)
            P.op("act", lambda e, p=p, b=b: e.activation(out=bxs.ap[:, b, 0:ntok], in_=p.ap[:, 0:ntok], func=AF.Identity,
                                                          bias=bcol[:, P_BX * 4 + b:P_BX * 4 + b + 1], scale=1.0),
                 r=[p, "bcol"], w=[bxs])
        if stop == "p1a2":
            raise _Stop()
        for b in range(4):
            P.dma("pool", bxT[b * 128:(b + 1) * 128, tok0:tok0 + ntok], bxs.ap[:, b, 0:ntok], r=[bxs], w=[("bxT", g)])
        if stop == "p1b":
            raise _Stop()
        kTs = av(6, [128, 4, 512], BF16)
        t1s = [av(8, [128, 512]), av(9, [128, 512])]
        t2s = [av(10, [128, 512]), av(11, [128, 512])]
        rope_proj(l, P_K, P_KP, ntok, cosv, sinv, kTs.ap, [kTs], t1s, t2s)
        for b in range(4):
            P.dma("pool", KT[b * 128:(b + 1) * 128, tok0:tok0 + ntok], kTs.ap[:, b, 0:ntok], r=[kTs], w=[("KT", g)])
        if stop == "p1c":
            raise _Stop()
        vts = av(12, [128, 4, 512], BF16)
        w = wload(l, P_V)
        wv = widev(w)
        for t in range(nt):
            p = ps()

            def f(e, t=t, p=p):
                for k in range(8):
                    ins = e.matmul(p.ap[:, 0:512], lhsT=uT[:, k, t * 128:(t + 1) * 128], rhs=wv[:, k, :], start=(k == 0), stop=(k == 7))
                return ins
            P.op("pe", f, r=[w, "uT"], w=[p])
            P.op("dve", lambda e, t=t, p=p: e.tensor_tensor(out=vts.ap[:, t, :], in0=p.ap[:, 0:512], in1=bv_bc[:], op=ALU.add),
                 r=[p, "bv_bc"], w=[vts])
        for t in range(nt):
            P.dma("pool", Vs[tok0 + t * 128:tok0 + (t + 1) * 128, :], vts.ap[:, t, :], r=[vts], w=[("V", g)])
        if stop == "p1d":
            raise _Stop()

    def lru_pass(l, with_ctx):
        NS = 1024
        bxh = [av(0, [128, 1536]), av(3, [128, 1536])]
        xc = av(6, [128, NS]); rr = av(8, [128, NS]); ii = av(10, [128, NS]); aa = av(12, [128, NS])
        bq = av(14, [128, NS]); hh = av(16, [128, NS]); hfl = [av(18, [128, NS]), av(20, [128, NS])]
        yo = av(22, [128, NS], BF16)
        bi = [0]
        allbx = [("bxT", g) for g in range(17)]

        def seg(ct, d, colbase, seqlen, s0, n, first, is_ctx):
            b_ = bxh[bi[0] % 2]
            bi[0] += 1
            lo, hi = max(s0 - 2, 0), min(s0 + n + 1, seqlen)
            if lo > s0 - 2 or hi < s0 + n + 1:
                P.op("pool", lambda e: e.memset(b_.ap[:, 0:n + 3], 0.0), w=[b_])
            P.dma("sp", b_.ap[:, lo - (s0 - 2):hi - (s0 - 2)], bxT[ct * 128:(ct + 1) * 128, colbase + lo:colbase + hi],
                  r=allbx, w=[b_])
            P.op("pool", lambda e: e.tensor_scalar(out=xc.ap[:, 0:n], in0=b_.ap[:, 0:n], scalar1=ccol[:, ct, 0:1], scalar2=ccol[:, ct, 4:5],
                                                    op0=ALU.mult, op1=ALU.add), r=[b_, "ccol"], w=[xc])
            for j in range(1, 4):
                P.op("dve", lambda e, j=j: e.scalar_tensor_tensor(out=xc.ap[:, 0:n], in0=b_.ap[:, j:j + n], scalar=ccol[:, ct, j:j + 1],
                                                                    in1=xc.ap[:, 0:n], op0=ALU.mult, op1=ALU.add), r=[b_, xc, "ccol"], w=[xc])
            for c in range((n + 511) // 512):
                cn = min(512, n - c * 512)
                for wi, dst in ((0, rr), (1, ii)):
                    p = ps()
                    P.op("pe", lambda e, p=p, c=c, cn=cn, wi=wi: e.matmul(p.ap[:, 0:cn], lhsT=lruW[:, d, ct, wi, :], rhs=xc.ap[:, c * 512:c * 512 + cn],
                                                                           start=True, stop=True), r=[xc, "lruW"], w=[p])
                    P.op("act", lambda e, p=p, c=c, cn=cn, wi=wi, dst=dst: e.activation(
                        out=dst.ap[:, c * 512:c * 512 + cn], in_=p.ap[:, 0:cn], func=AF.Sigmoid, bias=lcol[:, d, ct, wi:wi + 1], scale=1.0),
                        r=[p, "lcol"], w=[dst])
            P.op("act", lambda e: e.activation(out=aa.ap[:, 0:n], in_=rr.ap[:, 0:n], func=AF.Exp, scale=lcc[:, d, ct:ct + 1]), r=[rr, "lcc"], w=[aa])
            P.op("pool", lambda e: e.tensor_tensor(out=bq.ap[:, 0:n], in0=aa.ap[:, 0:n], in1=aa.ap[:, 0:n], op=ALU.mult), r=[aa], w=[bq])
            P.op("act", lambda e: e.activation(out=bq.ap[:, 0:n], in_=bq.ap[:, 0:n], func=AF.Sqrt, bias=1.0, scale=-1.0), r=[bq], w=[bq])
            P.op("pool", lambda e: e.tensor_tensor(out=ii.ap[:, 0:n], in0=ii.ap[:, 0:n], in1=xc.ap[:, 0:n], op=ALU.mult), r=[ii, xc], w=[ii])
            P.op("dve", lambda e: e.tensor_tensor(out=bq.ap[:, 0:n], in0=bq.ap[:, 0:n], in1=ii.ap[:, 0:n], op=ALU.mult), r=[bq, ii], w=[bq])
            init = 0.0 if first else carry[:, 0:1]
            if d == 0:
                P.op("dve", lambda e: e.tensor_tensor_scan(out=hh.ap[:, 0:n], data0=aa.ap[:, 0:n], data1=bq.ap[:, 0:n], initial=init,
                                                            op0=ALU.mult, op1=ALU.add), r=[aa, bq, "carry"], w=[hh])
                P.op("dve", lambda e: e.tensor_copy(out=carry[:], in_=hh.ap[:, n - 1:n]), r=[hh], w=["carry"])
            else:
                P.op("dve", lambda e: e.tensor_tensor_scan(out=rev_ap(hh, n), data0=rev_ap(aa, n), data1=rev_ap(bq, n), initial=init,
                                                            op0=ALU.mult, op1=ALU.add), r=[aa, bq, "carry"], w=[hh])
                P.op("dve", lambda e: e.tensor_copy(out=carry[:], in_=hh.ap[:, 0:1]), r=[hh], w=["carry"])
            rows = slice(ct * 128, (ct + 1) * 128)
            if d == 0:
                if is_ctx:
                    P.op("pool", lambda e: e.tensor_copy(out=hcf[:], in_=hh.ap[:, 0:n]), r=[hh], w=["hcf"])
                else:
                    P.dma("pool", hfs[rows, s0:s0 + n], hh.ap[:, 0:n], r=[hh], w=[("hfs", ct, s0)])
            else:
                if is_ctx:
                    if with_ctx:
                        P.op("dve", lambda e: e.tensor_tensor(out=yo.ap[:, 0:n], in0=hh.ap[:, 0:n], in1=hcf[:], op=ALU.add), r=[hh, "hcf"], w=[yo])
                        P.dma("pool", ybT[rows, SEQ:SEQ + n], yo.ap[:, 0:n], r=[yo], w=[("ybT", ct, "c")])
                else:
                    hf_ = hfl[bi[0] % 2]
                    P.dma("sp", hf_.ap[:, 0:n], hfs[rows, s0:s0 + n], r=[("hfs", ct, s0)], w=[hf_])
                    P.op("dve", lambda e: e.tensor_tensor(out=yo.ap[:, 0:n], in0=hh.ap[:, 0:n], in1=hf_.ap[:, 0:n], op=ALU.add), r=[hh, hf_], w=[yo])
                    P.dma("pool", ybT[rows, s0:s0 + n], yo.ap[:, 0:n], r=[yo], w=[("ybT", ct, s0)])

        for ct in range(4):
            for d in range(2):
                seg(ct, d, SEQ, CTX, 0, CTX, True, True)
                order = range(0, SEQ, NS) if d == 0 else range(SEQ - NS, -1, -NS)
                for s0 in order:
                    seg(ct, d, 0, SEQ, s0, NS, False, False)

    def attention(l, g, tok0, ntok, is_ctx, qT, ycT):
        nrow = ntok // 64
        kwin = av(4, [128, 4, 1024], BF16)
        vwin = [av(8, [128, 4, 512], BF16), av(10, [128, 4, 512], BF16)]
        pTs = [av(12, [128, 6, 512], BF16), av(25, [128, 6, 512], BF16)]
        sps = [av(15, [128, 512]), av(16, [128, 512])]
        rz = av(17, [128, 512])
        if not is_ctx:
            t_lo, t_hi = max(0, tok0 - 256), min(SEQ, tok0 + 512 + 256)
            gs = [gg for gg in (g - 1, g, g + 1) if 0 <= gg < 16]
            for b in range(4):
                P.dma("sp", kwin.ap[:, b, 0:t_hi - t_lo], KT[b * 128:(b + 1) * 128, t_lo:t_hi],
                      r=[("KT", gg) for gg in gs], w=[kwin])
        for rq in range(nrow):
            pT = pTs[rq % 2]
            tiles = []
            if not is_ctx:
                r = (tok0 // 64) + rq
                r0 = min(max(r - 4, 0), 120)
                jv = r0 - r + 7
                koff = r0 * 64 - t_lo
                vw = vwin[rq % 2]
                for jj_ in range(4):
                    P.dma("sp", vw.ap[:, jj_, :], Vs[r0 * 64 + jj_ * 128:r0 * 64 + (jj_ + 1) * 128, :],
                          r=[("V", gg) for gg in gs], w=[vw])
                tiles += [("loc", j) for j in range(4)]
            tiles += [("ctx", 0), ("ctx", 1)]
            ntile = len(tiles)
            for ti, (kind, j) in enumerate(tiles):
                p = psp()
                p3 = p.ap.rearrange("p (e y) -> p e y", e=2)[:, :, 0:256]

                def f(e, kind=kind, j=j, p=p, rq=rq, koff=(koff if not is_ctx else 0)):
                    for h in range(8):
                        c, ee = h // 2, h % 2
                        if kind == "loc":
                            lt = kwin.ap[64 * ee:64 * ee + 64, c, koff + j * 128:koff + (j + 1) * 128]
                        else:
                            lt = kctx[64 * ee:64 * ee + 64, c, j * 128:(j + 1) * 128]
                        ins = e.matmul(p.ap[:, ee * 512 + c * 64:ee * 512 + (c + 1) * 64], lhsT=lt,
                                       rhs=qT.ap[64 * ee:64 * ee + 64, c, rq * 64:(rq + 1) * 64], start=True, stop=True)
                    return ins
                P.op("pe", f, r=[kwin if kind == "loc" else "kctx", qT], w=[p])
                if kind == "loc":
                    sp_ = sps[ti % 2]
                    P.op("dve", lambda e, p3=p3, sp_=sp_, j=j, jv=jv: e.scalar_tensor_tensor(
                        out=sp_.ap.rearrange("p (e x) -> p e x", e=2), in0=p3, scalar=0.125,
                        in1=btab[:, 2 * j + jv, :].rearrange("p (e x) -> p e x", e=2), op0=ALU.mult, op1=ALU.add),
                        r=[p, "btab"], w=[sp_])
                    P.op("act", lambda e, sp_=sp_, ti=ti, pT=pT: e.activation(out=pT.ap[:, ti, :], in_=sp_.ap, func=AF.Exp),
                         r=[sp_], w=[pT])
                else:
                    P.op("act", lambda e, p3=p3, ti=ti, pT=pT: e.activation(out=pT.ap[:, ti, :].rearrange("p (e x) -> p e x", e=2), in_=p3,
                                                                              func=AF.Exp, scale=0.125), r=[p], w=[pT])
            if stop == "p2s":
                raise _Stop()
            po, pz = ps(), ps()

            def fo(e, tiles=tiles, po=po, pT=pT, vw=(vw if not is_ctx else None)):
                for h in range(8):
                    c, ee = h // 2, h % 2
                    for ti, (kind, j) in enumerate(tiles):
                        lt = vw.ap[:, j, h * 64:(h + 1) * 64] if kind == "loc" else vctx[:, j, h * 64:(h + 1) * 64]
                        ins = e.matmul(po.ap[64 * ee:64 * ee + 64, c * 64:(c + 1) * 64], lhsT=lt, rhs=pT.ap[:, ti, ee * 256 + c * 64:ee * 256 + (c + 1) * 64],
                                       start=(ti == 0), stop=(ti == len(tiles) - 1), tile_position=(0, 64 * ee))
                return ins
            P.op("pe", fo, r=[pT, "vctx"] + ([vw] if not is_ctx else []), w=[po])

            def fz(e, pz=pz, pT=pT, ntile=ntile):
                for ti in range(ntile):
                    ins = e.matmul(pz.ap[:, 0:512], lhsT=ones_bf[:], rhs=pT.ap[:, ti, :], start=(ti == 0), stop=(ti == ntile - 1))
                return ins
            if stop == "p2o":
                raise _Stop()
            P.op("pe", fz, r=[pT, "ones_bf"], w=[pz])
            if stop == "p2z":
                raise _Stop()
            P.op("dve", lambda e, pz=pz: e.reciprocal(out=rz.ap, in_=pz.ap[:, 0:512]), r=[pz], w=[rz])
            for ee in range(2):
                P.op("dve", lambda e, ee=ee, po=po, rq=rq: e.tensor_tensor(
                    out=ycT.ap[64 * ee:64 * ee + 64, :, rq * 64:(rq + 1) * 64],
                    in0=po.ap[64 * ee:64 * ee + 64, 0:256].rearrange("p (c q) -> p c q", c=4),
                    in1=rz.ap[64 * ee:64 * ee + 64, ee * 256:(ee + 1) * 256].rearrange("p (c q) -> p c q", c=4),
                    op=ALU.mult), r=[po, rz], w=[ycT])

    def pass2_group(l, src, dst, dstkey, tok0, ntok, who, is_ctx, g):
        nt = ntok // 128
        load_uT(src, ntok, who)
        gu = av(0, [128, 4, 512]); sgA = av(4, [128, 4, 512]); vn = av(8, [128, 4, 512])
        vpre = av(12, [128, 512]); gv = av(13, [128, 512]); ta = av(14, [128, 512])
        for pc, dstv, fn in ((P_AU, gu, AF.Gelu), (P_AG, sgA, AF.Silu)):
            w = wload(l, pc)
            for b in range(4):
                p = mm_fm(w, b, ntok)
                P.op("act", lambda e, p=p, b=b, dstv=dstv, fn=fn, pc=pc: e.activation(
                    out=dstv.ap[:, b, 0:ntok], in_=p.ap[:, 0:ntok], func=fn, bias=bcol[:, pc * 4 + b:pc * 4 + b + 1], scale=1.0),
                    r=[p, "bcol"], w=[dstv])
        w = wload(l, P_AV)
        wv = widev(w)
        for t in range(nt):
            p = ps()

            def f(e, t=t, p=p, wv=wv):
                for k in range(8):
                    ins = e.matmul(p.ap[:, 0:512], lhsT=uT[:, k, t * 128:(t + 1) * 128], rhs=wv[:, k, :], start=(k == 0), stop=(k == 7))
                return ins
            P.op("pe", f, r=[w, "uT"], w=[p])
            P.op("dve", lambda e, p=p: e.tensor_tensor(out=vpre.ap, in0=p.ap[:, 0:512], in1=bav_bc[:], op=ALU.add), r=[p, "bav_bc"], w=[vpre])
            P.op("act", lambda e: e.activation(out=gv.ap, in_=vpre.ap, func=AF.Gelu), r=[vpre], w=[gv])
            P.op("dve", lambda e: e.bn_stats(out=st6[:, 0:6], in_=gv.ap), r=[gv], w=["st6"])
            P.op("dve", lambda e: e.bn_aggr(out=mv2[:], in_=st6[:, 0:6]), r=["st6"], w=["mv2"])
            P.op("dve", lambda e: e.tensor_scalar(out=rs1[:], in0=mv2[:, 1:2], scalar1=LN_EPS, scalar2=None, op0=ALU.add), r=["mv2"], w=["rs1"])
            P.op("act", lambda e: e.activation(out=rs1[:], in_=rs1[:], func=AF.Sqrt), r=["rs1"], w=["rs1"])
            P.op("dve", lambda e: e.reciprocal(out=rs1[:], in_=rs1[:]), r=["rs1"], w=["rs1"])
            P.op("dve", lambda e, t=t: e.tensor_scalar(out=vn.ap[:, t, :], in0=gv.ap, scalar1=mv2[:, 0:1], scalar2=rs1[:, 0:1],
                                                       op0=ALU.subtract, op1=ALU.mult), r=[gv, "mv2", "rs1"], w=[vn])
            P.op("pool", lambda e, t=t: e.tensor_tensor(out=vn.ap[:, t, :], in0=vn.ap[:, t, :], in1=sg_bc[:], op=ALU.mult), r=[vn, "sg_bc"], w=[vn])
            P.op("pool", lambda e, t=t: e.tensor_tensor(out=vn.ap[:, t, :], in0=vn.ap[:, t, :], in1=sb_bc[:], op=ALU.add), r=[vn, "sb_bc"], w=[vn])
        for g4 in range(4):
            p = ps()

            def f(e, g4=g4, p=p):
                for t in range(nt):
                    ins = e.matmul(p.ap[:, t * 128:(t + 1) * 128], lhsT=vn.ap[:, t, g4 * 128:(g4 + 1) * 128], rhs=wsT[:, g4, :], start=True, stop=True)
                return ins
            P.op("pe", f, r=[vn, "wsT"], w=[p])
            P.op("dve", lambda e, g4=g4, p=p: e.tensor_tensor(
                out=ta.ap[:, 0:ntok].rearrange("p (t q) -> p t q", t=nt), in0=p.ap[:, 0:ntok].rearrange("p (t q) -> p t q", t=nt),
                in1=bs_bc[:, g4 * 128:(g4 + 1) * 128].unsqueeze(1).broadcast_to([128, nt, 128]), op=ALU.add), r=[p, "bs_bc"], w=[ta])
            P.op("dve", lambda e, g4=g4: e.tensor_tensor(out=ta.ap[:, 0:ntok], in0=ta.ap[:, 0:ntok], in1=gu.ap[:, g4, 0:ntok], op=ALU.mult),
                 r=[ta, gu], w=[ta])
            P.op("pool", lambda e, g4=g4: e.tensor_tensor(out=yaT[:, g4, 0:ntok], in0=ta.ap[:, 0:ntok], in1=sgA.ap[:, g4, 0:ntok], op=ALU.mult),
                 r=[ta, sgA], w=["yaT"])
        if stop == "p2a":
            raise _Stop()
        ybl = av(16, [128, 4, 512], BF16)
        tmpb = av(12, [128, 512])
        ycol = SEQ if is_ctx else tok0
        for b in range(4):
            P.dma("sp", ybl.ap[:, b, 0:ntok], ybT[b * 128:(b + 1) * 128, ycol:ycol + ntok],
                  r=[("ybT", b, "c") if is_ctx else ("ybT", b, (tok0 // 1024) * 1024)], w=[ybl])
        w = wload(l, P_BG)
        for b in range(4):
            p = mm_fm(w, b, ntok)
            P.op("act", lambda e, p=p, b=b: e.activation(out=tmpb.ap[:, 0:ntok], in_=p.ap[:, 0:ntok], func=AF.Silu,
                                                          bias=bcol[:, P_BG * 4 + b:P_BG * 4 + b + 1], scale=1.0), r=[p, "bcol"], w=[tmpb])
            P.op("dve", lambda e, b=b: e.tensor_tensor(out=ybs[:, b, 0:ntok], in0=tmpb.ap[:, 0:ntok], in1=ybl.ap[:, b, 0:ntok], op=ALU.mult),
                 r=[tmpb, ybl], w=["ybs"])
        if stop == "p2b":
            raise _Stop()
        cosv, sinv = av(0, [128, 512]), av(1, [128, 512])
        P.dma("sp", cosv.ap[:, 0:ntok], cosT_in[:, tok0:tok0 + ntok], w=[cosv])
        P.dma("sp", sinv.ap[:, 0:ntok], sinT_in[:, tok0:tok0 + ntok], w=[sinv])
        qT = av(23, [128, 4, 512], BF16)
        ycT = av(18, [128, 4, 512])
        rope_proj(l, P_Q, P_QP, ntok, cosv, sinv, qT.ap, [qT], [av(2, [128, 512])], [av(3, [128, 512])])
        if stop == "p2q":
            raise _Stop()
        attention(l, g, tok0, ntok, is_ctx, qT, ycT)
        if stop == "p2c":
            raise _Stop()
        if "dbgy" in dbg and l == 0 and (is_ctx or g == 0):
            nm = "c" if is_ctx else "l"
            dya = nc.dram_tensor("dbg_ya_" + nm, [512, ntok], F32, kind="ExternalOutput").ap()
            dyc = nc.dram_tensor("dbg_yc_" + nm, [512, ntok], F32, kind="ExternalOutput").ap()
            dyb = nc.dram_tensor("dbg_yb_" + nm, [512, ntok], F32, kind="ExternalOutput").ap()
            for b in range(4):
                P.dma("sp", dya[b * 128:(b + 1) * 128, :], yaT[:, b, 0:ntok], r=["yaT"], w=[("dbg", nm, 0, b)])
                P.dma("sp", dyc[b * 128:(b + 1) * 128, :], ycT.ap[:, b, 0:ntok], r=[ycT], w=[("dbg", nm, 1, b)])
                P.dma("sp", dyb[b * 128:(b + 1) * 128, :], ybs[:, b, 0:ntok], r=["ybs"], w=[("dbg", nm, 2, b)])
        tmpc = av(22, [128, 512])
        w = wload(l, P_CG)
        for b in range(4):
            p = mm_fm(w, b, ntok)
            P.op("act", lambda e, p=p, b=b: e.activation(out=tmpc.ap[:, 0:ntok], in_=p.ap[:, 0:ntok], func=AF.Silu,
                                                          bias=bcol[:, P_CG * 4 + b:P_CG * 4 + b + 1], scale=1.0), r=[p, "bcol"], w=[tmpc])
            P.op("dve", lambda e, b=b: e.tensor_tensor(out=ycs[:, b, 0:ntok], in0=tmpc.ap[:, 0:ntok], in1=ycT.ap[:, b, 0:ntok], op=ALU.mult),
                 r=[tmpc, ycT], w=["ycs"])
        sgs = [av(0, [128, 512]), av(1, [128, 512])]
        tms = [av(2, [128, 512]), av(3, [128, 512])]
        macc = av(4, [128, 4, 512])
        ys = [(yaT, "yaT"), (ybs, "ybs"), (ycs, "ycs")]
        cnt = 0
        for hh_ in range(2):
            for n in range(3):
                pcg = P_GM + hh_ * 3 + n
                wg = wload(l, pcg)
                wbr = wload(l, P_WBR + hh_ * 3 + n, 2048)
                wbv = widev(wbr, 4)
                yt, yk = ys[n]
                for jj in range(4):
                    j = hh_ * 4 + jj
                    pg = mm_fm(wg, jj, ntok)
                    pb = ps()

                    def f(e, pb=pb, jj=jj, yt=yt, wbv=wbv):
                        for k in range(4):
                            ins = e.matmul(pb.ap[:, 0:ntok], lhsT=wbv[:, k, jj * 128:(jj + 1) * 128], rhs=yt[:, k, 0:ntok], start=(k == 0), stop=(k == 3))
                        return ins
                    P.op("pe", f, r=[wbr, yk], w=[pb])
                    sg = sgs[cnt % 2]
                    tm = tms[cnt % 2]
                    cnt += 1
                    P.op("act", lambda e, pg=pg, sg=sg, jj=jj, pcg=pcg: e.activation(
                        out=sg.ap[:, 0:ntok], in_=pg.ap[:, 0:ntok], func=AF.Sigmoid, bias=bcol[:, pcg * 4 + jj:pcg * 4 + jj + 1], scale=1.0),
                        r=[pg, "bcol"], w=[sg])
                    if n == 0:
                        P.op("dve", lambda e, sg=sg, pb=pb, jj=jj: e.tensor_tensor(out=macc.ap[:, jj, 0:ntok], in0=sg.ap[:, 0:ntok], in1=pb.ap[:, 0:ntok], op=ALU.mult),
                             r=[sg, pb], w=[macc])
                    else:
                        P.op("dve", lambda e, sg=sg, pb=pb, tm=tm: e.tensor_tensor(out=tm.ap[:, 0:ntok], in0=sg.ap[:, 0:ntok], in1=pb.ap[:, 0:ntok], op=ALU.mult),
                             r=[sg, pb], w=[tm])
                        if n == 1:
                            P.op("pool", lambda e, tm=tm, jj=jj: e.tensor_tensor(out=macc.ap[:, jj, 0:ntok], in0=macc.ap[:, jj, 0:ntok], in1=tm.ap[:, 0:ntok], op=ALU.add),
                                 r=[tm, macc], w=[macc])
                        else:
                            P.op("pool", lambda e, tm=tm, jj=jj, j=j: e.tensor_tensor(out=mT[:, j, 0:ntok], in0=macc.ap[:, jj, 0:ntok], in1=tm.ap[:, 0:ntok], op=ALU.add),
                                 r=[tm, macc], w=["mT"])
        if stop == "p2d":
            raise _Stop()
        xres = av(8, [128, 4, 1024])
        tmo = [av(16, [128, 512]), av(17, [128, 512])]
        xn = [av(18, [128, 1024]), av(20, [128, 1024])]
        for t in range(nt):
            P.dma("sp", xres.ap[:, t, :], src[t * 128:(t + 1) * 128, :], r=[("src", id(src.tensor), t)], w=[xres])
        cnt = 0
        for half in range(2):
            w = wload(l, P_WOUT + half)
            wv = widev(w)
            for t in range(nt):
                p = ps()

                def f(e, t=t, p=p, wv=wv):
                    for k in range(8):
                        ins = e.matmul(p.ap[:, 0:512], lhsT=mT[:, k, t * 128:(t + 1) * 128], rhs=wv[:, k, :], start=(k == 0), stop=(k == 7))
                    return ins
                P.op("pe", f, r=[w, "mT"], w=[p])
                tm = tmo[cnt % 2]
                cnt += 1
                P.op("dve", lambda e, p=p, tm=tm, half=half: e.tensor_tensor(out=tm.ap, in0=p.ap[:, 0:512], in1=gate_bc[:, who, half * 512:(half + 1) * 512], op=ALU.mult),
                     r=[p, "gate_bc"], w=[tm])
                P.op("dve", lambda e, tm=tm, t=t, half=half: e.scalar_tensor_tensor(
                    out=xres.ap[:, t, half * 512:(half + 1) * 512], in0=xres.ap[:, t, half * 512:(half + 1) * 512], scalar=float(ALPHA), in1=tm.ap,
                    op0=ALU.mult, op1=ALU.add), r=[tm, xres], w=[xres])
        for t in range(nt):
            xo = xn[t % 2]
            P.op("dve", lambda e, t=t: e.bn_stats(out=st6[:, 0:6], in_=xres.ap[:, t, 0:512]), r=[xres], w=["st6"])
            P.op("dve", lambda e, t=t: e.bn_stats(out=st6[:, 6:12], in_=xres.ap[:, t, 512:1024]), r=[xres, "st6"], w=["st6"])
            P.op("dve", lambda e: e.bn_aggr(out=mv2[:], in_=st6[:, 0:12]), r=["st6"], w=["mv2"])
            P.op("dve", lambda e: e.tensor_scalar(out=rs1[:], in0=mv2[:, 1:2], scalar1=LN_EPS, scalar2=None, op0=ALU.add), r=["mv2"], w=["rs1"])
            P.op("act", lambda e: e.activation(out=rs1[:], in_=rs1[:], func=AF.Sqrt), r=["rs1"], w=["rs1"])
            P.op("dve", lambda e: e.reciprocal(out=rs1[:], in_=rs1[:]), r=["rs1"], w=["rs1"])
            P.op("dve", lambda e, t=t, xo=xo: e.tensor_scalar(out=xo.ap, in0=xres.ap[:, t, :], scalar1=mv2[:, 0:1], scalar2=rs1[:, 0:1],
                                                              op0=ALU.subtract, op1=ALU.mult), r=[xres, "mv2", "rs1"], w=[xo])
            P.op("pool", lambda e, xo=xo: e.tensor_tensor(out=xo.ap, in0=xo.ap, in1=lng_bc[:], op=ALU.mult), r=[xo, "lng_bc"], w=[xo])
            P.op("pool", lambda e, xo=xo: e.tensor_tensor(out=xo.ap, in0=xo.ap, in1=lnb_bc[:], op=ALU.add), r=[xo, "lnb_bc"], w=[xo])
            P.dma("pool", dst[t * 128:(t + 1) * 128, :], xo.ap, r=[xo], w=[(dstkey, id(dst.tensor), t)])
        if stop == "p2e" or (stop == "p2g" and not is_ctx):
            raise _Stop()

    nl = len(layers)
    try:
      for li, l in enumerate(layers):
          last = (l == DEPTH - 1)
          with_ctx = not last
          xs = x_in if l == 0 else x1
          cs = ctx_in if l == 0 else xc1
          xd = out_d if last else x1
          layer_setup(l)
          if stop == "setup":
              break
          pass1_group(l, cs, SEQ, CTX, 1, 16)
          for g in range(16):
              pass1_group(l, xs[g * 512:(g + 1) * 512, :], g * 512, 512, 0, g)
          if stop == "pass1":
              break
          lru_pass(l, with_ctx)
          if stop == "lru":
              break
          P.dma("sp", kctx[:], KT.rearrange("(b p) t -> p b t", p=128)[:, :, SEQ:NTOK], r=[("KT", 16)], w=["kctx"])
          P.dma("sp", vctx[:], Vs[SEQ:NTOK, :].rearrange("(j p) c -> p j c", p=128), r=[("V", 16)], w=["vctx"])
          if with_ctx:
              pass2_group(l, cs, xc1, "src", SEQ, CTX, 1, True, 16)
          for g in range(16):
              pass2_group(l, xs[g * 512:(g + 1) * 512, :], xd[g * 512:(g + 1) * 512, :], "src", g * 512, 512, 0, False, g)
    except _Stop:
        pass
    P.finish()
    P.emit()
    return nc, P


_CACHE = {}


def kernel(**inputs):
    inp = {k: np.asarray(v) for k, v in inputs.items()}
    sh = prep_shared(inp)

    def fl(a, n):
        return np.ascontiguousarray(a.reshape(a.shape[:n] + (-1,)))

    shared = dict(wcat=sh["wcat"], wada=sh["wada"], bcol=sh["bcol"], rows=sh["rows"], wsT=fl(sh["wsT"], 2),
                  lruW=fl(sh["lruW"], 2), lcol=fl(sh["lcol"], 2), ccol=fl(sh["ccol"], 2), adac=sh["adac"],
                  btab=fl(sh["btab"], 2), cosT=sh["cosT"], sinT=sh["sinT"], ident=sh["ident"])
    in_maps = []
    for b in range(NB):
        m = prep_core(inp, b)
        m.update(shared)
        in_maps.append(m)
    if "nc" not in _CACHE:
        _CACHE["nc"] = build()[0]
    res = run_bass_kernel_spmd(_CACHE["nc"], in_maps, core_ids=list(range(NB)))
    return np.stack([np.asarray(r["out"], dtype=np.float32) for r in res.results], axis=0)
```

```python
import numpy as np
import concourse.bass as bass
import concourse.mybir as mybir
from concourse.bass_utils import run_bass_kernel_spmd

F32 = mybir.dt.float32
BF16 = mybir.dt.bfloat16
AF = mybir.ActivationFunctionType
ALU = mybir.AluOpType

D = 1024
SEQ = 8192
CTX = 256
NTOK = SEQ + CTX
DEPTH = 2
NB = 8
ALPHA = (2 * DEPTH) ** 0.25
LN_EPS = 1e-5
NEG = -30000.0
NPIECE = 25
PSZ = 4096

ENGS = ("pe", "act", "dve", "pool", "sp")


class V:
    def __init__(self, ap, keys):
        self.ap = ap
        self.keys = list(keys)


def _keys(items):
    out = []
    for it in items:
        if isinstance(it, V):
            out.extend(it.keys)
        else:
            out.append(it)
    return out


class Prog:
    def __init__(self, nc):
        self.nc = nc
        self.q = {e: [] for e in ENGS}
        self.esem = {e: nc.alloc_semaphore("s_" + e) for e in ("pe", "act", "dve", "pool")}
        self.ecnt = {e: 0 for e in self.esem}
        self.dsem = {}
        self.last_w = {}
        self.readers = {}
        self.waited = {e: {} for e in ENGS}
        self.n_ops = 0
        self.ps_i = 0

    def _deps(self, eng, reads, writes):
        toks = {}

        def add(t):
            if t is None:
                return
            s, v = t
            k = id(s)
            if k not in toks or toks[k][1] < v:
                toks[k] = (s, v)

        for r in reads:
            add(self.last_w.get(r))
        for w in writes:
            add(self.last_w.get(w))
            for t in self.readers.get(w, {}).values():
                add(t)
        out = []
        wd = self.waited[eng]
        for k, (s, v) in toks.items():
            if wd.get(k, 0) >= v:
                continue
            wd[k] = v
            out.append((s, v))
        return out

    def _commit(self, tok, reads, writes):
        for w in writes:
            self.last_w[w] = tok
            self.readers[w] = {}
        for r in reads:
            self.readers.setdefault(r, {})[id(tok[0])] = tok

    def op(self, eng, fn, r=(), w=()):
        reads, writes = _keys(r), _keys(w)
        waits = self._deps(eng, reads, writes)
        sem = self.esem[eng]
        self.ecnt[eng] += 1
        tok = (sem, self.ecnt[eng])

        def run(e, waits=waits, fn=fn, sem=sem):
            for s, v in waits:
                e.wait_ge(s, v)
            fn(e).then_inc(sem, 1)

        self.q[eng].append(run)
        self._commit(tok, reads, writes)
        self.n_ops += 1
        return tok

    def dma(self, eng, out, in_, r=(), w=(), semkey=None):
        reads, writes = _keys(r), _keys(w)
        if semkey is None:
            semkey = next(k for k in (writes + reads) if isinstance(k, str) or k[0] == "pg")
        if semkey not in self.dsem:
            self.dsem[semkey] = [self.nc.alloc_semaphore("d%d" % len(self.dsem)), 0]
        ent = self.dsem[semkey]
        waits = self._deps(eng, reads, writes)
        ent[1] += 16
        tok = (ent[0], ent[1])

        def run(e, waits=waits, sem=ent[0], out=out, in_=in_):
            for s, v in waits:
                e.wait_ge(s, v)
            e.dma_start(out=out, in_=in_).then_inc(sem, 16)

        self.q[eng].append(run)
        self._commit(tok, reads, writes)
        self.n_ops += 1
        return tok

    def finish(self, eng="sp"):
        ents = [tuple(v) for v in self.dsem.values()]
        ecs = [(self.esem[e], self.ecnt[e]) for e in self.esem if self.ecnt[e] > 0]

        def run(e):
            for sem, cnt in ents:
                e.wait_ge(sem, cnt)
            for sem, cnt in ecs:
                e.wait_ge(sem, cnt)

        self.q[eng].append(run)

    def emit(self):
        with self.nc.Block() as block:
            @block.tensor
            def _(e):
                for f in self.q["pe"]:
                    f(e)

            @block.scalar
            def _(e):
                for f in self.q["act"]:
                    f(e)

            @block.vector
            def _(e):
                for f in self.q["dve"]:
                    f(e)

            @block.gpsimd
            def _(e):
                for f in self.q["pool"]:
                    f(e)

            @block.sync
            def _(e):
                for f in self.q["sp"]:
                    f(e)


def _partner():
    d = np.arange(64)
    return np.where(d < 16, d + 16, np.where(d < 32, d - 16, np.where(d < 48, d + 16, d - 16)))


def _piece_cols():
    ar = np.arange(512)
    permcols = (np.arange(8)[:, None] * 64 + _partner()[None, :]).reshape(-1)
    pcs = [(1536 + ar, "fm"), (3072 + ar, "fm"), (3072 + permcols, "fm"), (3584 + ar, "wide"),
           (0 + ar, "fm"), (512 + ar, "wide"), (1024 + ar, "fm"), (2048 + ar, "fm"),
           (2560 + ar, "fm"), (2560 + permcols, "fm"), (4096 + ar, "fm")]
    for hh in range(2):
        for n in range(3):
            pcs.append((4608 + n * 1024 + hh * 512 + ar, "fm"))
    return pcs


P_BX, P_K, P_KP, P_V, P_AU, P_AV, P_AG, P_BG, P_Q, P_QP, P_CG = range(11)
P_GM = 11
P_WBR = 17
P_WOUT = 23


def _fm(Wc):
    return np.ascontiguousarray(Wc.reshape(8, 128, 4, 128).transpose(1, 2, 0, 3)).reshape(128, PSZ)


def _wide(Wc):
    return np.ascontiguousarray(Wc.reshape(8, 128, 512).transpose(1, 0, 2)).reshape(128, PSZ)


def _col(v, n):
    return np.ascontiguousarray(np.asarray(v).reshape(n, 128).T)


def prep_shared(inp):
    f32 = np.float32
    pcs = _piece_cols()
    sh = {}
    wcat = np.zeros((DEPTH, NPIECE, 128, PSZ), f32)
    wada = np.zeros((DEPTH, 6, 128, PSZ), f32)
    bcol = np.zeros((DEPTH, 128, 17 * 4), f32)
    rows = np.zeros((DEPTH, 9, 1024), f32)
    wsT = np.zeros((DEPTH, 128, 4, 128), f32)
    lruW = np.zeros((DEPTH, 128, 2, 4, 2, 128), f32)
    lcol = np.zeros((DEPTH, 128, 2, 4, 3), f32)
    ccol = np.zeros((DEPTH, 128, 4, 5), f32)
    adac = np.zeros((DEPTH, 128, 16), f32)
    btab = np.full((DEPTH, 128, 14, 512), NEG, f32)
    qc = np.arange(64)
    c0 = np.clip(qc - 8, 0, 48)
    kc = np.arange(64)
    valid = (kc[:, None] >= c0[None, :]) & (kc[:, None] < c0[None, :] + 16)
    bidx = np.clip(kc[:, None] - qc[None, :] + 15, 0, 30)
    for l in range(DEPTH):
        w_in, b_in = inp["w_in"][l], inp["b_in"][l]
        for i, (cols, kind) in enumerate(pcs):
            wcat[l, i] = _fm(w_in[:, cols]) if kind == "fm" else _wide(w_in[:, cols])
            if kind == "fm":
                bcol[l, :, i * 4:(i + 1) * 4] = _col(b_in[cols], 4)
        for hh in range(2):
            for n in range(3):
                wb = inp["w_br"][l][n][:, hh * 512:(hh + 1) * 512]
                wcat[l, P_WBR + hh * 3 + n, :, :2048] = np.ascontiguousarray(
                    wb.reshape(4, 128, 512).transpose(1, 0, 2)).reshape(128, 2048)
        for half in range(2):
            wcat[l, P_WOUT + half] = _wide(inp["w_out"][l][:, half * 512:(half + 1) * 512])
        wa = inp["w_ada"][l]
        for i in range(4):
            wada[l, i] = _fm(wa[:, i * 512:(i + 1) * 512])
        for i in range(2):
            wada[l, 4 + i] = _wide(wa[:, 2048 + i * 512:2048 + (i + 1) * 512])
        adac[l] = _col(inp["b_ada"][l][:2048], 16)
        rows[l, 0] = inp["b_ada"][l][2048:]
        rows[l, 1] = inp["ln_g"][l]
        rows[l, 2] = inp["ln_b"][l]
        rows[l, 3, :512] = b_in[512:1024]
        rows[l, 4, :512] = b_in[3584:4096]
        rows[l, 5, :512] = inp["sgu_ln_g"][l]
        rows[l, 6, :512] = inp["sgu_ln_b"][l]
        rows[l, 7, :512] = inp["b_s"][l].reshape(-1)
        wsT[l] = inp["w_s"][l].transpose(2, 0, 1)
        for d in range(2):
            for ct in range(4):
                for wi, nm in enumerate(("lru_wa", "lru_wx")):
                    for e in range(2):
                        lruW[l, e * 64:(e + 1) * 64, d, ct, wi, e * 64:(e + 1) * 64] = inp[nm][l][d][2 * ct + e]
            lcol[l, :, d, :, 0] = _col(inp["lru_ba"][l][d], 4)
            lcol[l, :, d, :, 1] = _col(inp["lru_bx"][l][d], 4)
            lcol[l, :, d, :, 2] = _col(inp["lru_lam"][l][d], 4)
        for j in range(4):
            ccol[l, :, :, j] = _col(inp["conv_w"][l][j], 4)
        ccol[l, :, :, 4] = _col(inp["conv_b"][l], 4)
        rpb = inp["rpb"][l]
        for m in range(14):
            for e in range(2):
                g = rpb[:, m + e][:, bidx]
                g = np.where(valid[None], g, f32(NEG))
                g = g.reshape(4, 2, 64, 64).transpose(2, 1, 0, 3)
                btab[l, e * 64:(e + 1) * 64, m] = g.reshape(64, 512)
    inv_freq = (f32(10000.0) ** (-np.arange(16, dtype=f32) / f32(16))).astype(f32)
    pos = np.arange(SEQ)
    rw, cl = (pos // 64).astype(f32), (pos % 64).astype(f32)
    cosT = np.ones((128, NTOK), f32)
    sinT = np.zeros((128, NTOK), f32)
    for d in range(64):
        p = rw if d < 32 else cl
        ang = (p * inv_freq[d % 16]).astype(f32)
        sgn = -1.0 if (d % 32) < 16 else 1.0
        for e in range(2):
            cosT[e * 64 + d, :SEQ] = np.cos(ang)
            sinT[e * 64 + d, :SEQ] = sgn * np.sin(ang)
    sh.update(wcat=wcat, wada=wada, bcol=bcol, rows=rows, wsT=wsT, lruW=lruW, lcol=lcol, ccol=ccol,
              adac=adac, btab=btab, cosT=cosT, sinT=sinT, ident=np.eye(128, dtype=f32))
    return sh


def prep_core(inp, b):
    cvec = np.zeros((128, 8, 2), np.float32)
    cvec[:, :, 0] = _col(inp["c"][b], 8)
    cvec[:, :, 1] = _col(inp["c_ctx"], 8)
    return {"x": np.ascontiguousarray(inp["x"][b]), "ctx": np.ascontiguousarray(inp["ctx"][b]), "cvec": cvec}


ARF = 22528
PG = 512


class _Stop(Exception):
    pass


BF = True
NWB = 3


def build(layers=(0, 1), dbg=(), stop=None):
    WDT = BF16 if BF else F32
    nc = bass.Bass("TRN2", target_bir_lowering=False)
    P = Prog(nc)

    def din(name, shape, dt=F32):
        return nc.dram_tensor(name, list(shape), dt, kind="ExternalInput").ap()

    def dscr(name, shape, dt, out=False):
        kind = "ExternalOutput" if (out or name in dbg) else "Internal"
        return nc.dram_tensor(name, list(shape), dt, kind=kind).ap()

    x_in = din("x", [SEQ, D]); ctx_in = din("ctx", [CTX, D]); cvec_in = din("cvec", [128, 8, 2])
    wcat = din("wcat", [DEPTH, NPIECE, 128, PSZ]); wada = din("wada", [DEPTH, 6, 128, PSZ])
    bcol_in = din("bcol", [DEPTH, 128, 68]); rows_in = din("rows", [DEPTH, 9, 1024])
    wsT_in = din("wsT", [DEPTH, 128, 512]); lruW_in = din("lruW", [DEPTH, 128, 2048])
    lcol_in = din("lcol", [DEPTH, 128, 24]); ccol_in = din("ccol", [DEPTH, 128, 20])
    adac_in = din("adac", [DEPTH, 128, 16]); btab_in = din("btab", [DEPTH, 128, 14 * 512])
    cosT_in = din("cosT", [128, NTOK]); sinT_in = din("sinT", [128, NTOK]); ident_in = din("ident", [128, 128])

    out_d = dscr("out", [SEQ, D], F32, out=True)
    wbf = dscr("wbf", [DEPTH, NPIECE, 128, PSZ], BF16) if BF else None
    bxT = dscr("bxT", [512, NTOK], F32)
    KT = dscr("KT", [512, NTOK], BF16)
    Vs = dscr("Vs", [NTOK, 512], BF16)
    hfs = dscr("hfs", [512, SEQ], F32)
    ybT = dscr("ybT", [512, NTOK], BF16)
    x1 = dscr("x1", [SEQ, D], F32)
    xc1 = dscr("xc1", [CTX, D], F32)

    def S(name, shape, dt=F32):
        return nc.alloc_sbuf_tensor("sb_" + name, list(shape), dt)

    AR = S("AR", [128, ARF])
    bcol = S("bcol", [128, 68]); adac = S("adac", [128, 16]); lcol = S("lcol", [128, 2, 4, 3]); ccol = S("ccol", [128, 4, 5])
    wsT = S("wsT", [128, 4, 128]); lruW = S("lruW", [128, 2, 4, 2, 128]); ident = S("ident", [128, 128])
    ones_bf = S("ones_bf", [128, 128], BF16)
    gate_bc = S("gate_bc", [128, 2, 1024]); lng_bc = S("lng_bc", [128, 1024]); lnb_bc = S("lnb_bc", [128, 1024])
    bav_bc = S("bav_bc", [128, 512]); bv_bc = S("bv_bc", [128, 512]); sg_bc = S("sg_bc", [128, 512])
    sb_bc = S("sb_bc", [128, 512]); bs_bc = S("bs_bc", [128, 512])
    btab = S("btab", [128, 14, 512], BF16)
    kctx = S("kctx", [128, 4, 256], BF16); vctx = S("vctx", [128, 2, 512], BF16)
    cv = S("cv", [128, 8, 2]); scv = S("scv", [128, 8, 2]); modc = S("modc", [128, 16, 2]); lcc = S("lcc", [128, 2, 4]); lcc2 = S("lcc2", [128, 2, 4])
    ltmp = S("ltmp", [128, 2, 4])
    uT = S("uT", [128, 8, 512], WDT)
    wb = [S("wb%d" % i, [128, PSZ], WDT) for i in range(NWB)]
    yaT = S("yaT", [128, 4, 512], WDT); ybs = S("ybs", [128, 4, 512], WDT); ycs = S("ycs", [128, 4, 512], WDT)
    mT = S("mT", [128, 8, 512], WDT)
    carry = S("carry", [128, 1]); hcf = S("hcf", [128, 256])
    st6 = S("st6", [128, 12]); mv2 = S("mv2", [128, 2]); rs1 = S("rs1", [128, 1])
    psum2 = [nc.alloc_psum_tensor("ps%d" % i, [128, 1024], F32) for i in range(4)]

    def ps():
        i = P.ps_i % 8
        P.ps_i += 1
        return V(psum2[i // 2][:, (i % 2) * 512:(i % 2 + 1) * 512], ["ps%d" % i])

    def psp():
        if P.ps_i % 2:
            P.ps_i += 1
        i = P.ps_i % 8
        P.ps_i += 2
        return V(psum2[i // 2][:, :], ["ps%d" % i, "ps%d" % (i + 1)])

    def av(off_pg, shape, dt=F32, off=0):
        n = int(np.prod(shape[1:]))
        nf = n if dt == F32 else n // 2
        o = off_pg * PG + off
        ap = AR[:, o:o + nf]
        if dt != F32:
            ap = ap.bitcast(BF16)
        if len(shape) == 3:
            ap = ap.rearrange("p (a b) -> p a b", a=shape[1])
        elif len(shape) == 4:
            ap = ap.rearrange("p (a b c) -> p a b c", a=shape[1], b=shape[2])
        v = V(ap, [("pg", i) for i in range(o // PG, (o + nf - 1) // PG + 1)])
        v.o = o
        v.nf = nf
        return v

    def rev_ap(v, n):
        return bass.AP(AR, v.o + n - 1, [[ARF, 128], [-1, n]])

    wslot = [0]

    def wload(l, piece, nelem=PSZ):
        i = wslot[0] % NWB
        wslot[0] += 1
        if BF:
            P.dma("sp", wb[i][:, 0:nelem], wbf[l, piece, :, 0:nelem], r=[("wbf", l, piece)], w=["wb%d" % i])
        else:
            P.dma("sp", wb[i][:, 0:nelem], wcat[l, piece, :, 0:nelem], w=["wb%d" % i])
        return V(wb[i], ["wb%d" % i])

    def fmv(w):
        return w.ap[:, :].rearrange("p (b k c) -> p b k c", b=4, k=8)

    def widev(w, k=8):
        return w.ap[:, 0:k * 512].rearrange("p (k c) -> p k c", k=k)

    def mm_fm(w, b, ntok):
        p = ps()
        wv = fmv(w)

        def f(e):
            for k in range(8):
                ins = e.matmul(p.ap[:, 0:ntok], lhsT=wv[:, b, k, :], rhs=uT[:, k, 0:ntok], start=(k == 0), stop=(k == 7))
            return ins
        P.op("pe", f, r=[w, "uT"], w=[p])
        return p

    def cast_weights(l):
        st32 = [av(0, [128, PSZ]), av(8, [128, PSZ])]
        st16 = [av(16, [128, PSZ], BF16), av(20, [128, PSZ], BF16)]
        for i in range(NPIECE):
            s_ = i % 2
            P.dma("sp", st32[s_].ap, wcat[l, i], w=[st32[s_]])
            P.op("dve", lambda e, s_=s_: e.tensor_copy(out=st16[s_].ap, in_=st32[s_].ap), r=[st32[s_]], w=[st16[s_]])
            P.dma("pool", wbf[l, i], st16[s_].ap, r=[st16[s_]], w=[("wbf", l, i)])

    if BF:
        for l in layers:
            cast_weights(l)
    P.dma("sp", ident[:], ident_in, w=["ident"])
    P.op("pool", lambda e: e.memset(ones_bf[:], 1.0), w=["ones_bf"])
    P.dma("sp", cv[:], cvec_in, w=["cv"])
    P.op("act", lambda e: e.activation(out=scv[:], in_=cv[:], func=AF.Silu), r=["cv"], w=["scv"])

    def load_uT(src, ntok, who):
        xts = [av(24, [128, 1024]), av(26, [128, 1024])]
        for t in range(ntok // 128):
            xv = xts[t % 2]
            P.dma("sp", xv.ap, src[t * 128:(t + 1) * 128, :], r=[("src", id(src.tensor), t)], w=[xv])
            pa, pb = ps(), ps()

            def f(e, xv=xv, pa=pa, pb=pb):
                for k in range(8):
                    pp = pa if k < 4 else pb
                    ins = e.transpose(pp.ap[:, (k % 4) * 128:(k % 4 + 1) * 128], xv.ap[:, k * 128:(k + 1) * 128], ident[:])
                return ins
            P.op("pe", f, r=[xv, "ident"], w=[pa, pb])
            for k in range(8):
                pp = pa if k < 4 else pb
                P.op("act", lambda e, k=k, pp=pp, t=t: e.activation(
                    out=uT[:, k, t * 128:(t + 1) * 128], in_=pp.ap[:, (k % 4) * 128:(k % 4 + 1) * 128],
                    func=AF.Identity, bias=modc[:, k, who:who + 1], scale=modc[:, 8 + k, who:who + 1]),
                    r=[pp, "modc"], w=["uT"])

    def rope_proj(l, pc, pcp, ntok, cosv, sinv, dst_ap, dstkeys, t1s, t2s):
        w1 = wload(l, pc)
        w2 = wload(l, pcp)
        for b in range(4):
            p1 = mm_fm(w1, b, ntok)
            p2 = mm_fm(w2, b, ntok)
            t1, t2 = t1s[b % len(t1s)], t2s[b % len(t2s)]
            P.op("dve", lambda e, p1=p1, t1=t1, b=b: e.scalar_tensor_tensor(
                out=t1.ap[:, 0:ntok], in0=p1.ap[:, 0:ntok], scalar=bcol[:, pc * 4 + b:pc * 4 + b + 1], in1=cosv.ap[:, 0:ntok],
                op0=ALU.add, op1=ALU.mult), r=[p1, cosv, "bcol"], w=[t1])
            P.op("dve", lambda e, p2=p2, t2=t2, b=b: e.scalar_tensor_tensor(
                out=t2.ap[:, 0:ntok], in0=p2.ap[:, 0:ntok], scalar=bcol[:, pcp * 4 + b:pcp * 4 + b + 1], in1=sinv.ap[:, 0:ntok],
                op0=ALU.add, op1=ALU.mult), r=[p2, sinv, "bcol"], w=[t2])
            P.op("pool", lambda e, t1=t1, t2=t2, b=b: e.tensor_tensor(
                out=dst_ap[:, b, 0:ntok], in0=t1.ap[:, 0:ntok], in1=t2.ap[:, 0:ntok], op=ALU.add),
                r=[t1, t2], w=dstkeys)

    def layer_setup(l):
        P.dma("sp", bcol[:], bcol_in[l], w=["bcol"])
        P.dma("sp", adac[:], adac_in[l], w=["adac"])
        P.dma("sp", lcol[:].rearrange("p a b c -> p (a b c)"), lcol_in[l], w=["lcol"])
        P.dma("sp", ccol[:].rearrange("p a b -> p (a b)"), ccol_in[l], w=["ccol"])
        P.dma("sp", wsT[:].rearrange("p a b -> p (a b)"), wsT_in[l], w=["wsT"])
        P.dma("sp", lruW[:].rearrange("p a b c d -> p (a b c d)"), lruW_in[l], w=["lruW"])
        for hf_ in range(2):
            bt32 = av(hf_ * 7, [128, 7 * 512])
            P.dma("sp", bt32.ap, btab_in[l][:, hf_ * 3584:(hf_ + 1) * 3584], w=[bt32])
            P.op("pool", lambda e, hf_=hf_, bt32=bt32: e.tensor_copy(out=btab[:, hf_ * 7:(hf_ + 1) * 7, :].rearrange("p a b -> p (a b)"), in_=bt32.ap),
                 r=[bt32], w=["btab"])
        P.dma("sp", lng_bc[:], rows_in[l, 1, :].partition_broadcast(128), w=["lng_bc"])
        P.dma("sp", lnb_bc[:], rows_in[l, 2, :].partition_broadcast(128), w=["lnb_bc"])
        for i, (t, nm) in enumerate([(bav_bc, "bav_bc"), (bv_bc, "bv_bc"), (sg_bc, "sg_bc"), (sb_bc, "sb_bc"), (bs_bc, "bs_bc")]):
            P.dma("sp", t[:], rows_in[l, 3 + i, 0:512].partition_broadcast(128), w=[nm])
        badag = av(24, [128, 1024])
        P.dma("sp", badag.ap, rows_in[l, 0, :].partition_broadcast(128), w=[badag])
        P.op("act", lambda e: e.activation(out=ltmp[:], in_=lcol[:, :, :, 2], func=AF.Exp, scale=-1.0), r=["lcol"], w=["ltmp"])
        P.op("act", lambda e: e.activation(out=ltmp[:], in_=ltmp[:], func=AF.Ln, bias=1.0), r=["ltmp"], w=["ltmp"])
        P.op("dve", lambda e: e.tensor_scalar(out=lcc[:], in0=ltmp[:], scalar1=-8.0, scalar2=None, op0=ALU.mult), r=["ltmp"], w=["lcc"])
        P.op("dve", lambda e: e.tensor_scalar(out=lcc2[:], in0=ltmp[:], scalar1=-16.0, scalar2=None, op0=ALU.mult), r=["ltmp"], w=["lcc2"])
        screp = av(20, [128, 16, 128])
        P.op("dve", lambda e: e.tensor_copy(out=screp.ap, in_=scv[:].rearrange("p a b -> p (a b)").unsqueeze(2).broadcast_to([128, 16, 128])),
             r=["scv"], w=[screp])
        for i in range(4):
            wv = av((i % 2) * 8, [128, 4, 8, 128])
            P.dma("sp", wv.ap.rearrange("p a b c -> p (a b c)"), wada[l, i], w=[wv])
            for b in range(4):
                blk = i * 4 + b
                p = ps()

                def f(e, wv=wv, b=b, p=p):
                    for k in range(8):
                        ins = e.matmul(p.ap[:, 0:2], lhsT=wv.ap[:, b, k, :], rhs=scv[:, k, :], start=(k == 0), stop=(k == 7))
                    return ins
                P.op("pe", f, r=[wv, "scv"], w=[p])
                P.op("dve", lambda e, p=p, blk=blk: e.tensor_scalar(
                    out=modc[:, blk, :], in0=p.ap[:, 0:2], scalar1=adac[:, blk:blk + 1], scalar2=(1.0 if blk >= 8 else 0.0),
                    op0=ALU.add, op1=ALU.add), r=[p, "adac"], w=["modc"])
        for i in range(2):
            wv = av((i % 2) * 8, [128, 8, 512])
            P.dma("sp", wv.ap.rearrange("p a b -> p (a b)"), wada[l, 4 + i], w=[wv])
            for who in range(2):
                p = ps()

                def f(e, wv=wv, who=who, p=p):
                    for k in range(8):
                        ins = e.matmul(p.ap[:, 0:512], lhsT=screp.ap[:, k * 2 + who, :], rhs=wv.ap[:, k, :], start=(k == 0), stop=(k == 7))
                    return ins
                P.op("pe", f, r=[wv, screp], w=[p])
                P.op("dve", lambda e, p=p, who=who, i=i: e.tensor_tensor(
                    out=gate_bc[:, who, i * 512:(i + 1) * 512], in0=p.ap[:, 0:512], in1=badag.ap[:, i * 512:(i + 1) * 512], op=ALU.add),
                    r=[p, badag], w=["gate_bc"])

    def pass1_group(l, src, tok0, ntok, who, g):
        nt = ntok // 128
        load_uT(src, ntok, who)
        if stop == "p1a":
            raise _Stop()
        cosv, sinv = av(0, [128, 512]), av(1, [128, 512])
        P.dma("sp", cosv.ap[:, 0:ntok], cosT_in[:, tok0:tok0 + ntok], w=[cosv])
        P.dma("sp", sinv.ap[:, 0:ntok], sinT_in[:, tok0:tok0 + ntok], w=[sinv])
        bxs = av(2, [128, 4, 512])
        w = wload(l, P_BX)
        for b in range(4):
            p = mm_fm(w, b, ntok)
            P.op("act", lambda e, p=p, b=b: e.activation(out=bxs.ap[:, b, 0:ntok], in_=p.ap[:, 0:ntok], func=AF.Identity,
                                                          bias=bcol[:, P_BX * 4 + b:P_BX * 4 + b + 1], scale=1.0),
                 r=[p, "bcol"], w=[bxs])
        if stop == "p1a2":
            raise _Stop()
        for b in range(4):
            P.dma("pool", bxT[b * 128:(b + 1) * 128, tok0:tok0 + ntok], bxs.ap[:, b, 0:ntok], r=[bxs], w=[("bxT", g)])
        if stop == "p1b":
            raise _Stop()
        kTs = av(6, [128, 4, 512], BF16)
        t1s = [av(8, [128, 512]), av(9, [128, 512])]
        t2s = [av(10, [128, 512]), av(11, [128, 512])]
        rope_proj(l, P_K, P_KP, ntok, cosv, sinv, kTs.ap, [kTs], t1s, t2s)
        for b in range(4):
            P.dma("pool", KT[b * 128:(b + 1) * 128, tok0:tok0 + ntok], kTs.ap[:, b, 0:ntok], r=[kTs], w=[("KT", g)])
        if stop == "p1c":
            raise _Stop()
        vts = av(12, [128, 4, 512], BF16)
        w = wload(l, P_V)
        wv = widev(w)
        for t in range(nt):
            p = ps()

            def f(e, t=t, p=p):
                for k in range(8):
                    ins = e.matmul(p.ap[:, 0:512], lhsT=uT[:, k, t * 128:(t + 1) * 128], rhs=wv[:, k, :], start=(k == 0), stop=(k == 7))
                return ins
            P.op("pe", f, r=[w, "uT"], w=[p])
            P.op("dve", lambda e, t=t, p=p: e.tensor_tensor(out=vts.ap[:, t, :], in0=p.ap[:, 0:512], in1=bv_bc[:], op=ALU.add),
                 r=[p, "bv_bc"], w=[vts])
        for t in range(nt):
            P.dma("pool", Vs[tok0 + t * 128:tok0 + (t + 1) * 128, :], vts.ap[:, t, :], r=[vts], w=[("V", g)])
        if stop == "p1d":
            raise _Stop()

    def lru_pass(l, with_ctx):
        NS = 1024
        sets = []
        for si in range(2):
            o = si * 18
            sets.append(dict(bxh=av(o, [128, 1536]), xc=av(o + 3, [128, NS]), rr=av(o + 5, [128, NS]), ii=av(o + 7, [128, NS]),
                             aa=av(o + 9, [128, NS]), bq=av(o + 11, [128, NS]), hh=av(o + 13, [128, NS]), hfl=av(o + 15, [128, NS]),
                             yo=av(o + 17, [128, NS], BF16)))
        allbx = [("bxT", g) for g in range(17)]

        def front(sg):
            B_ = sets[sg["i"] % 2]
            ct, d, colbase, seqlen, s0, n, is_ctx = sg["ct"], sg["d"], sg["colbase"], sg["seqlen"], sg["s0"], sg["n"], sg["is_ctx"]
            b_, xc, rr, ii, aa, bq = B_["bxh"], B_["xc"], B_["rr"], B_["ii"], B_["aa"], B_["bq"]
            lo, hi = max(s0 - 2, 0), min(s0 + n + 1, seqlen)
            if lo > s0 - 2 or hi < s0 + n + 1:
                P.op("pool", lambda e: e.memset(b_.ap[:, 0:n + 3], 0.0), w=[b_])
            P.dma("sp", b_.ap[:, lo - (s0 - 2):hi - (s0 - 2)], bxT[ct * 128:(ct + 1) * 128, colbase + lo:colbase + hi],
                  r=allbx, w=[b_])
            if d == 1 and not is_ctx:
                P.dma("sp", B_["hfl"].ap[:, 0:n], hfs[ct * 128:(ct + 1) * 128, s0:s0 + n], r=[("hfs", ct, s0)], w=[B_["hfl"]])
            P.op("dve", lambda e: e.tensor_scalar(out=xc.ap[:, 0:n], in0=b_.ap[:, 0:n], scalar1=ccol[:, ct, 0:1], scalar2=ccol[:, ct, 4:5],
                                                   op0=ALU.mult, op1=ALU.add), r=[b_, "ccol"], w=[xc])
            for j in range(1, 4):
                P.op("dve", lambda e, j=j: e.scalar_tensor_tensor(out=xc.ap[:, 0:n], in0=b_.ap[:, j:j + n], scalar=ccol[:, ct, j:j + 1],
                                                                   in1=xc.ap[:, 0:n], op0=ALU.mult, op1=ALU.add), r=[b_, xc, "ccol"], w=[xc])
            for c in range((n + 511) // 512):
                cn = min(512, n - c * 512)
                for wi, dst in ((0, rr), (1, ii)):
                    p = ps()
                    P.op("pe", lambda e, p=p, c=c, cn=cn, wi=wi: e.matmul(p.ap[:, 0:cn], lhsT=lruW[:, d, ct, wi, :], rhs=xc.ap[:, c * 512:c * 512 + cn],
                                                                           start=True, stop=True), r=[xc, "lruW"], w=[p])
                    P.op("act", lambda e, p=p, c=c, cn=cn, wi=wi, dst=dst: e.activation(
                        out=dst.ap[:, c * 512:c * 512 + cn], in_=p.ap[:, 0:cn], func=AF.Sigmoid, bias=lcol[:, d, ct, wi:wi + 1], scale=1.0),
                        r=[p, "lcol"], w=[dst])
            P.op("act", lambda e: e.activation(out=aa.ap[:, 0:n], in_=rr.ap[:, 0:n], func=AF.Exp, scale=lcc[:, d, ct:ct + 1]), r=[rr, "lcc"], w=[aa])
            P.op("act", lambda e: e.activation(out=bq.ap[:, 0:n], in_=rr.ap[:, 0:n], func=AF.Exp, scale=lcc2[:, d, ct:ct + 1]), r=[rr, "lcc2"], w=[bq])
            P.op("act", lambda e: e.activation(out=bq.ap[:, 0:n], in_=bq.ap[:, 0:n], func=AF.Sqrt, bias=1.0, scale=-1.0), r=[bq], w=[bq])
            P.op("pool", lambda e: e.tensor_tensor(out=ii.ap[:, 0:n], in0=ii.ap[:, 0:n], in1=xc.ap[:, 0:n], op=ALU.mult), r=[ii, xc], w=[ii])

        def back(sg):
            B_ = sets[sg["i"] % 2]
            ct, d, s0, n, is_ctx, first = sg["ct"], sg["d"], sg["s0"], sg["n"], sg["is_ctx"], sg["first"]
            ii, aa, bq, hh, yo, hf_ = B_["ii"], B_["aa"], B_["bq"], B_["hh"], B_["yo"], B_["hfl"]
            P.op("dve", lambda e: e.tensor_tensor(out=bq.ap[:, 0:n], in0=bq.ap[:, 0:n], in1=ii.ap[:, 0:n], op=ALU.mult), r=[bq, ii], w=[bq])
            init = 0.0 if first else carry[:, 0:1]
            if d == 0:
                P.op("dve", lambda e: e.tensor_tensor_scan(out=hh.ap[:, 0:n], data0=aa.ap[:, 0:n], data1=bq.ap[:, 0:n], initial=init,
                                                            op0=ALU.mult, op1=ALU.add), r=[aa, bq, "carry"], w=[hh])
                P.op("dve", lambda e: e.tensor_copy(out=carry[:], in_=hh.ap[:, n - 1:n]), r=[hh], w=["carry"])
            else:
                P.op("dve", lambda e: e.tensor_tensor_scan(out=rev_ap(hh, n), data0=rev_ap(aa, n), data1=rev_ap(bq, n), initial=init,
                                                            op0=ALU.mult, op1=ALU.add), r=[aa, bq, "carry"], w=[hh])
                P.op("dve", lambda e: e.tensor_copy(out=carry[:], in_=hh.ap[:, 0:1]), r=[hh], w=["carry"])
            rows = slice(ct * 128, (ct + 1) * 128)
            if d == 0:
                if is_ctx:
                    P.op("pool", lambda e: e.tensor_copy(out=hcf[:], in_=hh.ap[:, 0:n]), r=[hh], w=["hcf"])
                else:
                    P.dma("pool", hfs[rows, s0:s0 + n], hh.ap[:, 0:n], r=[hh], w=[("hfs", ct, s0)])
            else:
                if is_ctx:
                    if with_ctx:
                        P.op("pool", lambda e: e.tensor_tensor(out=yo.ap[:, 0:n], in0=hh.ap[:, 0:n], in1=hcf[:], op=ALU.add), r=[hh, "hcf"], w=[yo])
                        P.dma("pool", ybT[rows, SEQ:SEQ + n], yo.ap[:, 0:n], r=[yo], w=[("ybT", ct, "c")])
                else:
                    P.op("pool", lambda e: e.tensor_tensor(out=yo.ap[:, 0:n], in0=hh.ap[:, 0:n], in1=hf_.ap[:, 0:n], op=ALU.add), r=[hh, hf_], w=[yo])
                    P.dma("pool", ybT[rows, s0:s0 + n], yo.ap[:, 0:n], r=[yo], w=[("ybT", ct, s0)])

        segs = []
        for ct in range(4):
            for d in range(2):
                segs.append(dict(ct=ct, d=d, colbase=SEQ, seqlen=CTX, s0=0, n=CTX, first=True, is_ctx=True))
                order = range(0, SEQ, NS) if d == 0 else range(SEQ - NS, -1, -NS)
                for s0 in order:
                    segs.append(dict(ct=ct, d=d, colbase=0, seqlen=SEQ, s0=s0, n=NS, first=False, is_ctx=False))
        for i, sg in enumerate(segs):
            sg["i"] = i
        front(segs[0])
        for i in range(len(segs)):
            if i + 1 < len(segs):
                front(segs[i + 1])
            back(segs[i])

    def attention(l, g, tok0, ntok, is_ctx, qT, ycT):
        nrow = ntok // 64
        kwin = av(4, [128, 4, 1024], BF16)
        vwin = [av(8, [128, 4, 512], BF16), av(10, [128, 4, 512], BF16)]
        pTs = [av(12, [128, 6, 512], BF16), av(25, [128, 6, 512], BF16)]
        sps = [av(15, [128, 512]), av(16, [128, 512])]
        rz = av(17, [128, 512])
        if not is_ctx:
            t_lo, t_hi = max(0, tok0 - 256), min(SEQ, tok0 + 512 + 256)
            gs = [gg for gg in (g - 1, g, g + 1) if 0 <= gg < 16]
            for b in range(4):
                P.dma("sp", kwin.ap[:, b, 0:t_hi - t_lo], KT[b * 128:(b + 1) * 128, t_lo:t_hi],
                      r=[("KT", gg) for gg in gs], w=[kwin])
        for rq in range(nrow):
            pT = pTs[rq % 2]
            tiles = []
            if not is_ctx:
                r = (tok0 // 64) + rq
                r0 = min(max(r - 4, 0), 120)
                jv = r0 - r + 7
                koff = r0 * 64 - t_lo
                vw = vwin[rq % 2]
                for jj_ in range(4):
                    P.dma("sp", vw.ap[:, jj_, :], Vs[r0 * 64 + jj_ * 128:r0 * 64 + (jj_ + 1) * 128, :],
                          r=[("V", gg) for gg in gs], w=[vw])
                tiles += [("loc", j) for j in range(4)]
            tiles += [("ctx", 0), ("ctx", 1)]
            ntile = len(tiles)
            for ti, (kind, j) in enumerate(tiles):
                p = psp()
                p3 = p.ap.rearrange("p (e y) -> p e y", e=2)[:, :, 0:256]

                def f(e, kind=kind, j=j, p=p, rq=rq, koff=(koff if not is_ctx else 0)):
                    for h in range(8):
                        c, ee = h // 2, h % 2
                        if kind == "loc":
                            lt = kwin.ap[64 * ee:64 * ee + 64, c, koff + j * 128:koff + (j + 1) * 128]
                        else:
                            lt = kctx[64 * ee:64 * ee + 64, c, j * 128:(j + 1) * 128]
                        ins = e.matmul(p.ap[:, ee * 512 + c * 64:ee * 512 + (c + 1) * 64], lhsT=lt,
                                       rhs=qT.ap[64 * ee:64 * ee + 64, c, rq * 64:(rq + 1) * 64], start=True, stop=True)
                    return ins
                P.op("pe", f, r=[kwin if kind == "loc" else "kctx", qT], w=[p])
                if kind == "loc":
                    sp_ = sps[ti % 2]
                    P.op("dve", lambda e, p3=p3, sp_=sp_, j=j, jv=jv: e.scalar_tensor_tensor(
                        out=sp_.ap.rearrange("p (e x) -> p e x", e=2), in0=p3, scalar=0.125,
                        in1=btab[:, 2 * j + jv, :].rearrange("p (e x) -> p e x", e=2), op0=ALU.mult, op1=ALU.add),
                        r=[p, "btab"], w=[sp_])
                    P.op("act", lambda e, sp_=sp_, ti=ti, pT=pT: e.activation(out=pT.ap[:, ti, :], in_=sp_.ap, func=AF.Exp),
                         r=[sp_], w=[pT])
                else:
                    P.op("act", lambda e, p3=p3, ti=ti, pT=pT: e.activation(out=pT.ap[:, ti, :].rearrange("p (e x) -> p e x", e=2), in_=p3,
                                                                              func=AF.Exp, scale=0.125), r=[p], w=[pT])
            if stop == "p2s":
                raise _Stop()
            po, pz = ps(), ps()

            def fo(e, tiles=tiles, po=po, pT=pT, vw=(vw if not is_ctx else None)):
                for h in range(8):
                    c, ee = h // 2, h % 2
                    for ti, (kind, j) in enumerate(tiles):
                        lt = vw.ap[:, j, h * 64:(h + 1) * 64] if kind == "loc" else vctx[:, j, h * 64:(h + 1) * 64]
                        ins = e.matmul(po.ap[64 * ee:64 * ee + 64, c * 64:(c + 1) * 64], lhsT=lt, rhs=pT.ap[:, ti, ee * 256 + c * 64:ee * 256 + (c + 1) * 64],
                                       start=(ti == 0), stop=(ti == len(tiles) - 1), tile_position=(0, 64 * ee))
                return ins
            P.op("pe", fo, r=[pT, "vctx"] + ([vw] if not is_ctx else []), w=[po])

            def fz(e, pz=pz, pT=pT, ntile=ntile):
                for ti in range(ntile):
                    ins = e.matmul(pz.ap[:, 0:512], lhsT=ones_bf[:], rhs=pT.ap[:, ti, :], start=(ti == 0), stop=(ti == ntile - 1))
                return ins
            if stop == "p2o":
                raise _Stop()
            P.op("pe", fz, r=[pT, "ones_bf"], w=[pz])
            if stop == "p2z":
                raise _Stop()
            P.op("dve", lambda e, pz=pz: e.reciprocal(out=rz.ap, in_=pz.ap[:, 0:512]), r=[pz], w=[rz])
            for ee in range(2):
                P.op("dve", lambda e, ee=ee, po=po, rq=rq: e.tensor_tensor(
                    out=ycT.ap[64 * ee:64 * ee + 64, :, rq * 64:(rq + 1) * 64],
                    in0=po.ap[64 * ee:64 * ee + 64, 0:256].rearrange("p (c q) -> p c q", c=4),
                    in1=rz.ap[64 * ee:64 * ee + 64, ee * 256:(ee + 1) * 256].rearrange("p (c q) -> p c q", c=4),
                    op=ALU.mult), r=[po, rz], w=[ycT])

    def pass2_group(l, src, dst, dstkey, tok0, ntok, who, is_ctx, g):
        nt = ntok // 128
        load_uT(src, ntok, who)
        gu = av(0, [128, 4, 512]); sgA = av(4, [128, 4, 512]); vn = av(8, [128, 4, 512])
        vpre = av(12, [128, 512]); gv = av(13, [128, 512]); ta = av(14, [128, 512])
        for pc, dstv, fn in ((P_AU, gu, AF.Gelu), (P_AG, sgA, AF.Silu)):
            w = wload(l, pc)
            for b in range(4):
                p = mm_fm(w, b, ntok)
                P.op("act", lambda e, p=p, b=b, dstv=dstv, fn=fn, pc=pc: e.activation(
                    out=dstv.ap[:, b, 0:ntok], in_=p.ap[:, 0:ntok], func=fn, bias=bcol[:, pc * 4 + b:pc * 4 + b + 1], scale=1.0),
                    r=[p, "bcol"], w=[dstv])
        w = wload(l, P_AV)
        wv = widev(w)
        for t in range(nt):
            p = ps()

            def f(e, t=t, p=p, wv=wv):
                for k in range(8):
                    ins = e.matmul(p.ap[:, 0:512], lhsT=uT[:, k, t * 128:(t + 1) * 128], rhs=wv[:, k, :], start=(k == 0), stop=(k == 7))
                return ins
            P.op("pe", f, r=[w, "uT"], w=[p])
            P.op("dve", lambda e, p=p: e.tensor_tensor(out=vpre.ap, in0=p.ap[:, 0:512], in1=bav_bc[:], op=ALU.add), r=[p, "bav_bc"], w=[vpre])
            P.op("act", lambda e: e.activation(out=gv.ap, in_=vpre.ap, func=AF.Gelu), r=[vpre], w=[gv])
            P.op("dve", lambda e: e.bn_stats(out=st6[:, 0:6], in_=gv.ap), r=[gv], w=["st6"])
            P.op("dve", lambda e: e.bn_aggr(out=mv2[:], in_=st6[:, 0:6]), r=["st6"], w=["mv2"])
            P.op("dve", lambda e: e.tensor_scalar(out=rs1[:], in0=mv2[:, 1:2], scalar1=LN_EPS, scalar2=None, op0=ALU.add), r=["mv2"], w=["rs1"])
            P.op("act", lambda e: e.activation(out=rs1[:], in_=rs1[:], func=AF.Sqrt), r=["rs1"], w=["rs1"])
            P.op("dve", lambda e: e.reciprocal(out=rs1[:], in_=rs1[:]), r=["rs1"], w=["rs1"])
            P.op("dve", lambda e, t=t: e.tensor_scalar(out=vn.ap[:, t, :], in0=gv.ap, scalar1=mv2[:, 0:1], scalar2=rs1[:, 0:1],
                                                       op0=ALU.subtract, op1=ALU.mult), r=[gv, "mv2", "rs1"], w=[vn])
            P.op("pool", lambda e, t=t: e.tensor_tensor(out=vn.ap[:, t, :], in0=vn.ap[:, t, :], in1=sg_bc[:], op=ALU.mult), r=[vn, "sg_bc"], w=[vn])
            P.op("pool", lambda e, t=t: e.tensor_tensor(out=vn.ap[:, t, :], in0=vn.ap[:, t, :], in1=sb_bc[:], op=ALU.add), r=[vn, "sb_bc"], w=[vn])
        for g4 in range(4):
            p = ps()

            def f(e, g4=g4, p=p):
                for t in range(nt):
                    ins = e.matmul(p.ap[:, t * 128:(t + 1) * 128], lhsT=vn.ap[:, t, g4 * 128:(g4 + 1) * 128], rhs=wsT[:, g4, :], start=True, stop=True)
                return ins
            P.op("pe", f, r=[vn, "wsT"], w=[p])
            P.op("dve", lambda e, g4=g4, p=p: e.tensor_tensor(
                out=ta.ap[:, 0:ntok].rearrange("p (t q) -> p t q", t=nt), in0=p.ap[:, 0:ntok].rearrange("p (t q) -> p t q", t=nt),
                in1=bs_bc[:, g4 * 128:(g4 + 1) * 128].unsqueeze(1).broadcast_to([128, nt, 128]), op=ALU.add), r=[p, "bs_bc"], w=[ta])
            P.op("dve", lambda e, g4=g4: e.tensor_tensor(out=ta.ap[:, 0:ntok], in0=ta.ap[:, 0:ntok], in1=gu.ap[:, g4, 0:ntok], op=ALU.mult),
                 r=[ta, gu], w=[ta])
            P.op("pool", lambda e, g4=g4: e.tensor_tensor(out=yaT[:, g4, 0:ntok], in0=ta.ap[:, 0:ntok], in1=sgA.ap[:, g4, 0:ntok], op=ALU.mult),
                 r=[ta, sgA], w=["yaT"])
        if stop == "p2a":
            raise _Stop()
        ybl = av(16, [128, 4, 512], BF16)
        tmpb = av(12, [128, 512])
        ycol = SEQ if is_ctx else tok0
        for b in range(4):
            P.dma("sp", ybl.ap[:, b, 0:ntok], ybT[b * 128:(b + 1) * 128, ycol:ycol + ntok],
                  r=[("ybT", b, "c") if is_ctx else ("ybT", b, (tok0 // 1024) * 1024)], w=[ybl])
        w = wload(l, P_BG)
        for b in range(4):
            p = mm_fm(w, b, ntok)
            P.op("act", lambda e, p=p, b=b: e.activation(out=tmpb.ap[:, 0:ntok], in_=p.ap[:, 0:ntok], func=AF.Silu,
                                                          bias=bcol[:, P_BG * 4 + b:P_BG * 4 + b + 1], scale=1.0), r=[p, "bcol"], w=[tmpb])
            P.op("dve", lambda e, b=b: e.tensor_tensor(out=ybs[:, b, 0:ntok], in0=tmpb.ap[:, 0:ntok], in1=ybl.ap[:, b, 0:ntok], op=ALU.mult),
                 r=[tmpb, ybl], w=["ybs"])
        if stop == "p2b":
            raise _Stop()
        cosv, sinv = av(0, [128, 512]), av(1, [128, 512])
        P.dma("sp", cosv.ap[:, 0:ntok], cosT_in[:, tok0:tok0 + ntok], w=[cosv])
        P.dma("sp", sinv.ap[:, 0:ntok], sinT_in[:, tok0:tok0 + ntok], w=[sinv])
        qT = av(23, [128, 4, 512], BF16)
        ycT = av(18, [128, 4, 512])
        rope_proj(l, P_Q, P_QP, ntok, cosv, sinv, qT.ap, [qT], [av(2, [128, 512])], [av(3, [128, 512])])
        if stop == "p2q":
            raise _Stop()
        attention(l, g, tok0, ntok, is_ctx, qT, ycT)
        if stop == "p2c":
            raise _Stop()
        if "dbgy" in dbg and l == 0 and (is_ctx or g == 0):
            nm = "c" if is_ctx else "l"
            dya = nc.dram_tensor("dbg_ya_" + nm, [512, ntok], F32, kind="ExternalOutput").ap()
            dyc = nc.dram_tensor("dbg_yc_" + nm, [512, ntok], F32, kind="ExternalOutput").ap()
            dyb = nc.dram_tensor("dbg_yb_" + nm, [512, ntok], F32, kind="ExternalOutput").ap()
            for b in range(4):
                P.dma("sp", dya[b * 128:(b + 1) * 128, :], yaT[:, b, 0:ntok], r=["yaT"], w=[("dbg", nm, 0, b)])
                P.dma("sp", dyc[b * 128:(b + 1) * 128, :], ycT.ap[:, b, 0:ntok], r=[ycT], w=[("dbg", nm, 1, b)])
                P.dma("sp", dyb[b * 128:(b + 1) * 128, :], ybs[:, b, 0:ntok], r=["ybs"], w=[("dbg", nm, 2, b)])
        tmpc = av(22, [128, 512])
        w = wload(l, P_CG)
        for b in range(4):
            p = mm_fm(w, b, ntok)
            P.op("act", lambda e, p=p, b=b: e.activation(out=tmpc.ap[:, 0:ntok], in_=p.ap[:, 0:ntok], func=AF.Silu,
                                                          bias=bcol[:, P_CG * 4 + b:P_CG * 4 + b + 1], scale=1.0), r=[p, "bcol"], w=[tmpc])
            P.op("dve", lambda e, b=b: e.tensor_tensor(out=ycs[:, b, 0:ntok], in0=tmpc.ap[:, 0:ntok], in1=ycT.ap[:, b, 0:ntok], op=ALU.mult),
                 r=[tmpc, ycT], w=["ycs"])
        sgs = [av(0, [128, 512]), av(1, [128, 512])]
        tms = [av(2, [128, 512]), av(3, [128, 512])]
        macc = av(4, [128, 4, 512])
        ys = [(yaT, "yaT"), (ybs, "ybs"), (ycs, "ycs")]
        cnt = 0
        for hh_ in range(2):
            for n in range(3):
                pcg = P_GM + hh_ * 3 + n
                wg = wload(l, pcg)
                wbr = wload(l, P_WBR + hh_ * 3 + n, 2048)
                wbv = widev(wbr, 4)
                yt, yk = ys[n]
                for jj in range(4):
                    j = hh_ * 4 + jj
                    pg = mm_fm(wg, jj, ntok)
                    pb = ps()

                    def f(e, pb=pb, jj=jj, yt=yt, wbv=wbv):
                        for k in range(4):
                            ins = e.matmul(pb.ap[:, 0:ntok], lhsT=wbv[:, k, jj * 128:(jj + 1) * 128], rhs=yt[:, k, 0:ntok], start=(k == 0), stop=(k == 3))
                        return ins
                    P.op("pe", f, r=[wbr, yk], w=[pb])
                    sg = sgs[cnt % 2]
                    tm = tms[cnt % 2]
                    cnt += 1
                    P.op("act", lambda e, pg=pg, sg=sg, jj=jj, pcg=pcg: e.activation(
                        out=sg.ap[:, 0:ntok], in_=pg.ap[:, 0:ntok], func=AF.Sigmoid, bias=bcol[:, pcg * 4 + jj:pcg * 4 + jj + 1], scale=1.0),
                        r=[pg, "bcol"], w=[sg])
                    if n == 0:
                        P.op("dve", lambda e, sg=sg, pb=pb, jj=jj: e.tensor_tensor(out=macc.ap[:, jj, 0:ntok], in0=sg.ap[:, 0:ntok], in1=pb.ap[:, 0:ntok], op=ALU.mult),
                             r=[sg, pb], w=[macc])
                    else:
                        P.op("dve", lambda e, sg=sg, pb=pb, tm=tm: e.tensor_tensor(out=tm.ap[:, 0:ntok], in0=sg.ap[:, 0:ntok], in1=pb.ap[:, 0:ntok], op=ALU.mult),
                             r=[sg, pb], w=[tm])
                        if n == 1:
                            P.op("pool", lambda e, tm=tm, jj=jj: e.tensor_tensor(out=macc.ap[:, jj, 0:ntok], in0=macc.ap[:, jj, 0:ntok], in1=tm.ap[:, 0:ntok], op=ALU.add),
                                 r=[tm, macc], w=[macc])
                        else:
                            P.op("pool", lambda e, tm=tm, jj=jj, j=j: e.tensor_tensor(out=mT[:, j, 0:ntok], in0=macc.ap[:, jj, 0:ntok], in1=tm.ap[:, 0:ntok], op=ALU.add),
                                 r=[tm, macc], w=["mT"])
        if stop == "p2d":
            raise _Stop()
        xres = av(8, [128, 4, 1024])
        tmo = [av(16, [128, 512]), av(17, [128, 512])]
        xn = [av(18, [128, 1024]), av(20, [128, 1024])]
        for t in range(nt):
            P.dma("sp", xres.ap[:, t, :], src[t * 128:(t + 1) * 128, :], r=[("src", id(src.tensor), t)], w=[xres])
        cnt = 0
        for half in range(2):
            w = wload(l, P_WOUT + half)
            wv = widev(w)
            for t in range(nt):
                p = ps()

                def f(e, t=t, p=p, wv=wv):
                    for k in range(8):
                        ins = e.matmul(p.ap[:, 0:512], lhsT=mT[:, k, t * 128:(t + 1) * 128], rhs=wv[:, k, :], start=(k == 0), stop=(k == 7))
                    return ins
                P.op("pe", f, r=[w, "mT"], w=[p])
                tm = tmo[cnt % 2]
                cnt += 1
                P.op("dve", lambda e, p=p, tm=tm, half=half: e.tensor_tensor(out=tm.ap, in0=p.ap[:, 0:512], in1=gate_bc[:, who, half * 512:(half + 1) * 512], op=ALU.mult),
                     r=[p, "gate_bc"], w=[tm])
                P.op("dve", lambda e, tm=tm, t=t, half=half: e.scalar_tensor_tensor(
                    out=xres.ap[:, t, half * 512:(half + 1) * 512], in0=xres.ap[:, t, half * 512:(half + 1) * 512], scalar=float(ALPHA), in1=tm.ap,
                    op0=ALU.mult, op1=ALU.add), r=[tm, xres], w=[xres])
        for t in range(nt):
            xo = xn[t % 2]
            P.op("dve", lambda e, t=t: e.bn_stats(out=st6[:, 0:6], in_=xres.ap[:, t, 0:512]), r=[xres], w=["st6"])
            P.op("dve", lambda e, t=t: e.bn_stats(out=st6[:, 6:12], in_=xres.ap[:, t, 512:1024]), r=[xres, "st6"], w=["st6"])
            P.op("dve", lambda e: e.bn_aggr(out=mv2[:], in_=st6[:, 0:12]), r=["st6"], w=["mv2"])
            P.op("dve", lambda e: e.tensor_scalar(out=rs1[:], in0=mv2[:, 1:2], scalar1=LN_EPS, scalar2=None, op0=ALU.add), r=["mv2"], w=["rs1"])
            P.op("act", lambda e: e.activation(out=rs1[:], in_=rs1[:], func=AF.Sqrt), r=["rs1"], w=["rs1"])
            P.op("dve", lambda e: e.reciprocal(out=rs1[:], in_=rs1[:]), r=["rs1"], w=["rs1"])
            P.op("dve", lambda e, t=t, xo=xo: e.tensor_scalar(out=xo.ap, in0=xres.ap[:, t, :], scalar1=mv2[:, 0:1], scalar2=rs1[:, 0:1],
                                                              op0=ALU.subtract, op1=ALU.mult), r=[xres, "mv2", "rs1"], w=[xo])
            P.op("pool", lambda e, xo=xo: e.tensor_tensor(out=xo.ap, in0=xo.ap, in1=lng_bc[:], op=ALU.mult), r=[xo, "lng_bc"], w=[xo])
            P.op("pool", lambda e, xo=xo: e.tensor_tensor(out=xo.ap, in0=xo.ap, in1=lnb_bc[:], op=ALU.add), r=[xo, "lnb_bc"], w=[xo])
            P.dma("pool", dst[t * 128:(t + 1) * 128, :], xo.ap, r=[xo], w=[(dstkey, id(dst.tensor), t)])
        if stop == "p2e" or (stop == "p2g" and not is_ctx):
            raise _Stop()

    nl = len(layers)
    try:
      for li, l in enumerate(layers):
          last = (l == DEPTH - 1)
          with_ctx = not last
          xs = x_in if l == 0 else x1
          cs = ctx_in if l == 0 else xc1
          xd = out_d if last else x1
          layer_setup(l)
          if stop == "setup":
              break
          pass1_group(l, cs, SEQ, CTX, 1, 16)
          for g in range(16):
              pass1_group(l, xs[g * 512:(g + 1) * 512, :], g * 512, 512, 0, g)
          if stop == "pass1":
              break
          lru_pass(l, with_ctx)
          if stop == "lru":
              break
          P.dma("sp", kctx[:], KT.rearrange("(b p) t -> p b t", p=128)[:, :, SEQ:NTOK], r=[("KT", 16)], w=["kctx"])
          P.dma("sp", vctx[:], Vs[SEQ:NTOK, :].rearrange("(j p) c -> p j c", p=128), r=[("V", 16)], w=["vctx"])
          if with_ctx:
              pass2_group(l, cs, xc1, "src", SEQ, CTX, 1, True, 16)
          for g in range(16):
              pass2_group(l, xs[g * 512:(g + 1) * 512, :], xd[g * 512:(g + 1) * 512, :], "src", g * 512, 512, 0, False, g)
    except _Stop:
        pass
    P.finish()
    P.emit()
    return nc, P


_CACHE = {}


def kernel(**inputs):
    inp = {k: np.asarray(v) for k, v in inputs.items()}
    sh = prep_shared(inp)

    def fl(a, n):
        return np.ascontiguousarray(a.reshape(a.shape[:n] + (-1,)))

    shared = dict(wcat=sh["wcat"], wada=sh["wada"], bcol=sh["bcol"], rows=sh["rows"], wsT=fl(sh["wsT"], 2),
                  lruW=fl(sh["lruW"], 2), lcol=fl(sh["lcol"], 2), ccol=fl(sh["ccol"], 2), adac=sh["adac"],
                  btab=fl(sh["btab"], 2), cosT=sh["cosT"], sinT=sh["sinT"], ident=sh["ident"])
    in_maps = []
    for b in range(NB):
        m = prep_core(inp, b)
        m.update(shared)
        in_maps.append(m)
    if "nc" not in _CACHE:
        _CACHE["nc"] = build()[0]
    res = run_bass_kernel_spmd(_CACHE["nc"], in_maps, core_ids=list(range(NB)))
    return np.stack([np.asarray(r["out"], dtype=np.float32) for r in res.results], axis=0)
```

```python
import numpy as np
import concourse.bass as bass
import concourse.mybir as mybir
from concourse.bass_utils import run_bass_kernel_spmd

F32 = mybir.dt.float32
BF16 = mybir.dt.bfloat16
AF = mybir.ActivationFunctionType
ALU = mybir.AluOpType

D = 1024
SEQ = 8192
CTX = 256
NTOK = SEQ + CTX
DEPTH = 2
NB = 8
ALPHA = (2 * DEPTH) ** 0.25
LN_EPS = 1e-5
NEG = -30000.0
NPIECE = 25
PSZ = 4096

ENGS = ("pe", "act", "dve", "pool", "sp")


class V:
    def __init__(self, ap, keys):
        self.ap = ap
        self.keys = list(keys)


def _keys(items):
    out = []
    for it in items:
        if isinstance(it, V):
            out.extend(it.keys)
        else:
            out.append(it)
    return out


class Prog:
    def __init__(self, nc):
        self.nc = nc
        self.q = {e: [] for e in ENGS}
        self.esem = {e: nc.alloc_semaphore("s_" + e) for e in ("pe", "act", "dve", "pool")}
        self.ecnt = {e: 0 for e in self.esem}
        self.dsem = {}
        self.last_w = {}
        self.readers = {}
        self.waited = {e: {} for e in ENGS}
        self.n_ops = 0
        self.ps_i = 0

    def _deps(self, eng, reads, writes):
        toks = {}

        def add(t):
            if t is None:
                return
            s, v = t
            k = id(s)
            if k not in toks or toks[k][1] < v:
                toks[k] = (s, v)

        for r in reads:
            add(self.last_w.get(r))
        for w in writes:
            add(self.last_w.get(w))
            for t in self.readers.get(w, {}).values():
                add(t)
        out = []
        wd = self.waited[eng]
        for k, (s, v) in toks.items():
            if wd.get(k, 0) >= v:
                continue
            wd[k] = v
            out.append((s, v))
        return out

    def _commit(self, tok, reads, writes):
        for w in writes:
            self.last_w[w] = tok
            self.readers[w] = {}
        for r in reads:
            self.readers.setdefault(r, {})[id(tok[0])] = tok

    def op(self, eng, fn, r=(), w=()):
        reads, writes = _keys(r), _keys(w)
        waits = self._deps(eng, reads, writes)
        sem = self.esem[eng]
        self.ecnt[eng] += 1
        tok = (sem, self.ecnt[eng])

        def run(e, waits=waits, fn=fn, sem=sem):
            for s, v in waits:
                e.wait_ge(s, v)
            fn(e).then_inc(sem, 1)

        self.q[eng].append(run)
        self._commit(tok, reads, writes)
        self.n_ops += 1
        return tok

    def dma(self, eng, out, in_, r=(), w=(), semkey=None):
        reads, writes = _keys(r), _keys(w)
        if semkey is None:
            semkey = next(k for k in (writes + reads) if isinstance(k, str) or k[0] == "pg")
        if semkey not in self.dsem:
            self.dsem[semkey] = [self.nc.alloc_semaphore("d%d" % len(self.dsem)), 0]
        ent = self.dsem[semkey]
        waits = self._deps(eng, reads, writes)
        ent[1] += 16
        tok = (ent[0], ent[1])

        def run(e, waits=waits, sem=ent[0], out=out, in_=in_):
            for s, v in waits:
                e.wait_ge(s, v)
            e.dma_start(out=out, in_=in_).then_inc(sem, 16)

        self.q[eng].append(run)
        self._commit(tok, reads, writes)
        self.n_ops += 1
        return tok

    def finish(self, eng="sp"):
        ents = [tuple(v) for v in self.dsem.values()]
        ecs = [(self.esem[e], self.ecnt[e]) for e in self.esem if self.ecnt[e] > 0]

        def run(e):
            for sem, cnt in ents:
                e.wait_ge(sem, cnt)
            for sem, cnt in ecs:
                e.wait_ge(sem, cnt)

        self.q[eng].append(run)

    def emit(self):
        with self.nc.Block() as block:
            @block.tensor
            def _(e):
                for f in self.q["pe"]:
                    f(e)

            @block.scalar
            def _(e):
                for f in self.q["act"]:
                    f(e)

            @block.vector
            def _(e):
                for f in self.q["dve"]:
                    f(e)

            @block.gpsimd
            def _(e):
                for f in self.q["pool"]:
                    f(e)

            @block.sync
            def _(e):
                for f in self.q["sp"]:
                    f(e)


def _partner():
    d = np.arange(64)
    return np.where(d < 16, d + 16, np.where(d < 32, d - 16, np.where(d < 48, d + 16, d - 16)))


def _piece_cols():
    ar = np.arange(512)
    permcols = (np.arange(8)[:, None] * 64 + _partner()[None, :]).reshape(-1)
    pcs = [(1536 + ar, "fm"), (3072 + ar, "fm"), (3072 + permcols, "fm"), (3584 + ar, "wide"),
           (0 + ar, "fm"), (512 + ar, "wide"), (1024 + ar, "fm"), (2048 + ar, "fm"),
           (2560 + ar, "fm"), (2560 + permcols, "fm"), (4096 + ar, "fm")]
    for hh in range(2):
        for n in range(3):
            pcs.append((4608 + n * 1024 + hh * 512 + ar, "fm"))
    return pcs


P_BX, P_K, P_KP, P_V, P_AU, P_AV, P_AG, P_BG, P_Q, P_QP, P_CG = range(11)
P_GM = 11
P_WBR = 17
P_WOUT = 23


def _fm(Wc):
    return np.ascontiguousarray(Wc.reshape(8, 128, 4, 128).transpose(1, 2, 0, 3)).reshape(128, PSZ)


def _wide(Wc):
    return np.ascontiguousarray(Wc.reshape(8, 128, 512).transpose(1, 0, 2)).reshape(128, PSZ)


def _col(v, n):
    return np.ascontiguousarray(np.asarray(v).reshape(n, 128).T)


def prep_shared(inp):
    f32 = np.float32
    pcs = _piece_cols()
    sh = {}
    wcat = np.zeros((DEPTH, NPIECE, 128, PSZ), f32)
    wada = np.zeros((DEPTH, 6, 128, PSZ), f32)
    bcol = np.zeros((DEPTH, 128, 17 * 4), f32)
    rows = np.zeros((DEPTH, 9, 1024), f32)
    wsT = np.zeros((DEPTH, 128, 4, 128), f32)
    lruW = np.zeros((DEPTH, 128, 2, 4, 2, 128), f32)
    lcol = np.zeros((DEPTH, 128, 2, 4, 3), f32)
    ccol = np.zeros((DEPTH, 128, 4, 5), f32)
    adac = np.zeros((DEPTH, 128, 16), f32)
    btab = np.full((DEPTH, 128, 14, 512), NEG, f32)
    qc = np.arange(64)
    c0 = np.clip(qc - 8, 0, 48)
    kc = np.arange(64)
    valid = (kc[:, None] >= c0[None, :]) & (kc[:, None] < c0[None, :] + 16)
    bidx = np.clip(kc[:, None] - qc[None, :] + 15, 0, 30)
    for l in range(DEPTH):
        w_in, b_in = inp["w_in"][l], inp["b_in"][l]
        for i, (cols, kind) in enumerate(pcs):
            wcat[l, i] = _fm(w_in[:, cols]) if kind == "fm" else _wide(w_in[:, cols])
            if kind == "fm":
                bcol[l, :, i * 4:(i + 1) * 4] = _col(b_in[cols], 4)
        for hh in range(2):
            for n in range(3):
                wb = inp["w_br"][l][n][:, hh * 512:(hh + 1) * 512]
                wcat[l, P_WBR + hh * 3 + n, :, :2048] = np.ascontiguousarray(
                    wb.reshape(4, 128, 512).transpose(1, 0, 2)).reshape(128, 2048)
        for half in range(2):
            wcat[l, P_WOUT + half] = _wide(inp["w_out"][l][:, half * 512:(half + 1) * 512])
        wa = inp["w_ada"][l]
        for i in range(4):
            wada[l, i] = _fm(wa[:, i * 512:(i + 1) * 512])
        for i in range(2):
            wada[l, 4 + i] = _wide(wa[:, 2048 + i * 512:2048 + (i + 1) * 512])
        adac[l] = _col(inp["b_ada"][l][:2048], 16)
        rows[l, 0] = inp["b_ada"][l][2048:]
        rows[l, 1] = inp["ln_g"][l]
        rows[l, 2] = inp["ln_b"][l]
        rows[l, 3, :512] = b_in[512:1024]
        rows[l, 4, :512] = b_in[3584:4096]
        rows[l, 5, :512] = inp["sgu_ln_g"][l]
        rows[l, 6, :512] = inp["sgu_ln_b"][l]
        rows[l, 7, :512] = inp["b_s"][l].reshape(-1)
        wsT[l] = inp["w_s"][l].transpose(2, 0, 1)
        for d in range(2):
            for ct in range(4):
                for wi, nm in enumerate(("lru_wa", "lru_wx")):
                    for e in range(2):
                        lruW[l, e * 64:(e + 1) * 64, d, ct, wi, e * 64:(e + 1) * 64] = inp[nm][l][d][2 * ct + e]
            lcol[l, :, d, :, 0] = _col(inp["lru_ba"][l][d], 4)
            lcol[l, :, d, :, 1] = _col(inp["lru_bx"][l][d], 4)
            lcol[l, :, d, :, 2] = _col(inp["lru_lam"][l][d], 4)
        for j in range(4):
            ccol[l, :, :, j] = _col(inp["conv_w"][l][j], 4)
        ccol[l, :, :, 4] = _col(inp["conv_b"][l], 4)
        rpb = inp["rpb"][l]
        for m in range(14):
            for e in range(2):
                g = rpb[:, m + e][:, bidx]
                g = np.where(valid[None], g, f32(NEG))
                g = g.reshape(4, 2, 64, 64).transpose(2, 1, 0, 3)
                btab[l, e * 64:(e + 1) * 64, m] = g.reshape(64, 512)
    inv_freq = (f32(10000.0) ** (-np.arange(16, dtype=f32) / f32(16))).astype(f32)
    pos = np.arange(SEQ)
    rw, cl = (pos // 64).astype(f32), (pos % 64).astype(f32)
    cosT = np.ones((128, NTOK), f32)
    sinT = np.zeros((128, NTOK), f32)
    for d in range(64):
        p = rw if d < 32 else cl
        ang = (p * inv_freq[d % 16]).astype(f32)
        sgn = -1.0 if (d % 32) < 16 else 1.0
        for e in range(2):
            cosT[e * 64 + d, :SEQ] = np.cos(ang)
            sinT[e * 64 + d, :SEQ] = sgn * np.sin(ang)
    sh.update(wcat=wcat, wada=wada, bcol=bcol, rows=rows, wsT=wsT, lruW=lruW, lcol=lcol, ccol=ccol,
              adac=adac, btab=btab, cosT=cosT, sinT=sinT, ident=np.eye(128, dtype=f32))
    return sh


def prep_core(inp, b):
    cvec = np.zeros((128, 8, 2), np.float32)
    cvec[:, :, 0] = _col(inp["c"][b], 8)
    cvec[:, :, 1] = _col(inp["c_ctx"], 8)
    return {"x": np.ascontiguousarray(inp["x"][b]), "ctx": np.ascontiguousarray(inp["ctx"][b]), "cvec": cvec}


ARF = 22528
PG = 512


class _Stop(Exception):
    pass


BF = True
NWB = 3


def build(layers=(0, 1), dbg=(), stop=None):
    WDT = BF16 if BF else F32
    nc = bass.Bass("TRN2", target_bir_lowering=False)
    P = Prog(nc)

    def din(name, shape, dt=F32):
        return nc.dram_tensor(name, list(shape), dt, kind="ExternalInput").ap()

    def dscr(name, shape, dt, out=False):
        kind = "ExternalOutput" if (out or name in dbg) else "Internal"
        return nc.dram_tensor(name, list(shape), dt, kind=kind).ap()

    x_in = din("x", [SEQ, D]); ctx_in = din("ctx", [CTX, D]); cvec_in = din("cvec", [128, 8, 2])
    wcat = din("wcat", [DEPTH, NPIECE, 128, PSZ]); wada = din("wada", [DEPTH, 6, 128, PSZ])
    bcol_in = din("bcol", [DEPTH, 128, 68]); rows_in = din("rows", [DEPTH, 9, 1024])
    wsT_in = din("wsT", [DEPTH, 128, 512]); lruW_in = din("lruW", [DEPTH, 128, 2048])
    lcol_in = din("lcol", [DEPTH, 128, 24]); ccol_in = din("ccol", [DEPTH, 128, 20])
    adac_in = din("adac", [DEPTH, 128, 16]); btab_in = din("btab", [DEPTH, 128, 14 * 512])
    cosT_in = din("cosT", [128, NTOK]); sinT_in = din("sinT", [128, NTOK]); ident_in = din("ident", [128, 128])

    out_d = dscr("out", [SEQ, D], F32, out=True)
    wbf = dscr("wbf", [DEPTH, NPIECE, 128, PSZ], BF16) if BF else None
    bxT = dscr("bxT", [512, NTOK], F32)
    KT = dscr("KT", [512, NTOK], BF16)
    Vs = dscr("Vs", [NTOK, 512], BF16)
    hfs = dscr("hfs", [512, SEQ], F32)
    ybT = dscr("ybT", [512, NTOK], BF16)
    x1 = dscr("x1", [SEQ, D], F32)
    xc1 = dscr("xc1", [CTX, D], F32)

    def S(name, shape, dt=F32):
        return nc.alloc_sbuf_tensor("sb_" + name, list(shape), dt)

    AR = S("AR", [128, ARF])
    bcol = S("bcol", [128, 68]); adac = S("adac", [128, 16]); lcol = S("lcol", [128, 2, 4, 3]); ccol = S("ccol", [128, 4, 5])
    wsT = S("wsT", [128, 4, 128]); lruW = S("lruW", [128, 2, 4, 2, 128]); ident = S("ident", [128, 128])
    ones_bf = S("ones_bf", [128, 128], BF16)
    gate_bc = S("gate_bc", [128, 2, 1024]); lng_bc = S("lng_bc", [128, 1024]); lnb_bc = S("lnb_bc", [128, 1024])
    bav_bc = S("bav_bc", [128, 512]); bv_bc = S("bv_bc", [128, 512]); sg_bc = S("sg_bc", [128, 512])
    sb_bc = S("sb_bc", [128, 512]); bs_bc = S("bs_bc", [128, 512])
    btab = S("btab", [128, 14, 512], BF16)
    kctx = S("kctx", [128, 4, 256], BF16); vctx = S("vctx", [128, 2, 512], BF16)
    cv = S("cv", [128, 8, 2]); scv = S("scv", [128, 8, 2]); modc = S("modc", [128, 16, 2]); lcc = S("lcc", [128, 2, 4]); lcc2 = S("lcc2", [128, 2, 4])
    ltmp = S("ltmp", [128, 2, 4])
    uT = S("uT", [128, 8, 512], WDT)
    wb = [S("wb%d" % i, [128, PSZ], WDT) for i in range(NWB)]
    yaT = S("yaT", [128, 4, 512], WDT); ybs = S("ybs", [128, 4, 512], WDT); ycs = S("ycs", [128, 4, 512], WDT)
    mT = S("mT", [128, 8, 512], WDT)
    carry = S("carry", [128, 1]); hcf = S("hcf", [128, 256])
    st6 = S("st6", [128, 12]); mv2 = S("mv2", [128, 2]); rs1 = S("rs1", [128, 1])
    psum2 = [nc.alloc_psum_tensor("ps%d" % i, [128, 1024], F32) for i in range(4)]

    def ps():
        i = P.ps_i % 8
        P.ps_i += 1
        return V(psum2[i // 2][:, (i % 2) * 512:(i % 2 + 1) * 512], ["ps%d" % i])

    def psp():
        if P.ps_i % 2:
            P.ps_i += 1
        i = P.ps_i % 8
        P.ps_i += 2
        return V(psum2[i // 2][:, :], ["ps%d" % i, "ps%d" % (i + 1)])

    def av(off_pg, shape, dt=F32, off=0):
        n = int(np.prod(shape[1:]))
        nf = n if dt == F32 else n // 2
        o = off_pg * PG + off
        ap = AR[:, o:o + nf]
        if dt != F32:
            ap = ap.bitcast(BF16)
        if len(shape) == 3:
            ap = ap.rearrange("p (a b) -> p a b", a=shape[1])
        elif len(shape) == 4:
            ap = ap.rearrange("p (a b c) -> p a b c", a=shape[1], b=shape[2])
        v = V(ap, [("pg", i) for i in range(o // PG, (o + nf - 1) // PG + 1)])
        v.o = o
        v.nf = nf
        return v

    def rev_ap(v, n):
        return bass.AP(AR, v.o + n - 1, [[ARF, 128], [-1, n]])

    wslot = [0]

    def wload(l, piece, nelem=PSZ):
        i = wslot[0] % NWB
        wslot[0] += 1
        if BF:
            P.dma("sp", wb[i][:, 0:nelem], wbf[l, piece, :, 0:nelem], r=[("wbf", l, piece)], w=["wb%d" % i])
        else:
            P.dma("sp", wb[i][:, 0:nelem], wcat[l, piece, :, 0:nelem], w=["wb%d" % i])
        return V(wb[i], ["wb%d" % i])

    def fmv(w):
        return w.ap[:, :].rearrange("p (b k c) -> p b k c", b=4, k=8)

    def widev(w, k=8):
        return w.ap[:, 0:k * 512].rearrange("p (k c) -> p k c", k=k)

    def mm_fm(w, b, ntok):
        p = ps()
        wv = fmv(w)

        def f(e):
            for k in range(8):
                ins = e.matmul(p.ap[:, 0:ntok], lhsT=wv[:, b, k, :], rhs=uT[:, k, 0:ntok], start=(k == 0), stop=(k == 7))
            return ins
        P.op("pe", f, r=[w, "uT"], w=[p])
        return p

    def cast_weights(l):
        st32 = [av(0, [128, PSZ]), av(8, [128, PSZ])]
        st16 = [av(16, [128, PSZ], BF16), av(20, [128, PSZ], BF16)]
        for i in range(NPIECE):
            s_ = i % 2
            P.dma("sp", st32[s_].ap, wcat[l, i], w=[st32[s_]])
            P.op("dve", lambda e, s_=s_: e.tensor_copy(out=st16[s_].ap, in_=st32[s_].ap), r=[st32[s_]], w=[st16[s_]])
            P.dma("pool", wbf[l, i], st16[s_].ap, r=[st16[s_]], w=[("wbf", l, i)])

    if BF:
        for l in layers:
            cast_weights(l)
    P.dma("sp", ident[:], ident_in, w=["ident"])
    P.op("pool", lambda e: e.memset(ones_bf[:], 1.0), w=["ones_bf"])
    P.dma("sp", cv[:], cvec_in, w=["cv"])
    P.op("act", lambda e: e.activation(out=scv[:], in_=cv[:], func=AF.Silu), r=["cv"], w=["scv"])

    def load_uT(src, ntok, who):
        xts = [av(24, [128, 1024]), av(26, [128, 1024])]
        for t in range(ntok // 128):
            xv = xts[t % 2]
            P.dma("sp", xv.ap, src[t * 128:(t + 1) * 128, :], r=[("src", id(src.tensor), t)], w=[xv])
            pa, pb = ps(), ps()

            def f(e, xv=xv, pa=pa, pb=pb):
                for k in range(8):
                    pp = pa if k < 4 else pb
                    ins = e.transpose(pp.ap[:, (k % 4) * 128:(k % 4 + 1) * 128], xv.ap[:, k * 128:(k + 1) * 128], ident[:])
                return ins
            P.op("pe", f, r=[xv, "ident"], w=[pa, pb])
            for k in range(8):
                pp = pa if k < 4 else pb
                P.op("act", lambda e, k=k, pp=pp, t=t: e.activation(
                    out=uT[:, k, t * 128:(t + 1) * 128], in_=pp.ap[:, (k % 4) * 128:(k % 4 + 1) * 128],
                    func=AF.Identity, bias=modc[:, k, who:who + 1], scale=modc[:, 8 + k, who:who + 1]),
                    r=[pp, "modc"], w=["uT"])

    def rope_proj(l, pc, pcp, ntok, cosv, sinv, dst_ap, dstkeys, t1s, t2s):
        w1 = wload(l, pc)
        w2 = wload(l, pcp)
        for b in range(4):
            p1 = mm_fm(w1, b, ntok)
            p2 = mm_fm(w2, b, ntok)
            t1, t2 = t1s[b % len(t1s)], t2s[b % len(t2s)]
            P.op("dve", lambda e, p1=p1, t1=t1, b=b: e.scalar_tensor_tensor(
                out=t1.ap[:, 0:ntok], in0=p1.ap[:, 0:ntok], scalar=bcol[:, pc * 4 + b:pc * 4 + b + 1], in1=cosv.ap[:, 0:ntok],
                op0=ALU.add, op1=ALU.mult), r=[p1, cosv, "bcol"], w=[t1])
            P.op("dve", lambda e, p2=p2, t2=t2, b=b: e.scalar_tensor_tensor(
                out=t2.ap[:, 0:ntok], in0=p2.ap[:, 0:ntok], scalar=bcol[:, pcp * 4 + b:pcp * 4 + b + 1], in1=sinv.ap[:, 0:ntok],
                op0=ALU.add, op1=ALU.mult), r=[p2, sinv, "bcol"], w=[t2])
            P.op("pool", lambda e, t1=t1, t2=t2, b=b: e.tensor_tensor(
                out=dst_ap[:, b, 0:ntok], in0=t1.ap[:, 0:ntok], in1=t2.ap[:, 0:ntok], op=ALU.add),
                r=[t1, t2], w=dstkeys)

    def layer_setup(l):
        P.dma("sp", bcol[:], bcol_in[l], w=["bcol"])
        P.dma("sp", adac[:], adac_in[l], w=["adac"])
        P.dma("sp", lcol[:].rearrange("p a b c -> p (a b c)"), lcol_in[l], w=["lcol"])
        P.dma("sp", ccol[:].rearrange("p a b -> p (a b)"), ccol_in[l], w=["ccol"])
        P.dma("sp", wsT[:].rearrange("p a b -> p (a b)"), wsT_in[l], w=["wsT"])
        P.dma("sp", lruW[:].rearrange("p a b c d -> p (a b c d)"), lruW_in[l], w=["lruW"])
        for hf_ in range(2):
            bt32 = av(hf_ * 7, [128, 7 * 512])
            P.dma("sp", bt32.ap, btab_in[l][:, hf_ * 3584:(hf_ + 1) * 3584], w=[bt32])
            P.op("pool", lambda e, hf_=hf_, bt32=bt32: e.tensor_copy(out=btab[:, hf_ * 7:(hf_ + 1) * 7, :].rearrange("p a b -> p (a b)"), in_=bt32.ap),
                 r=[bt32], w=["btab"])
        P.dma("sp", lng_bc[:], rows_in[l, 1, :].partition_broadcast(128), w=["lng_bc"])
        P.dma("sp", lnb_bc[:], rows_in[l, 2, :].partition_broadcast(128), w=["lnb_bc"])
        for i, (t, nm) in enumerate([(bav_bc, "bav_bc"), (bv_bc, "bv_bc"), (sg_bc, "sg_bc"), (sb_bc, "sb_bc"), (bs_bc, "bs_bc")]):
            P.dma("sp", t[:], rows_in[l, 3 + i, 0:512].partition_broadcast(128), w=[nm])
        badag = av(24, [128, 1024])
        P.dma("sp", badag.ap, rows_in[l, 0, :].partition_broadcast(128), w=[badag])
        P.op("act", lambda e: e.activation(out=ltmp[:], in_=lcol[:, :, :, 2], func=AF.Exp, scale=-1.0), r=["lcol"], w=["ltmp"])
        P.op("act", lambda e: e.activation(out=ltmp[:], in_=ltmp[:], func=AF.Ln, bias=1.0), r=["ltmp"], w=["ltmp"])
        P.op("dve", lambda e: e.tensor_scalar(out=lcc[:], in0=ltmp[:], scalar1=-8.0, scalar2=None, op0=ALU.mult), r=["ltmp"], w=["lcc"])
        P.op("dve", lambda e: e.tensor_scalar(out=lcc2[:], in0=ltmp[:], scalar1=-16.0, scalar2=None, op0=ALU.mult), r=["ltmp"], w=["lcc2"])
        screp = av(20, [128, 16, 128])
        P.op("dve", lambda e: e.tensor_copy(out=screp.ap, in_=scv[:].rearrange("p a b -> p (a b)").unsqueeze(2).broadcast_to([128, 16, 128])),
             r=["scv"], w=[screp])
        for i in range(4):
            wv = av((i % 2) * 8, [128, 4, 8, 128])
            P.dma("sp", wv.ap.rearrange("p a b c -> p (a b c)"), wada[l, i], w=[wv])
            for b in range(4):
                blk = i * 4 + b
                p = ps()

                def f(e, wv=wv, b=b, p=p):
                    for k in range(8):
                        ins = e.matmul(p.ap[:, 0:2], lhsT=wv.ap[:, b, k, :], rhs=scv[:, k, :], start=(k == 0), stop=(k == 7))
                    return ins
                P.op("pe", f, r=[wv, "scv"], w=[p])
                P.op("dve", lambda e, p=p, blk=blk: e.tensor_scalar(
                    out=modc[:, blk, :], in0=p.ap[:, 0:2], scalar1=adac[:, blk:blk + 1], scalar2=(1.0 if blk >= 8 else 0.0),
                    op0=ALU.add, op1=ALU.add), r=[p, "adac"], w=["modc"])
        for i in range(2):
            wv = av((i % 2) * 8, [128, 8, 512])
            P.dma("sp", wv.ap.rearrange("p a b -> p (a b)"), wada[l, 4 + i], w=[wv])
            for who in range(2):
                p = ps()

                def f(e, wv=wv, who=who, p=p):
                    for k in range(8):
                        ins = e.matmul(p.ap[:, 0:512], lhsT=screp.ap[:, k * 2 + who, :], rhs=wv.ap[:, k, :], start=(k == 0), stop=(k == 7))
                    return ins
                P.op("pe", f, r=[wv, screp], w=[p])
                P.op("dve", lambda e, p=p, who=who, i=i: e.tensor_tensor(
                    out=gate_bc[:, who, i * 512:(i + 1) * 512], in0=p.ap[:, 0:512], in1=badag.ap[:, i * 512:(i + 1) * 512], op=ALU.add),
                    r=[p, badag], w=["gate_bc"])

    def pass1_group(l, src, tok0, ntok, who, g):
        nt = ntok // 128
        load_uT(src, ntok, who)
        if stop == "p1a":
            raise _Stop()
        cosv, sinv = av(0, [128, 512]), av(1, [128, 512])
        P.dma("sp", cosv.ap[:, 0:ntok], cosT_in[:, tok0:tok0 + ntok], w=[cosv])
        P.dma("sp", sinv.ap[:, 0:ntok], sinT_in[:, tok0:tok0 + ntok], w=[sinv])
        bxs = av(2, [128, 4, 512])
        w = wload(l, P_BX)
        for b in range(4):
            p = mm_fm(w, b, ntok)
            P.op("act", lambda e, p=p, b=b: e.activation(out=bxs.ap[:, b, 0:ntok], in_=p.ap[:, 0:ntok], func=AF.Identity,
                                                          bias=bcol[:, P_BX * 4 + b:P_BX * 4 + b + 1], scale=1.0),
                 r=[p, "bcol"], w=[bxs])
        if stop == "p1a2":
            raise _Stop()
        for b in range(4):
            P.dma("pool", bxT[b * 128:(b + 1) * 128, tok0:tok0 + ntok], bxs.ap[:, b, 0:ntok], r=[bxs], w=[("bxT", g)])
        if stop == "p1b":
            raise _Stop()
        kTs = av(6, [128, 4, 512], BF16)
        t1s = [av(8, [128, 512]), av(9, [128, 512])]
        t2s = [av(10, [128, 512]), av(11, [128, 512])]
        rope_proj(l, P_K, P_KP, ntok, cosv, sinv, kTs.ap, [kTs], t1s, t2s)
        for b in range(4):
            P.dma("pool", KT[b * 128:(b + 1) * 128, tok0:tok0 + ntok], kTs.ap[:, b, 0:ntok], r=[kTs], w=[("KT", g)])
        if stop == "p1c":
            raise _Stop()
        vts = av(12, [128, 4, 512], BF16)
        w = wload(l, P_V)
        wv = widev(w)
        for t in range(nt):
            p = ps()

            def f(e, t=t, p=p):
                for k in range(8):
                    ins = e.matmul(p.ap[:, 0:512], lhsT=uT[:, k, t * 128:(t + 1) * 128], rhs=wv[:, k, :], start=(k == 0), stop=(k == 7))
                return ins
            P.op("pe", f, r=[w, "uT"], w=[p])
            P.op("dve", lambda e, t=t, p=p: e.tensor_tensor(out=vts.ap[:, t, :], in0=p.ap[:, 0:512], in1=bv_bc[:], op=ALU.add),
                 r=[p, "bv_bc"], w=[vts])
        for t in range(nt):
            P.dma("pool", Vs[tok0 + t * 128:tok0 + (t + 1) * 128, :], vts.ap[:, t, :], r=[vts], w=[("V", g)])
        if stop == "p1d":
            raise _Stop()

    def lru_pass(l, with_ctx):
        NS = 1024
        sets = []
        for si in range(2):
            o = si * 18
            sets.append(dict(bxh=av(o, [128, 1536]), xc=av(o + 3, [128, NS]), rr=av(o + 5, [128, NS]), ii=av(o + 7, [128, NS]),
                             aa=av(o + 9, [128, NS]), bq=av(o + 11, [128, NS]), hh=av(o + 13, [128, NS]), hfl=av(o + 15, [128, NS]),
                             yo=av(o + 17, [128, NS], BF16)))
        allbx = [("bxT", g) for g in range(17)]

        def front(sg):
            B_ = sets[sg["i"] % 2]
            ct, d, colbase, seqlen, s0, n, is_ctx = sg["ct"], sg["d"], sg["colbase"], sg["seqlen"], sg["s0"], sg["n"], sg["is_ctx"]
            b_, xc, rr, ii, aa, bq = B_["bxh"], B_["xc"], B_["rr"], B_["ii"], B_["aa"], B_["bq"]
            lo, hi = max(s0 - 2, 0), min(s0 + n + 1, seqlen)
            if lo > s0 - 2 or hi < s0 + n + 1:
                P.op("pool", lambda e: e.memset(b_.ap[:, 0:n + 3], 0.0), w=[b_])
            P.dma("sp", b_.ap[:, lo - (s0 - 2):hi - (s0 - 2)], bxT[ct * 128:(ct + 1) * 128, colbase + lo:colbase + hi],
                  r=allbx, w=[b_])
            if d == 1 and not is_ctx:
                P.dma("sp", B_["hfl"].ap[:, 0:n], hfs[ct * 128:(ct + 1) * 128, s0:s0 + n], r=[("hfs", ct, s0)], w=[B_["hfl"]])
            P.op("dve", lambda e: e.tensor_scalar(out=xc.ap[:, 0:n], in0=b_.ap[:, 0:n], scalar1=ccol[:, ct, 0:1], scalar2=ccol[:, ct, 4:5],
                                                   op0=ALU.mult, op1=ALU.add), r=[b_, "ccol"], w=[xc])
            for j in range(1, 4):
                P.op("dve", lambda e, j=j: e.scalar_tensor_tensor(out=xc.ap[:, 0:n], in0=b_.ap[:, j:j + n], scalar=ccol[:, ct, j:j + 1],
                                                                   in1=xc.ap[:, 0:n], op0=ALU.mult, op1=ALU.add), r=[b_, xc, "ccol"], w=[xc])
            for c in range((n + 511) // 512):
                cn = min(512, n - c * 512)
                for wi, dst in ((0, rr), (1, ii)):
                    p = ps()
                    P.op("pe", lambda e, p=p, c=c, cn=cn, wi=wi: e.matmul(p.ap[:, 0:cn], lhsT=lruW[:, d, ct, wi, :], rhs=xc.ap[:, c * 512:c * 512 + cn],
                                                                           start=True, stop=True), r=[xc, "lruW"], w=[p])
                    P.op("act", lambda e, p=p, c=c, cn=cn, wi=wi, dst=dst: e.activation(
                        out=dst.ap[:, c * 512:c * 512 + cn], in_=p.ap[:, 0:cn], func=AF.Sigmoid, bias=lcol[:, d, ct, wi:wi + 1], scale=1.0),
                        r=[p, "lcol"], w=[dst])
            P.op("act", lambda e: e.activation(out=aa.ap[:, 0:n], in_=rr.ap[:, 0:n], func=AF.Exp, scale=lcc[:, d, ct:ct + 1]), r=[rr, "lcc"], w=[aa])
            P.op("act", lambda e: e.activation(out=bq.ap[:, 0:n], in_=rr.ap[:, 0:n], func=AF.Exp, scale=lcc2[:, d, ct:ct + 1]), r=[rr, "lcc2"], w=[bq])
            P.op("act", lambda e: e.activation(out=bq.ap[:, 0:n], in_=bq.ap[:, 0:n], func=AF.Sqrt, bias=1.0, scale=-1.0), r=[bq], w=[bq])
            P.op("pool", lambda e: e.tensor_tensor(out=ii.ap[:, 0:n], in0=ii.ap[:, 0:n], in1=xc.ap[:, 0:n], op=ALU.mult), r=[ii, xc], w=[ii])

        def back(sg):
            B_ = sets[sg["i"] % 2]
            ct, d, s0, n, is_ctx, first = sg["ct"], sg["d"], sg["s0"], sg["n"], sg["is_ctx"], sg["first"]
            ii, aa, bq, hh, yo, hf_ = B_["ii"], B_["aa"], B_["bq"], B_["hh"], B_["yo"], B_["hfl"]
            P.op("dve", lambda e: e.tensor_tensor(out=bq.ap[:, 0:n], in0=bq.ap[:, 0:n], in1=ii.ap[:, 0:n], op=ALU.mult), r=[bq, ii], w=[bq])
            init = 0.0 if first else carry[:, 0:1]
            if d == 0:
                P.op("dve", lambda e: e.tensor_tensor_scan(out=hh.ap[:, 0:n], data0=aa.ap[:, 0:n], data1=bq.ap[:, 0:n], initial=init,
                                                            op0=ALU.mult, op1=ALU.add), r=[aa, bq, "carry"], w=[hh])
                P.op("dve", lambda e: e.tensor_copy(out=carry[:], in_=hh.ap[:, n - 1:n]), r=[hh], w=["carry"])
            else:
                P.op("dve", lambda e: e.tensor_tensor_scan(out=rev_ap(hh, n), data0=rev_ap(aa, n), data1=rev_ap(bq, n), initial=init,
                                                            op0=ALU.mult, op1=ALU.add), r=[aa, bq, "carry"], w=[hh])
                P.op("dve", lambda e: e.tensor_copy(out=carry[:], in_=hh.ap[:, 0:1]), r=[hh], w=["carry"])
            rows = slice(ct * 128, (ct + 1) * 128)
            if d == 0:
                if is_ctx:
                    P.op("pool", lambda e: e.tensor_copy(out=hcf[:], in_=hh.ap[:, 0:n]), r=[hh], w=["hcf"])
                else:
                    P.dma("pool", hfs[rows, s0:s0 + n], hh.ap[:, 0:n], r=[hh], w=[("hfs", ct, s0)])
            else:
                if is_ctx:
                    if with_ctx:
                        P.op("pool", lambda e: e.tensor_tensor(out=yo.ap[:, 0:n], in0=hh.ap[:, 0:n], in1=hcf[:], op=ALU.add), r=[hh, "hcf"], w=[yo])
                        P.dma("pool", ybT[rows, SEQ:SEQ + n], yo.ap[:, 0:n], r=[yo], w=[("ybT", ct, "c")])
                else:
                    P.op("pool", lambda e: e.tensor_tensor(out=yo.ap[:, 0:n], in0=hh.ap[:, 0:n], in1=hf_.ap[:, 0:n], op=ALU.add), r=[hh, hf_], w=[yo])
                    P.dma("pool", ybT[rows, s0:s0 + n], yo.ap[:, 0:n], r=[yo], w=[("ybT", ct, s0)])

        segs = []
        for ct in range(4):
            for d in range(2):
                segs.append(dict(ct=ct, d=d, colbase=SEQ, seqlen=CTX, s0=0, n=CTX, first=True, is_ctx=True))
                order = range(0, SEQ, NS) if d == 0 else range(SEQ - NS, -1, -NS)
                for s0 in order:
                    segs.append(dict(ct=ct, d=d, colbase=0, seqlen=SEQ, s0=s0, n=NS, first=False, is_ctx=False))
        for i, sg in enumerate(segs):
            sg["i"] = i
        front(segs[0])
        for i in range(len(segs)):
            if i + 1 < len(segs):
                front(segs[i + 1])
            back(segs[i])

    def attention(l, g, tok0, ntok, is_ctx, qT, ycT):
        nrow = ntok // 64
        kwin = av(4, [128, 4, 1024], BF16)
        vwin = [av(8, [128, 4, 512], BF16), av(10, [128, 4, 512], BF16)]
        pTs = [av(12, [128, 6, 512], BF16), av(25, [128, 6, 512], BF16)]
        sps = [av(15, [128, 512]), av(16, [128, 512])]
        rz = av(17, [128, 512])
        if not is_ctx:
            t_lo, t_hi = max(0, tok0 - 256), min(SEQ, tok0 + 512 + 256)
            gs = [gg for gg in (g - 1, g, g + 1) if 0 <= gg < 16]
            for b in range(4):
                P.dma("sp", kwin.ap[:, b, 0:t_hi - t_lo], KT[b * 128:(b + 1) * 128, t_lo:t_hi],
                      r=[("KT", gg) for gg in gs], w=[kwin])
        state = {}

        def part1(rq):
            pT = pTs[rq % 2]
            koff = jv = 0
            vw = None
            tiles = []
            if not is_ctx:
                r = (tok0 // 64) + rq
                r0 = min(max(r - 4, 0), 120)
                jv = r0 - r + 7
                koff = r0 * 64 - t_lo
                vw = vwin[rq % 2]
                for jj_ in range(4):
                    P.dma("sp", vw.ap[:, jj_, :], Vs[r0 * 64 + jj_ * 128:r0 * 64 + (jj_ + 1) * 128, :],
                          r=[("V", gg) for gg in gs], w=[vw])
                tiles += [("loc", j) for j in range(4)]
            tiles += [("ctx", 0), ("ctx", 1)]
            ntile = len(tiles)
            for ti, (kind, j) in enumerate(tiles):
                p = psp()
                p3 = p.ap.rearrange("p (e y) -> p e y", e=2)[:, :, 0:256]

                def f(e, kind=kind, j=j, p=p, rq=rq, koff=(koff if not is_ctx else 0)):
                    for h in range(8):
                        c, ee = h // 2, h % 2
                        if kind == "loc":
                            lt = kwin.ap[64 * ee:64 * ee + 64, c, koff + j * 128:koff + (j + 1) * 128]
                        else:
                            lt = kctx[64 * ee:64 * ee + 64, c, j * 128:(j + 1) * 128]
                        ins = e.matmul(p.ap[:, ee * 512 + c * 64:ee * 512 + (c + 1) * 64], lhsT=lt,
                                       rhs=qT.ap[64 * ee:64 * ee + 64, c, rq * 64:(rq + 1) * 64], start=True, stop=True)
                    return ins
                P.op("pe", f, r=[kwin if kind == "loc" else "kctx", qT], w=[p])
                if kind == "loc":
                    sp_ = sps[ti % 2]
                    P.op("dve", lambda e, p3=p3, sp_=sp_, j=j, jv=jv: e.scalar_tensor_tensor(
                        out=sp_.ap.rearrange("p (e x) -> p e x", e=2), in0=p3, scalar=0.125,
                        in1=btab[:, 2 * j + jv, :].rearrange("p (e x) -> p e x", e=2), op0=ALU.mult, op1=ALU.add),
                        r=[p, "btab"], w=[sp_])
                    P.op("act", lambda e, sp_=sp_, ti=ti, pT=pT: e.activation(out=pT.ap[:, ti, :], in_=sp_.ap, func=AF.Exp),
                         r=[sp_], w=[pT])
                else:
                    P.op("act", lambda e, p3=p3, ti=ti, pT=pT: e.activation(out=pT.ap[:, ti, :].rearrange("p (e x) -> p e x", e=2), in_=p3,
                                                                              func=AF.Exp, scale=0.125), r=[p], w=[pT])
            state[rq] = (pT, tiles, vw)

        def part2(rq):
            pT, tiles, vw = state.pop(rq)
            ntile = len(tiles)
            po, pz = ps(), ps()

            def fo(e, tiles=tiles, po=po, pT=pT, vw=(vw if not is_ctx else None)):
                for h in range(8):
                    c, ee = h // 2, h % 2
                    for ti, (kind, j) in enumerate(tiles):
                        lt = vw.ap[:, j, h * 64:(h + 1) * 64] if kind == "loc" else vctx[:, j, h * 64:(h + 1) * 64]
                        ins = e.matmul(po.ap[64 * ee:64 * ee + 64, c * 64:(c + 1) * 64], lhsT=lt, rhs=pT.ap[:, ti, ee * 256 + c * 64:ee * 256 + (c + 1) * 64],
                                       start=(ti == 0), stop=(ti == len(tiles) - 1), tile_position=(0, 64 * ee))
                return ins
            P.op("pe", fo, r=[pT, "vctx"] + ([vw] if not is_ctx else []), w=[po])

            def fz(e, pz=pz, pT=pT, ntile=ntile):
                for ti in range(ntile):
                    ins = e.matmul(pz.ap[:, 0:512], lhsT=ones_bf[:], rhs=pT.ap[:, ti, :], start=(ti == 0), stop=(ti == ntile - 1))
                return ins
            P.op("pe", fz, r=[pT, "ones_bf"], w=[pz])

            P.op("dve", lambda e, pz=pz: e.reciprocal(out=rz.ap, in_=pz.ap[:, 0:512]), r=[pz], w=[rz])
            for ee in range(2):
                P.op("dve", lambda e, ee=ee, po=po, rq=rq: e.tensor_tensor(
                    out=ycT.ap[64 * ee:64 * ee + 64, :, rq * 64:(rq + 1) * 64],
                    in0=po.ap[64 * ee:64 * ee + 64, 0:256].rearrange("p (c q) -> p c q", c=4),
                    in1=rz.ap[64 * ee:64 * ee + 64, ee * 256:(ee + 1) * 256].rearrange("p (c q) -> p c q", c=4),
                    op=ALU.mult), r=[po, rz], w=[ycT])

        part1(0)
        for rq in range(nrow):
            if rq + 1 < nrow:
                part1(rq + 1)
            part2(rq)

    def pass2_group(l, src, dst, dstkey, tok0, ntok, who, is_ctx, g):
        nt = ntok // 128
        load_uT(src, ntok, who)
        gu = av(0, [128, 4, 512]); sgA = av(4, [128, 4, 512]); vn = av(8, [128, 4, 512])
        vpre = av(12, [128, 512]); gv = av(13, [128, 512]); ta = av(14, [128, 512])
        for pc, dstv, fn in ((P_AU, gu, AF.Gelu), (P_AG, sgA, AF.Silu)):
            w = wload(l, pc)
            for b in range(4):
                p = mm_fm(w, b, ntok)
                P.op("act", lambda e, p=p, b=b, dstv=dstv, fn=fn, pc=pc: e.activation(
                    out=dstv.ap[:, b, 0:ntok], in_=p.ap[:, 0:ntok], func=fn, bias=bcol[:, pc * 4 + b:pc * 4 + b + 1], scale=1.0),
                    r=[p, "bcol"], w=[dstv])
        w = wload(l, P_AV)
        wv = widev(w)
        for t in range(nt):
            p = ps()

            def f(e, t=t, p=p, wv=wv):
                for k in range(8):
                    ins = e.matmul(p.ap[:, 0:512], lhsT=uT[:, k, t * 128:(t + 1) * 128], rhs=wv[:, k, :], start=(k == 0), stop=(k == 7))
                return ins
            P.op("pe", f, r=[w, "uT"], w=[p])
            P.op("dve", lambda e, p=p: e.tensor_tensor(out=vpre.ap, in0=p.ap[:, 0:512], in1=bav_bc[:], op=ALU.add), r=[p, "bav_bc"], w=[vpre])
            P.op("act", lambda e: e.activation(out=gv.ap, in_=vpre.ap, func=AF.Gelu), r=[vpre], w=[gv])
            P.op("dve", lambda e: e.bn_stats(out=st6[:, 0:6], in_=gv.ap), r=[gv], w=["st6"])
            P.op("dve", lambda e: e.bn_aggr(out=mv2[:], in_=st6[:, 0:6]), r=["st6"], w=["mv2"])
            P.op("dve", lambda e: e.tensor_scalar(out=rs1[:], in0=mv2[:, 1:2], scalar1=LN_EPS, scalar2=None, op0=ALU.add), r=["mv2"], w=["rs1"])
            P.op("act", lambda e: e.activation(out=rs1[:], in_=rs1[:], func=AF.Sqrt), r=["rs1"], w=["rs1"])
            P.op("dve", lambda e: e.reciprocal(out=rs1[:], in_=rs1[:]), r=["rs1"], w=["rs1"])
            P.op("dve", lambda e, t=t: e.tensor_scalar(out=vn.ap[:, t, :], in0=gv.ap, scalar1=mv2[:, 0:1], scalar2=rs1[:, 0:1],
                                                       op0=ALU.subtract, op1=ALU.mult), r=[gv, "mv2", "rs1"], w=[vn])
            P.op("pool", lambda e, t=t: e.tensor_tensor(out=vn.ap[:, t, :], in0=vn.ap[:, t, :], in1=sg_bc[:], op=ALU.mult), r=[vn, "sg_bc"], w=[vn])
            P.op("pool", lambda e, t=t: e.tensor_tensor(out=vn.ap[:, t, :], in0=vn.ap[:, t, :], in1=sb_bc[:], op=ALU.add), r=[vn, "sb_bc"], w=[vn])
        for g4 in range(4):
            p = ps()

            def f(e, g4=g4, p=p):
                for t in range(nt):
                    ins = e.matmul(p.ap[:, t * 128:(t + 1) * 128], lhsT=vn.ap[:, t, g4 * 128:(g4 + 1) * 128], rhs=wsT[:, g4, :], start=True, stop=True)
                return ins
            P.op("pe", f, r=[vn, "wsT"], w=[p])
            P.op("dve", lambda e, g4=g4, p=p: e.tensor_tensor(
                out=ta.ap[:, 0:ntok].rearrange("p (t q) -> p t q", t=nt), in0=p.ap[:, 0:ntok].rearrange("p (t q) -> p t q", t=nt),
                in1=bs_bc[:, g4 * 128:(g4 + 1) * 128].unsqueeze(1).broadcast_to([128, nt, 128]), op=ALU.add), r=[p, "bs_bc"], w=[ta])
            P.op("dve", lambda e, g4=g4: e.tensor_tensor(out=ta.ap[:, 0:ntok], in0=ta.ap[:, 0:ntok], in1=gu.ap[:, g4, 0:ntok], op=ALU.mult),
                 r=[ta, gu], w=[ta])
            P.op("pool", lambda e, g4=g4: e.tensor_tensor(out=yaT[:, g4, 0:ntok], in0=ta.ap[:, 0:ntok], in1=sgA.ap[:, g4, 0:ntok], op=ALU.mult),
                 r=[ta, sgA], w=["yaT"])
        if stop == "p2a":
            raise _Stop()
        ybl = av(16, [128, 4, 512], BF16)
        tmpb = av(12, [128, 512])
        ycol = SEQ if is_ctx else tok0
        for b in range(4):
            P.dma("sp", ybl.ap[:, b, 0:ntok], ybT[b * 128:(b + 1) * 128, ycol:ycol + ntok],
                  r=[("ybT", b, "c") if is_ctx else ("ybT", b, (tok0 // 1024) * 1024)], w=[ybl])
        w = wload(l, P_BG)
        for b in range(4):
            p = mm_fm(w, b, ntok)
            P.op("act", lambda e, p=p, b=b: e.activation(out=tmpb.ap[:, 0:ntok], in_=p.ap[:, 0:ntok], func=AF.Silu,
                                                          bias=bcol[:, P_BG * 4 + b:P_BG * 4 + b + 1], scale=1.0), r=[p, "bcol"], w=[tmpb])
            P.op("dve", lambda e, b=b: e.tensor_tensor(out=ybs[:, b, 0:ntok], in0=tmpb.ap[:, 0:ntok], in1=ybl.ap[:, b, 0:ntok], op=ALU.mult),
                 r=[tmpb, ybl], w=["ybs"])
        if stop == "p2b":
            raise _Stop()
        cosv, sinv = av(0, [128, 512]), av(1, [128, 512])
        P.dma("sp", cosv.ap[:, 0:ntok], cosT_in[:, tok0:tok0 + ntok], w=[cosv])
        P.dma("sp", sinv.ap[:, 0:ntok], sinT_in[:, tok0:tok0 + ntok], w=[sinv])
        qT = av(23, [128, 4, 512], BF16)
        ycT = av(18, [128, 4, 512])
        rope_proj(l, P_Q, P_QP, ntok, cosv, sinv, qT.ap, [qT], [av(2, [128, 512])], [av(3, [128, 512])])
        if stop == "p2q":
            raise _Stop()
        attention(l, g, tok0, ntok, is_ctx, qT, ycT)
        if stop == "p2c":
            raise _Stop()
        if "dbgy" in dbg and l == 0 and (is_ctx or g == 0):
            nm = "c" if is_ctx else "l"
            dya = nc.dram_tensor("dbg_ya_" + nm, [512, ntok], F32, kind="ExternalOutput").ap()
            dyc = nc.dram_tensor("dbg_yc_" + nm, [512, ntok], F32, kind="ExternalOutput").ap()
            dyb = nc.dram_tensor("dbg_yb_" + nm, [512, ntok], F32, kind="ExternalOutput").ap()
            for b in range(4):
                P.dma("sp", dya[b * 128:(b + 1) * 128, :], yaT[:, b, 0:ntok], r=["yaT"], w=[("dbg", nm, 0, b)])
                P.dma("sp", dyc[b * 128:(b + 1) * 128, :], ycT.ap[:, b, 0:ntok], r=[ycT], w=[("dbg", nm, 1, b)])
                P.dma("sp", dyb[b * 128:(b + 1) * 128, :], ybs[:, b, 0:ntok], r=["ybs"], w=[("dbg", nm, 2, b)])
        tmpc = av(22, [128, 512])
        w = wload(l, P_CG)
        for b in range(4):
            p = mm_fm(w, b, ntok)
            P.op("act", lambda e, p=p, b=b: e.activation(out=tmpc.ap[:, 0:ntok], in_=p.ap[:, 0:ntok], func=AF.Silu,
                                                          bias=bcol[:, P_CG * 4 + b:P_CG * 4 + b + 1], scale=1.0), r=[p, "bcol"], w=[tmpc])
            P.op("dve", lambda e, b=b: e.tensor_tensor(out=ycs[:, b, 0:ntok], in0=tmpc.ap[:, 0:ntok], in1=ycT.ap[:, b, 0:ntok], op=ALU.mult),
                 r=[tmpc, ycT], w=["ycs"])
        sgs = [av(0, [128, 512]), av(1, [128, 512])]
        tms = [av(2, [128, 512]), av(3, [128, 512])]
        macc = av(4, [128, 4, 512])
        ys = [(yaT, "yaT"), (ybs, "ybs"), (ycs, "ycs")]
        cnt = 0
        for hh_ in range(2):
            for n in range(3):
                pcg = P_GM + hh_ * 3 + n
                wg = wload(l, pcg)
                wbr = wload(l, P_WBR + hh_ * 3 + n, 2048)
                wbv = widev(wbr, 4)
                yt, yk = ys[n]
                for jj in range(4):
                    j = hh_ * 4 + jj
                    pg = mm_fm(wg, jj, ntok)
                    pb = ps()

                    def f(e, pb=pb, jj=jj, yt=yt, wbv=wbv):
                        for k in range(4):
                            ins = e.matmul(pb.ap[:, 0:ntok], lhsT=wbv[:, k, jj * 128:(jj + 1) * 128], rhs=yt[:, k, 0:ntok], start=(k == 0), stop=(k == 3))
                        return ins
                    P.op("pe", f, r=[wbr, yk], w=[pb])
                    sg = sgs[cnt % 2]
                    tm = tms[cnt % 2]
                    cnt += 1
                    P.op("act", lambda e, pg=pg, sg=sg, jj=jj, pcg=pcg: e.activation(
                        out=sg.ap[:, 0:ntok], in_=pg.ap[:, 0:ntok], func=AF.Sigmoid, bias=bcol[:, pcg * 4 + jj:pcg * 4 + jj + 1], scale=1.0),
                        r=[pg, "bcol"], w=[sg])
                    if n == 0:
                        P.op("dve", lambda e, sg=sg, pb=pb, jj=jj: e.tensor_tensor(out=macc.ap[:, jj, 0:ntok], in0=sg.ap[:, 0:ntok], in1=pb.ap[:, 0:ntok], op=ALU.mult),
                             r=[sg, pb], w=[macc])
                    else:
                        P.op("dve", lambda e, sg=sg, pb=pb, tm=tm: e.tensor_tensor(out=tm.ap[:, 0:ntok], in0=sg.ap[:, 0:ntok], in1=pb.ap[:, 0:ntok], op=ALU.mult),
                             r=[sg, pb], w=[tm])
                        if n == 1:
                            P.op("pool", lambda e, tm=tm, jj=jj: e.tensor_tensor(out=macc.ap[:, jj, 0:ntok], in0=macc.ap[:, jj, 0:ntok], in1=tm.ap[:, 0:ntok], op=ALU.add),
                                 r=[tm, macc], w=[macc])
                        else:
                            P.op("pool", lambda e, tm=tm, jj=jj, j=j: e.tensor_tensor(out=mT[:, j, 0:ntok], in0=macc.ap[:, jj, 0:ntok], in1=tm.ap[:, 0:ntok], op=ALU.add),
                                 r=[tm, macc], w=["mT"])
        if stop == "p2d":
            raise _Stop()
        xres = av(8, [128, 4, 1024])
        tmo = [av(16, [128, 512]), av(17, [128, 512])]
        xn = [av(18, [128, 1024]), av(20, [128, 1024])]
        for t in range(nt):
            P.dma("sp", xres.ap[:, t, :], src[t * 128:(t + 1) * 128, :], r=[("src", id(src.tensor), t)], w=[xres])
        cnt = 0
        for half in range(2):
            w = wload(l, P_WOUT + half)
            wv = widev(w)
            for t in range(nt):
                p = ps()

                def f(e, t=t, p=p, wv=wv):
                    for k in range(8):
                        ins = e.matmul(p.ap[:, 0:512], lhsT=mT[:, k, t * 128:(t + 1) * 128], rhs=wv[:, k, :], start=(k == 0), stop=(k == 7))
                    return ins
                P.op("pe", f, r=[w, "mT"], w=[p])
                tm = tmo[cnt % 2]
                cnt += 1
                P.op("dve", lambda e, p=p, tm=tm, half=half: e.tensor_tensor(out=tm.ap, in0=p.ap[:, 0:512], in1=gate_bc[:, who, half * 512:(half + 1) * 512], op=ALU.mult),
                     r=[p, "gate_bc"], w=[tm])
                P.op("dve", lambda e, tm=tm, t=t, half=half: e.scalar_tensor_tensor(
                    out=xres.ap[:, t, half * 512:(half + 1) * 512], in0=xres.ap[:, t, half * 512:(half + 1) * 512], scalar=float(ALPHA), in1=tm.ap,
                    op0=ALU.mult, op1=ALU.add), r=[tm, xres], w=[xres])
        for t in range(nt):
            xo = xn[t % 2]
            P.op("dve", lambda e, t=t: e.bn_stats(out=st6[:, 0:6], in_=xres.ap[:, t, 0:512]), r=[xres], w=["st6"])
            P.op("dve", lambda e, t=t: e.bn_stats(out=st6[:, 6:12], in_=xres.ap[:, t, 512:1024]), r=[xres, "st6"], w=["st6"])
            P.op("dve", lambda e: e.bn_aggr(out=mv2[:], in_=st6[:, 0:12]), r=["st6"], w=["mv2"])
            P.op("dve", lambda e: e.tensor_scalar(out=rs1[:], in0=mv2[:, 1:2], scalar1=LN_EPS, scalar2=None, op0=ALU.add), r=["mv2"], w=["rs1"])
            P.op("act", lambda e: e.activation(out=rs1[:], in_=rs1[:], func=AF.Sqrt), r=["rs1"], w=["rs1"])
            P.op("dve", lambda e: e.reciprocal(out=rs1[:], in_=rs1[:]), r=["rs1"], w=["rs1"])
            P.op("dve", lambda e, t=t, xo=xo: e.tensor_scalar(out=xo.ap, in0=xres.ap[:, t, :], scalar1=mv2[:, 0:1], scalar2=rs1[:, 0:1],
                                                              op0=ALU.subtract, op1=ALU.mult), r=[xres, "mv2", "rs1"], w=[xo])
            P.op("pool", lambda e, xo=xo: e.tensor_tensor(out=xo.ap, in0=xo.ap, in1=lng_bc[:], op=ALU.mult), r=[xo, "lng_bc"], w=[xo])
            P.op("pool", lambda e, xo=xo: e.tensor_tensor(out=xo.ap, in0=xo.ap, in1=lnb_bc[:], op=ALU.add), r=[xo, "lnb_bc"], w=[xo])
            P.dma("pool", dst[t * 128:(t + 1) * 128, :], xo.ap, r=[xo], w=[(dstkey, id(dst.tensor), t)])
        if stop == "p2e" or (stop == "p2g" and not is_ctx):
            raise _Stop()

    nl = len(layers)
    try:
      for li, l in enumerate(layers):
          last = (l == DEPTH - 1)
          with_ctx = not last
          xs = x_in if l == 0 else x1
          cs = ctx_in if l == 0 else xc1
          xd = out_d if last else x1
          layer_setup(l)
          if stop == "setup":
              break
          pass1_group(l, cs, SEQ, CTX, 1, 16)
          for g in range(16):
              pass1_group(l, xs[g * 512:(g + 1) * 512, :], g * 512, 512, 0, g)
          if stop == "pass1":
              break
          lru_pass(l, with_ctx)
          if stop == "lru":
              break
          P.dma("sp", kctx[:], KT.rearrange("(b p) t -> p b t", p=128)[:, :, SEQ:NTOK], r=[("KT", 16)], w=["kctx"])
          P.dma("sp", vctx[:], Vs[SEQ:NTOK, :].rearrange("(j p) c -> p j c", p=128), r=[("V", 16)], w=["vctx"])
          if with_ctx:
              pass2_group(l, cs, xc1, "src", SEQ, CTX, 1, True, 16)
          for g in range(16):
              pass2_group(l, xs[g * 512:(g + 1) * 512, :], xd[g * 512:(g + 1) * 512, :], "src", g * 512, 512, 0, False, g)
    except _Stop:
        pass
    P.finish()
    P.emit()
    return nc, P


_CACHE = {}


def kernel(**inputs):
    inp = {k: np.asarray(v) for k, v in inputs.items()}
    sh = prep_shared(inp)

    def fl(a, n):
        return np.ascontiguousarray(a.reshape(a.shape[:n] + (-1,)))

    shared = dict(wcat=sh["wcat"], wada=sh["wada"], bcol=sh["bcol"], rows=sh["rows"], wsT=fl(sh["wsT"], 2),
                  lruW=fl(sh["lruW"], 2), lcol=fl(sh["lcol"], 2), ccol=fl(sh["ccol"], 2), adac=sh["adac"],
                  btab=fl(sh["btab"], 2), cosT=sh["cosT"], sinT=sh["sinT"], ident=sh["ident"])
    in_maps = []
    for b in range(NB):
        m = prep_core(inp, b)
        m.update(shared)
        in_maps.append(m)
    if "nc" not in _CACHE:
        _CACHE["nc"] = build()[0]
    res = run_bass_kernel_spmd(_CACHE["nc"], in_maps, core_ids=list(range(NB)))
    return np.stack([np.asarray(r["out"], dtype=np.float32) for r in res.results], axis=0)
```

```python
import numpy as np
import concourse.bass as bass
import concourse.mybir as mybir
from concourse.bass_utils import run_bass_kernel_spmd

F32 = mybir.dt.float32
BF16 = mybir.dt.bfloat16
AF = mybir.ActivationFunctionType
ALU = mybir.AluOpType

D = 1024
SEQ = 8192
CTX = 256
NTOK = SEQ + CTX
DEPTH = 2
NB = 8
ALPHA = (2 * DEPTH) ** 0.25
LN_EPS = 1e-5
NEG = -30000.0
NPIECE = 25
PSZ = 4096

ENGS = ("pe", "act", "dve", "pool", "sp")


class V:
    def __init__(self, ap, keys):
        self.ap = ap
        self.keys = list(keys)


def _keys(items):
    out = []
    for it in items:
        if isinstance(it, V):
            out.extend(it.keys)
        else:
            out.append(it)
    return out


class Prog:
    def __init__(self, nc):
        self.nc = nc
        self.q = {e: [] for e in ENGS}
        self.esem = {e: nc.alloc_semaphore("s_" + e) for e in ("pe", "act", "dve", "pool")}
        self.ecnt = {e: 0 for e in self.esem}
        self.dsem = {}
        self.last_w = {}
        self.readers = {}
        self.waited = {e: {} for e in ENGS}
        self.n_ops = 0
        self.ps_i = 0

    def _deps(self, eng, reads, writes):
        toks = {}

        def add(t):
            if t is None:
                return
            s, v = t
            k = id(s)
            if k not in toks or toks[k][1] < v:
                toks[k] = (s, v)

        for r in reads:
            add(self.last_w.get(r))
        for w in writes:
            add(self.last_w.get(w))
            for t in self.readers.get(w, {}).values():
                add(t)
        out = []
        wd = self.waited[eng]
        for k, (s, v) in toks.items():
            if wd.get(k, 0) >= v:
                continue
            wd[k] = v
            out.append((s, v))
        return out

    def _commit(self, tok, reads, writes):
        for w in writes:
            self.last_w[w] = tok
            self.readers[w] = {}
        for r in reads:
            self.readers.setdefault(r, {})[id(tok[0])] = tok

    def op(self, eng, fn, r=(), w=()):
        reads, writes = _keys(r), _keys(w)
        waits = self._deps(eng, reads, writes)
        sem = self.esem[eng]
        self.ecnt[eng] += 1
        tok = (sem, self.ecnt[eng])

        def run(e, waits=waits, fn=fn, sem=sem):
            for s, v in waits:
                e.wait_ge(s, v)
            fn(e).then_inc(sem, 1)

        self.q[eng].append(run)
        self._commit(tok, reads, writes)
        self.n_ops += 1
        return tok

    def dma(self, eng, out, in_, r=(), w=(), semkey=None):
        reads, writes = _keys(r), _keys(w)
        if semkey is None:
            semkey = next(k for k in (writes + reads) if isinstance(k, str) or k[0] == "pg")
        if semkey not in self.dsem:
            self.dsem[semkey] = [self.nc.alloc_semaphore("d%d" % len(self.dsem)), 0]
        ent = self.dsem[semkey]
        waits = self._deps(eng, reads, writes)
        ent[1] += 16
        tok = (ent[0], ent[1])

        def run(e, waits=waits, sem=ent[0], out=out, in_=in_):
            for s, v in waits:
                e.wait_ge(s, v)
            e.dma_start(out=out, in_=in_).then_inc(sem, 16)

        self.q[eng].append(run)
        self._commit(tok, reads, writes)
        self.n_ops += 1
        return tok

    def finish(self, eng="sp"):
        ents = [tuple(v) for v in self.dsem.values()]
        ecs = [(self.esem[e], self.ecnt[e]) for e in self.esem if self.ecnt[e] > 0]

        def run(e):
            for sem, cnt in ents:
                e.wait_ge(sem, cnt)
            for sem, cnt in ecs:
                e.wait_ge(sem, cnt)

        self.q[eng].append(run)

    def emit(self):
        with self.nc.Block() as block:
            @block.tensor
            def _(e):
                for f in self.q["pe"]:
                    f(e)

            @block.scalar
            def _(e):
                for f in self.q["act"]:
                    f(e)

            @block.vector
            def _(e):
                for f in self.q["dve"]:
                    f(e)

            @block.gpsimd
            def _(e):
                for f in self.q["pool"]:
                    f(e)

            @block.sync
            def _(e):
                for f in self.q["sp"]:
                    f(e)


def _partner():
    d = np.arange(64)
    return np.where(d < 16, d + 16, np.where(d < 32, d - 16, np.where(d < 48, d + 16, d - 16)))


def _piece_cols():
    ar = np.arange(512)
    permcols = (np.arange(8)[:, None] * 64 + _partner()[None, :]).reshape(-1)
    pcs = [(1536 + ar, "fm"), (3072 + ar, "fm"), (3072 + permcols, "fm"), (3584 + ar, "wide"),
           (0 + ar, "fm"), (512 + ar, "wide"), (1024 + ar, "fm"), (2048 + ar, "fm"),
           (2560 + ar, "fm"), (2560 + permcols, "fm"), (4096 + ar, "fm")]
    for hh in range(2):
        for n in range(3):
            pcs.append((4608 + n * 1024 + hh * 512 + ar, "fm"))
    return pcs


P_BX, P_K, P_KP, P_V, P_AU, P_AV, P_AG, P_BG, P_Q, P_QP, P_CG = range(11)
P_GM = 11
P_WBR = 17
P_WOUT = 23


def _fm(Wc):
    return np.ascontiguousarray(Wc.reshape(8, 128, 4, 128).transpose(1, 2, 0, 3)).reshape(128, PSZ)


def _wide(Wc):
    return np.ascontiguousarray(Wc.reshape(8, 128, 512).transpose(1, 0, 2)).reshape(128, PSZ)


def _col(v, n):
    return np.ascontiguousarray(np.asarray(v).reshape(n, 128).T)


def prep_shared(inp):
    f32 = np.float32
    pcs = _piece_cols()
    sh = {}
    wcat = np.zeros((DEPTH, NPIECE, 128, PSZ), f32)
    wada = np.zeros((DEPTH, 6, 128, PSZ), f32)
    bcol = np.zeros((DEPTH, 128, 17 * 4), f32)
    rows = np.zeros((DEPTH, 9, 1024), f32)
    wsT = np.zeros((DEPTH, 128, 4, 128), f32)
    lruW = np.zeros((DEPTH, 128, 2, 4, 2, 128), f32)
    lcol = np.zeros((DEPTH, 128, 2, 4, 3), f32)
    ccol = np.zeros((DEPTH, 128, 4, 5), f32)
    adac = np.zeros((DEPTH, 128, 16), f32)
    btab = np.full((DEPTH, 128, 14, 512), NEG, f32)
    qc = np.arange(64)
    c0 = np.clip(qc - 8, 0, 48)
    kc = np.arange(64)
    valid = (kc[:, None] >= c0[None, :]) & (kc[:, None] < c0[None, :] + 16)
    bidx = np.clip(kc[:, None] - qc[None, :] + 15, 0, 30)
    for l in range(DEPTH):
        w_in, b_in = inp["w_in"][l], inp["b_in"][l]
        for i, (cols, kind) in enumerate(pcs):
            wcat[l, i] = _fm(w_in[:, cols]) if kind == "fm" else _wide(w_in[:, cols])
            if kind == "fm":
                bcol[l, :, i * 4:(i + 1) * 4] = _col(b_in[cols], 4)
        for hh in range(2):
            for n in range(3):
                wb = inp["w_br"][l][n][:, hh * 512:(hh + 1) * 512]
                wcat[l, P_WBR + hh * 3 + n, :, :2048] = np.ascontiguousarray(
                    wb.reshape(4, 128, 512).transpose(1, 0, 2)).reshape(128, 2048)
        for half in range(2):
            wcat[l, P_WOUT + half] = _wide(inp["w_out"][l][:, half * 512:(half + 1) * 512])
        wa = inp["w_ada"][l]
        for i in range(4):
            wada[l, i] = _fm(wa[:, i * 512:(i + 1) * 512])
        for i in range(2):
            wada[l, 4 + i] = _wide(wa[:, 2048 + i * 512:2048 + (i + 1) * 512])
        adac[l] = _col(inp["b_ada"][l][:2048], 16)
        rows[l, 0] = inp["b_ada"][l][2048:]
        rows[l, 1] = inp["ln_g"][l]
        rows[l, 2] = inp["ln_b"][l]
        rows[l, 3, :512] = b_in[512:1024]
        rows[l, 4, :512] = b_in[3584:4096]
        rows[l, 5, :512] = inp["sgu_ln_g"][l]
        rows[l, 6, :512] = inp["sgu_ln_b"][l]
        rows[l, 7, :512] = inp["b_s"][l].reshape(-1)
        wsT[l] = inp["w_s"][l].transpose(2, 0, 1)
        for d in range(2):
            for ct in range(4):
                for wi, nm in enumerate(("lru_wa", "lru_wx")):
                    for e in range(2):
                        lruW[l, e * 64:(e + 1) * 64, d, ct, wi, e * 64:(e + 1) * 64] = inp[nm][l][d][2 * ct + e]
            lcol[l, :, d, :, 0] = _col(inp["lru_ba"][l][d], 4)
            lcol[l, :, d, :, 1] = _col(inp["lru_bx"][l][d], 4)
            lcol[l, :, d, :, 2] = _col(inp["lru_lam"][l][d], 4)
        for j in range(4):
            ccol[l, :, :, j] = _col(inp["conv_w"][l][j], 4)
        ccol[l, :, :, 4] = _col(inp["conv_b"][l], 4)
        rpb = inp["rpb"][l]
        for m in range(14):
            for e in range(2):
                g = rpb[:, m + e][:, bidx]
                g = np.where(valid[None], g, f32(NEG))
                g = g.reshape(4, 2, 64, 64).transpose(2, 1, 0, 3)
                btab[l, e * 64:(e + 1) * 64, m] = g.reshape(64, 512)
    inv_freq = (f32(10000.0) ** (-np.arange(16, dtype=f32) / f32(16))).astype(f32)
    pos = np.arange(SEQ)
    rw, cl = (pos // 64).astype(f32), (pos % 64).astype(f32)
    cosT = np.ones((128, NTOK), f32)
    sinT = np.zeros((128, NTOK), f32)
    for d in range(64):
        p = rw if d < 32 else cl
        ang = (p * inv_freq[d % 16]).astype(f32)
        sgn = -1.0 if (d % 32) < 16 else 1.0
        for e in range(2):
            cosT[e * 64 + d, :SEQ] = np.cos(ang)
            sinT[e * 64 + d, :SEQ] = sgn * np.sin(ang)
    sh.update(wcat=wcat, wada=wada, bcol=bcol, rows=rows, wsT=wsT, lruW=lruW, lcol=lcol, ccol=ccol,
              adac=adac, btab=btab, cosT=cosT, sinT=sinT, ident=np.eye(128, dtype=f32))
    return sh


def prep_core(inp, b):
    cvec = np.zeros((128, 8, 2), np.float32)
    cvec[:, :, 0] = _col(inp["c"][b], 8)
    cvec[:, :, 1] = _col(inp["c_ctx"], 8)
    return {"x": np.ascontiguousarray(inp["x"][b]), "ctx": np.ascontiguousarray(inp["ctx"][b]), "cvec": cvec}


ARF = 22528
PG = 512


class _Stop(Exception):
    pass


BF = True
NWB = 3


def build(layers=(0, 1), dbg=(), stop=None):
    WDT = BF16 if BF else F32
    nc = bass.Bass("TRN2", target_bir_lowering=False)
    P = Prog(nc)

    def din(name, shape, dt=F32):
        return nc.dram_tensor(name, list(shape), dt, kind="ExternalInput").ap()

    def dscr(name, shape, dt, out=False):
        kind = "ExternalOutput" if (out or name in dbg) else "Internal"
        return nc.dram_tensor(name, list(shape), dt, kind=kind).ap()

    x_in = din("x", [SEQ, D]); ctx_in = din("ctx", [CTX, D]); cvec_in = din("cvec", [128, 8, 2])
    wcat = din("wcat", [DEPTH, NPIECE, 128, PSZ]); wada = din("wada", [DEPTH, 6, 128, PSZ])
    bcol_in = din("bcol", [DEPTH, 128, 68]); rows_in = din("rows", [DEPTH, 9, 1024])
    wsT_in = din("wsT", [DEPTH, 128, 512]); lruW_in = din("lruW", [DEPTH, 128, 2048])
    lcol_in = din("lcol", [DEPTH, 128, 24]); ccol_in = din("ccol", [DEPTH, 128, 20])
    adac_in = din("adac", [DEPTH, 128, 16]); btab_in = din("btab", [DEPTH, 128, 14 * 512])
    cosT_in = din("cosT", [128, NTOK]); sinT_in = din("sinT", [128, NTOK]); ident_in = din("ident", [128, 128])

    out_d = dscr("out", [SEQ, D], F32, out=True)
    wbf = dscr("wbf", [DEPTH, NPIECE, 128, PSZ], BF16) if BF else None
    bxT = dscr("bxT", [512, NTOK], F32)
    KT = dscr("KT", [512, NTOK], BF16)
    Vs = dscr("Vs", [NTOK, 512], BF16)
    hfs = dscr("hfs", [512, SEQ], F32)
    ybT = dscr("ybT", [512, NTOK], BF16)
    x1 = dscr("x1", [SEQ, D], F32)
    xc1 = dscr("xc1", [CTX, D], F32)

    def S(name, shape, dt=F32):
        return nc.alloc_sbuf_tensor("sb_" + name, list(shape), dt)

    AR = S("AR", [128, ARF])
    bcol = S("bcol", [128, 68]); adac = S("adac", [128, 16]); lcol = S("lcol", [128, 2, 4, 3]); ccol = S("ccol", [128, 4, 5])
    wsT = S("wsT", [128, 4, 128]); lruW = S("lruW", [128, 2, 4, 2, 128]); ident = S("ident", [128, 128])
    ones_bf = S("ones_bf", [128, 128], BF16)
    gate_bc = S("gate_bc", [128, 2, 1024]); lng_bc = S("lng_bc", [128, 1024]); lnb_bc = S("lnb_bc", [128, 1024])
    bav_bc = S("bav_bc", [128, 512]); bv_bc = S("bv_bc", [128, 512]); sg_bc = S("sg_bc", [128, 512])
    sb_bc = S("sb_bc", [128, 512]); bs_bc = S("bs_bc", [128, 512])
    btab = S("btab", [128, 14, 512], BF16)
    kctx = S("kctx", [128, 4, 256], BF16); vctx = S("vctx", [128, 2, 512], BF16)
    cv = S("cv", [128, 8, 2]); scv = S("scv", [128, 8, 2]); modc = S("modc", [128, 16, 2]); lcc = S("lcc", [128, 2, 4]); lcc2 = S("lcc2", [128, 2, 4])
    ltmp = S("ltmp", [128, 2, 4])
    uT = S("uT", [128, 8, 512], WDT)
    wb = [S("wb%d" % i, [128, PSZ], WDT) for i in range(NWB)]
    yaT = S("yaT", [128, 4, 512], WDT); ybs = S("ybs", [128, 4, 512], WDT); ycs = S("ycs", [128, 4, 512], WDT)
    mT = S("mT", [128, 8, 512], WDT)
    carry = S("carry", [128, 1]); hcf = S("hcf", [128, 256])
    st6 = S("st6", [128, 12]); mv2 = S("mv2", [128, 2]); rs1 = S("rs1", [128, 1])
    psum2 = [nc.alloc_psum_tensor("ps%d" % i, [128, 1024], F32) for i in range(4)]

    def ps():
        i = P.ps_i % 8
        P.ps_i += 1
        return V(psum2[i // 2][:, (i % 2) * 512:(i % 2 + 1) * 512], ["ps%d" % i])

    def psp():
        if P.ps_i % 2:
            P.ps_i += 1
        i = P.ps_i % 8
        P.ps_i += 2
        return V(psum2[i // 2][:, :], ["ps%d" % i, "ps%d" % (i + 1)])

    def av(off_pg, shape, dt=F32, off=0):
        n = int(np.prod(shape[1:]))
        nf = n if dt == F32 else n // 2
        o = off_pg * PG + off
        ap = AR[:, o:o + nf]
        if dt != F32:
            ap = ap.bitcast(BF16)
        if len(shape) == 3:
            ap = ap.rearrange("p (a b) -> p a b", a=shape[1])
        elif len(shape) == 4:
            ap = ap.rearrange("p (a b c) -> p a b c", a=shape[1], b=shape[2])
        v = V(ap, [("pg", i) for i in range(o // PG, (o + nf - 1) // PG + 1)])
        v.o = o
        v.nf = nf
        return v

    def rev_ap(v, n):
        return bass.AP(AR, v.o + n - 1, [[ARF, 128], [-1, n]])

    wslot = [0]

    def wload(l, piece, nelem=PSZ):
        i = wslot[0] % NWB
        wslot[0] += 1
        if BF:
            P.dma("sp", wb[i][:, 0:nelem], wbf[l, piece, :, 0:nelem], r=[("wbf", l, piece)], w=["wb%d" % i])
        else:
            P.dma("sp", wb[i][:, 0:nelem], wcat[l, piece, :, 0:nelem], w=["wb%d" % i])
        return V(wb[i], ["wb%d" % i])

    def fmv(w):
        return w.ap[:, :].rearrange("p (b k c) -> p b k c", b=4, k=8)

    def widev(w, k=8):
        return w.ap[:, 0:k * 512].rearrange("p (k c) -> p k c", k=k)

    def mm_fm(w, b, ntok):
        p = ps()
        wv = fmv(w)

        def f(e):
            for k in range(8):
                ins = e.matmul(p.ap[:, 0:ntok], lhsT=wv[:, b, k, :], rhs=uT[:, k, 0:ntok], start=(k == 0), stop=(k == 7))
            return ins
        P.op("pe", f, r=[w, "uT"], w=[p])
        return p

    def cast_weights(l):
        st32 = [av(0, [128, PSZ]), av(8, [128, PSZ])]
        st16 = [av(16, [128, PSZ], BF16), av(20, [128, PSZ], BF16)]
        for i in range(NPIECE):
            s_ = i % 2
            P.dma("sp", st32[s_].ap, wcat[l, i], w=[st32[s_]])
            P.op("dve", lambda e, s_=s_: e.tensor_copy(out=st16[s_].ap, in_=st32[s_].ap), r=[st32[s_]], w=[st16[s_]])
            P.dma("pool", wbf[l, i], st16[s_].ap, r=[st16[s_]], w=[("wbf", l, i)])

    if BF:
        for l in layers:
            cast_weights(l)
    P.dma("sp", ident[:], ident_in, w=["ident"])
    P.op("pool", lambda e: e.memset(ones_bf[:], 1.0), w=["ones_bf"])
    P.dma("sp", cv[:], cvec_in, w=["cv"])
    P.op("act", lambda e: e.activation(out=scv[:], in_=cv[:], func=AF.Silu), r=["cv"], w=["scv"])

    def load_uT(src, ntok, who):
        xts = [av(24, [128, 1024]), av(26, [128, 1024])]
        for t in range(ntok // 128):
            xv = xts[t % 2]
            P.dma("sp", xv.ap, src[t * 128:(t + 1) * 128, :], r=[("src", id(src.tensor), t)], w=[xv])
            pa, pb = ps(), ps()

            def f(e, xv=xv, pa=pa, pb=pb):
                for k in range(8):
                    pp = pa if k < 4 else pb
                    ins = e.transpose(pp.ap[:, (k % 4) * 128:(k % 4 + 1) * 128], xv.ap[:, k * 128:(k + 1) * 128], ident[:])
                return ins
            P.op("pe", f, r=[xv, "ident"], w=[pa, pb])
            for k in range(8):
                pp = pa if k < 4 else pb
                P.op("act", lambda e, k=k, pp=pp, t=t: e.activation(
                    out=uT[:, k, t * 128:(t + 1) * 128], in_=pp.ap[:, (k % 4) * 128:(k % 4 + 1) * 128],
                    func=AF.Identity, bias=modc[:, k, who:who + 1], scale=modc[:, 8 + k, who:who + 1]),
                    r=[pp, "modc"], w=["uT"])

    def rope_proj(l, pc, pcp, ntok, cosv, sinv, dst_ap, dstkeys, t1s, t2s):
        w1 = wload(l, pc)
        w2 = wload(l, pcp)
        for b in range(4):
            p1 = mm_fm(w1, b, ntok)
            p2 = mm_fm(w2, b, ntok)
            t1, t2 = t1s[b % len(t1s)], t2s[b % len(t2s)]
            P.op("dve", lambda e, p1=p1, t1=t1, b=b: e.scalar_tensor_tensor(
                out=t1.ap[:, 0:ntok], in0=p1.ap[:, 0:ntok], scalar=bcol[:, pc * 4 + b:pc * 4 + b + 1], in1=cosv.ap[:, 0:ntok],
                op0=ALU.add, op1=ALU.mult), r=[p1, cosv, "bcol"], w=[t1])
            P.op("dve", lambda e, p2=p2, t2=t2, b=b: e.scalar_tensor_tensor(
                out=t2.ap[:, 0:ntok], in0=p2.ap[:, 0:ntok], scalar=bcol[:, pcp * 4 + b:pcp * 4 + b + 1], in1=sinv.ap[:, 0:ntok],
                op0=ALU.add, op1=ALU.mult), r=[p2, sinv, "bcol"], w=[t2])
            P.op("pool", lambda e, t1=t1, t2=t2, b=b: e.tensor_tensor(
                out=dst_ap[:, b, 0:ntok], in0=t1.ap[:, 0:ntok], in1=t2.ap[:, 0:ntok], op=ALU.add),
                r=[t1, t2], w=dstkeys)

    def layer_setup(l):
        P.dma("sp", bcol[:], bcol_in[l], w=["bcol"])
        P.dma("sp", adac[:], adac_in[l], w=["adac"])
        P.dma("sp", lcol[:].rearrange("p a b c -> p (a b c)"), lcol_in[l], w=["lcol"])
        P.dma("sp", ccol[:].rearrange("p a b -> p (a b)"), ccol_in[l], w=["ccol"])
        P.dma("sp", wsT[:].rearrange("p a b -> p (a b)"), wsT_in[l], w=["wsT"])
        P.dma("sp", lruW[:].rearrange("p a b c d -> p (a b c d)"), lruW_in[l], w=["lruW"])
        for hf_ in range(2):
            bt32 = av(hf_ * 7, [128, 7 * 512])
            P.dma("sp", bt32.ap, btab_in[l][:, hf_ * 3584:(hf_ + 1) * 3584], w=[bt32])
            P.op("pool", lambda e, hf_=hf_, bt32=bt32: e.tensor_copy(out=btab[:, hf_ * 7:(hf_ + 1) * 7, :].rearrange("p a b -> p (a b)"), in_=bt32.ap),
                 r=[bt32], w=["btab"])
        P.dma("sp", lng_bc[:], rows_in[l, 1, :].partition_broadcast(128), w=["lng_bc"])
        P.dma("sp", lnb_bc[:], rows_in[l, 2, :].partition_broadcast(128), w=["lnb_bc"])
        for i, (t, nm) in enumerate([(bav_bc, "bav_bc"), (bv_bc, "bv_bc"), (sg_bc, "sg_bc"), (sb_bc, "sb_bc"), (bs_bc, "bs_bc")]):
            P.dma("sp", t[:], rows_in[l, 3 + i, 0:512].partition_broadcast(128), w=[nm])
        badag = av(24, [128, 1024])
        P.dma("sp", badag.ap, rows_in[l, 0, :].partition_broadcast(128), w=[badag])
        P.op("act", lambda e: e.activation(out=ltmp[:], in_=lcol[:, :, :, 2], func=AF.Exp, scale=-1.0), r=["lcol"], w=["ltmp"])
        P.op("act", lambda e: e.activation(out=ltmp[:], in_=ltmp[:], func=AF.Ln, bias=1.0), r=["ltmp"], w=["ltmp"])
        P.op("dve", lambda e: e.tensor_scalar(out=lcc[:], in0=ltmp[:], scalar1=-8.0, scalar2=None, op0=ALU.mult), r=["ltmp"], w=["lcc"])
        P.op("dve", lambda e: e.tensor_scalar(out=lcc2[:], in0=ltmp[:], scalar1=-16.0, scalar2=None, op0=ALU.mult), r=["ltmp"], w=["lcc2"])
        screp = av(20, [128, 16, 128])
        P.op("dve", lambda e: e.tensor_copy(out=screp.ap, in_=scv[:].rearrange("p a b -> p (a b)").unsqueeze(2).broadcast_to([128, 16, 128])),
             r=["scv"], w=[screp])
        for i in range(4):
            wv = av((i % 2) * 8, [128, 4, 8, 128])
            P.dma("sp", wv.ap.rearrange("p a b c -> p (a b c)"), wada[l, i], w=[wv])
            for b in range(4):
                blk = i * 4 + b
                p = ps()

                def f(e, wv=wv, b=b, p=p):
                    for k in range(8):
                        ins = e.matmul(p.ap[:, 0:2], lhsT=wv.ap[:, b, k, :], rhs=scv[:, k, :], start=(k == 0), stop=(k == 7))
                    return ins
                P.op("pe", f, r=[wv, "scv"], w=[p])
                P.op("dve", lambda e, p=p, blk=blk: e.tensor_scalar(
                    out=modc[:, blk, :], in0=p.ap[:, 0:2], scalar1=adac[:, blk:blk + 1], scalar2=(1.0 if blk >= 8 else 0.0),
                    op0=ALU.add, op1=ALU.add), r=[p, "adac"], w=["modc"])
        for i in range(2):
            wv = av((i % 2) * 8, [128, 8, 512])
            P.dma("sp", wv.ap.rearrange("p a b -> p (a b)"), wada[l, 4 + i], w=[wv])
            for who in range(2):
                p = ps()

                def f(e, wv=wv, who=who, p=p):
                    for k in range(8):
                        ins = e.matmul(p.ap[:, 0:512], lhsT=screp.ap[:, k * 2 + who, :], rhs=wv.ap[:, k, :], start=(k == 0), stop=(k == 7))
                    return ins
                P.op("pe", f, r=[wv, screp], w=[p])
                P.op("dve", lambda e, p=p, who=who, i=i: e.tensor_tensor(
                    out=gate_bc[:, who, i * 512:(i + 1) * 512], in0=p.ap[:, 0:512], in1=badag.ap[:, i * 512:(i + 1) * 512], op=ALU.add),
                    r=[p, badag], w=["gate_bc"])

    def pass1_group(l, src, tok0, ntok, who, g):
        nt = ntok // 128
        load_uT(src, ntok, who)
        if stop == "p1a":
            raise _Stop()
        cosv, sinv = av(0, [128, 512]), av(1, [128, 512])
        P.dma("sp", cosv.ap[:, 0:ntok], cosT_in[:, tok0:tok0 + ntok], w=[cosv])
        P.dma("sp", sinv.ap[:, 0:ntok], sinT_in[:, tok0:tok0 + ntok], w=[sinv])
        bxs = av(2, [128, 4, 512])
        w = wload(l, P_BX)
        for b in range(4):
            p = mm_fm(w, b, ntok)
            P.op("act", lambda e, p=p, b=b: e.activation(out=bxs.ap[:, b, 0:ntok], in_=p.ap[:, 0:ntok], func=AF.Identity,
                                                          bias=bcol[:, P_BX * 4 + b:P_BX * 4 + b + 1], scale=1.0),
                 r=[p, "bcol"], w=[bxs])
        if stop == "p1a2":
            raise _Stop()
        for b in range(4):
            P.dma("pool", bxT[b * 128:(b + 1) * 128, tok0:tok0 + ntok], bxs.ap[:, b, 0:ntok], r=[bxs], w=[("bxT", g)])
        if stop == "p1b":
            raise _Stop()
        kTs = av(6, [128, 4, 512], BF16)
        t1s = [av(8, [128, 512]), av(9, [128, 512])]
        t2s = [av(10, [128, 512]), av(11, [128, 512])]
        rope_proj(l, P_K, P_KP, ntok, cosv, sinv, kTs.ap, [kTs], t1s, t2s)
        for b in range(4):
            P.dma("pool", KT[b * 128:(b + 1) * 128, tok0:tok0 + ntok], kTs.ap[:, b, 0:ntok], r=[kTs], w=[("KT", g)])
        if stop == "p1c":
            raise _Stop()
        vts = av(12, [128, 4, 512], BF16)
        w = wload(l, P_V)
        wv = widev(w)
        for t in range(nt):
            p = ps()

            def f(e, t=t, p=p):
                for k in range(8):
                    ins = e.matmul(p.ap[:, 0:512], lhsT=uT[:, k, t * 128:(t + 1) * 128], rhs=wv[:, k, :], start=(k == 0), stop=(k == 7))
                return ins
            P.op("pe", f, r=[w, "uT"], w=[p])
            P.op("dve", lambda e, t=t, p=p: e.tensor_tensor(out=vts.ap[:, t, :], in0=p.ap[:, 0:512], in1=bv_bc[:], op=ALU.add),
                 r=[p, "bv_bc"], w=[vts])
        for t in range(nt):
            P.dma("pool", Vs[tok0 + t * 128:tok0 + (t + 1) * 128, :], vts.ap[:, t, :], r=[vts], w=[("V", g)])
        if stop == "p1d":
            raise _Stop()

    def lru_pass(l, with_ctx):
        NS = 1024
        sets = []
        for si in range(2):
            o = si * 18
            sets.append(dict(bxh=av(o, [128, 1536]), xc=av(o + 3, [128, NS]), rr=av(o + 5, [128, NS]), ii=av(o + 7, [128, NS]),
                             aa=av(o + 9, [128, NS]), bq=av(o + 11, [128, NS]), hh=av(o + 13, [128, NS]), hfl=av(o + 15, [128, NS]),
                             yo=av(o + 17, [128, NS], BF16)))
        allbx = [("bxT", g) for g in range(17)]

        def front(sg):
            B_ = sets[sg["i"] % 2]
            ct, d, colbase, seqlen, s0, n, is_ctx = sg["ct"], sg["d"], sg["colbase"], sg["seqlen"], sg["s0"], sg["n"], sg["is_ctx"]
            b_, xc, rr, ii, aa, bq = B_["bxh"], B_["xc"], B_["rr"], B_["ii"], B_["aa"], B_["bq"]
            lo, hi = max(s0 - 2, 0), min(s0 + n + 1, seqlen)
            if lo > s0 - 2 or hi < s0 + n + 1:
                P.op("pool", lambda e: e.memset(b_.ap[:, 0:n + 3], 0.0), w=[b_])
            P.dma("sp", b_.ap[:, lo - (s0 - 2):hi - (s0 - 2)], bxT[ct * 128:(ct + 1) * 128, colbase + lo:colbase + hi],
                  r=allbx, w=[b_])
            if d == 1 and not is_ctx:
                P.dma("sp", B_["hfl"].ap[:, 0:n], hfs[ct * 128:(ct + 1) * 128, s0:s0 + n], r=[("hfs", ct, s0)], w=[B_["hfl"]])
            P.op("dve", lambda e: e.tensor_scalar(out=xc.ap[:, 0:n], in0=b_.ap[:, 0:n], scalar1=ccol[:, ct, 0:1], scalar2=ccol[:, ct, 4:5],
                                                   op0=ALU.mult, op1=ALU.add), r=[b_, "ccol"], w=[xc])
            for j in range(1, 4):
                P.op("dve", lambda e, j=j: e.scalar_tensor_tensor(out=xc.ap[:, 0:n], in0=b_.ap[:, j:j + n], scalar=ccol[:, ct, j:j + 1],
                                                                   in1=xc.ap[:, 0:n], op0=ALU.mult, op1=ALU.add), r=[b_, xc, "ccol"], w=[xc])
            for c in range((n + 511) // 512):
                cn = min(512, n - c * 512)
                for wi, dst in ((0, rr), (1, ii)):
                    p = ps()
                    P.op("pe", lambda e, p=p, c=c, cn=cn, wi=wi: e.matmul(p.ap[:, 0:cn], lhsT=lruW[:, d, ct, wi, :], rhs=xc.ap[:, c * 512:c * 512 + cn],
                                                                           start=True, stop=True), r=[xc, "lruW"], w=[p])
                    P.op("act", lambda e, p=p, c=c, cn=cn, wi=wi, dst=dst: e.activation(
                        out=dst.ap[:, c * 512:c * 512 + cn], in_=p.ap[:, 0:cn], func=AF.Sigmoid, bias=lcol[:, d, ct, wi:wi + 1], scale=1.0),
                        r=[p, "lcol"], w=[dst])
            P.op("act", lambda e: e.activation(out=aa.ap[:, 0:n], in_=rr.ap[:, 0:n], func=AF.Exp, scale=lcc[:, d, ct:ct + 1]), r=[rr, "lcc"], w=[aa])
            P.op("act", lambda e: e.activation(out=bq.ap[:, 0:n], in_=rr.ap[:, 0:n], func=AF.Exp, scale=lcc2[:, d, ct:ct + 1]), r=[rr, "lcc2"], w=[bq])
            P.op("act", lambda e: e.activation(out=bq.ap[:, 0:n], in_=bq.ap[:, 0:n], func=AF.Sqrt, bias=1.0, scale=-1.0), r=[bq], w=[bq])
            P.op("pool", lambda e: e.tensor_tensor(out=ii.ap[:, 0:n], in0=ii.ap[:, 0:n], in1=xc.ap[:, 0:n], op=ALU.mult), r=[ii, xc], w=[ii])

        def back(sg):
            B_ = sets[sg["i"] % 2]
            ct, d, s0, n, is_ctx, first = sg["ct"], sg["d"], sg["s0"], sg["n"], sg["is_ctx"], sg["first"]
            ii, aa, bq, hh, yo, hf_ = B_["ii"], B_["aa"], B_["bq"], B_["hh"], B_["yo"], B_["hfl"]
            P.op("dve", lambda e: e.tensor_tensor(out=bq.ap[:, 0:n], in0=bq.ap[:, 0:n], in1=ii.ap[:, 0:n], op=ALU.mult), r=[bq, ii], w=[bq])
            init = 0.0 if first else carry[:, 0:1]
            if d == 0:
                P.op("dve", lambda e: e.tensor_tensor_scan(out=hh.ap[:, 0:n], data0=aa.ap[:, 0:n], data1=bq.ap[:, 0:n], initial=init,
                                                            op0=ALU.mult, op1=ALU.add), r=[aa, bq, "carry"], w=[hh])
                P.op("dve", lambda e: e.tensor_copy(out=carry[:], in_=hh.ap[:, n - 1:n]), r=[hh], w=["carry"])
            else:
                P.op("dve", lambda e: e.tensor_tensor_scan(out=rev_ap(hh, n), data0=rev_ap(aa, n), data1=rev_ap(bq, n), initial=init,
                                                            op0=ALU.mult, op1=ALU.add), r=[aa, bq, "carry"], w=[hh])
                P.op("dve", lambda e: e.tensor_copy(out=carry[:], in_=hh.ap[:, 0:1]), r=[hh], w=["carry"])
            rows = slice(ct * 128, (ct + 1) * 128)
            if d == 0:
                if is_ctx:
                    P.op("pool", lambda e: e.tensor_copy(out=hcf[:], in_=hh.ap[:, 0:n]), r=[hh], w=["hcf"])
                else:
                    P.dma("pool", hfs[rows, s0:s0 + n], hh.ap[:, 0:n], r=[hh], w=[("hfs", ct, s0)])
            else:
                if is_ctx:
                    if with_ctx:
                        P.op("pool", lambda e: e.tensor_tensor(out=yo.ap[:, 0:n], in0=hh.ap[:, 0:n], in1=hcf[:], op=ALU.add), r=[hh, "hcf"], w=[yo])
                        P.dma("pool", ybT[rows, SEQ:SEQ + n], yo.ap[:, 0:n], r=[yo], w=[("ybT", ct, "c")])
                else:
                    P.op("pool", lambda e: e.tensor_tensor(out=yo.ap[:, 0:n], in0=hh.ap[:, 0:n], in1=hf_.ap[:, 0:n], op=ALU.add), r=[hh, hf_], w=[yo])
                    P.dma("pool", ybT[rows, s0:s0 + n], yo.ap[:, 0:n], r=[yo], w=[("ybT", ct, s0)])

        segs = []
        for ct in range(4):
            for d in range(2):
                segs.append(dict(ct=ct, d=d, colbase=SEQ, seqlen=CTX, s0=0, n=CTX, first=True, is_ctx=True))
                order = range(0, SEQ, NS) if d == 0 else range(SEQ - NS, -1, -NS)
                for s0 in order:
                    segs.append(dict(ct=ct, d=d, colbase=0, seqlen=SEQ, s0=s0, n=NS, first=False, is_ctx=False))
        for i, sg in enumerate(segs):
            sg["i"] = i
        front(segs[0])
        for i in range(len(segs)):
            if i + 1 < len(segs):
                front(segs[i + 1])
            back(segs[i])

    def attention(l, g, tok0, ntok, is_ctx, qT, ycT):
        nrow = ntok // 64
        kwin = av(4, [128, 4, 1024], BF16)
        vwin = [av(8, [128, 4, 512], BF16), av(10, [128, 4, 512], BF16)]
        pTs = [av(12, [128, 6, 512], BF16), av(25, [128, 6, 512], BF16)]
        sps = [av(15, [128, 512]), av(16, [128, 512])]
        rz = av(17, [128, 512])
        if not is_ctx:
            t_lo, t_hi = max(0, tok0 - 256), min(SEQ, tok0 + 512 + 256)
            gs = [gg for gg in (g - 1, g, g + 1) if 0 <= gg < 16]
            for b in range(4):
                P.dma("sp", kwin.ap[:, b, 0:t_hi - t_lo], KT[b * 128:(b + 1) * 128, t_lo:t_hi],
                      r=[("KT", gg) for gg in gs], w=[kwin])
        state = {}

        def part1(rq):
            pT = pTs[rq % 2]
            koff = jv = 0
            vw = None
            tiles = []
            if not is_ctx:
                r = (tok0 // 64) + rq
                r0 = min(max(r - 4, 0), 120)
                jv = r0 - r + 7
                koff = r0 * 64 - t_lo
                vw = vwin[rq % 2]
                for jj_ in range(4):
                    P.dma("sp", vw.ap[:, jj_, :], Vs[r0 * 64 + jj_ * 128:r0 * 64 + (jj_ + 1) * 128, :],
                          r=[("V", gg) for gg in gs], w=[vw])
                tiles += [("loc", j) for j in range(4)]
            tiles += [("ctx", 0), ("ctx", 1)]
            ntile = len(tiles)
            for ti, (kind, j) in enumerate(tiles):
                p = psp()
                p3 = p.ap.rearrange("p (e y) -> p e y", e=2)[:, :, 0:256]

                def f(e, kind=kind, j=j, p=p, rq=rq, koff=(koff if not is_ctx else 0)):
                    for h in range(8):
                        c, ee = h // 2, h % 2
                        if kind == "loc":
                            lt = kwin.ap[64 * ee:64 * ee + 64, c, koff + j * 128:koff + (j + 1) * 128]
                        else:
                            lt = kctx[64 * ee:64 * ee + 64, c, j * 128:(j + 1) * 128]
                        ins = e.matmul(p.ap[:, ee * 512 + c * 64:ee * 512 + (c + 1) * 64], lhsT=lt,
                                       rhs=qT.ap[64 * ee:64 * ee + 64, c, rq * 64:(rq + 1) * 64], start=True, stop=True)
                    return ins
                P.op("pe", f, r=[kwin if kind == "loc" else "kctx", qT], w=[p])
                if kind == "loc":
                    sp_ = sps[ti % 2]
                    P.op("dve", lambda e, p3=p3, sp_=sp_, j=j, jv=jv: e.scalar_tensor_tensor(
                        out=sp_.ap.rearrange("p (e x) -> p e x", e=2), in0=p3, scalar=0.125,
                        in1=btab[:, 2 * j + jv, :].rearrange("p (e x) -> p e x", e=2), op0=ALU.mult, op1=ALU.add),
                        r=[p, "btab"], w=[sp_])
                    P.op("act", lambda e, sp_=sp_, ti=ti, pT=pT: e.activation(out=pT.ap[:, ti, :], in_=sp_.ap, func=AF.Exp),
                         r=[sp_], w=[pT])
                else:
                    P.op("act", lambda e, p3=p3, ti=ti, pT=pT: e.activation(out=pT.ap[:, ti, :].rearrange("p (e x) -> p e x", e=2), in_=p3,
                                                                              func=AF.Exp, scale=0.125), r=[p], w=[pT])
            state[rq] = (pT, tiles, vw)

        def part2(rq):
            pT, tiles, vw = state.pop(rq)
            ntile = len(tiles)
            po, pz = ps(), ps()

            def fo(e, tiles=tiles, po=po, pT=pT, vw=(vw if not is_ctx else None)):
                for h in range(8):
                    c, ee = h // 2, h % 2
                    for ti, (kind, j) in enumerate(tiles):
                        lt = vw.ap[:, j, h * 64:(h + 1) * 64] if kind == "loc" else vctx[:, j, h * 64:(h + 1) * 64]
                        ins = e.matmul(po.ap[64 * ee:64 * ee + 64, c * 64:(c + 1) * 64], lhsT=lt, rhs=pT.ap[:, ti, ee * 256 + c * 64:ee * 256 + (c + 1) * 64],
                                       start=(ti == 0), stop=(ti == len(tiles) - 1), tile_position=(0, 64 * ee))
                return ins
            P.op("pe", fo, r=[pT, "vctx"] + ([vw] if not is_ctx else []), w=[po])

            def fz(e, pz=pz, pT=pT, ntile=ntile):
                for ti in range(ntile):
                    ins = e.matmul(pz.ap[:, 0:512], lhsT=ones_bf[:], rhs=pT.ap[:, ti, :], start=(ti == 0), stop=(ti == ntile - 1))
                return ins
            P.op("pe", fz, r=[pT, "ones_bf"], w=[pz])

            P.op("dve", lambda e, pz=pz: e.reciprocal(out=rz.ap, in_=pz.ap[:, 0:512]), r=[pz], w=[rz])
            for ee in range(2):
                P.op("dve", lambda e, ee=ee, po=po, rq=rq: e.tensor_tensor(
                    out=ycT.ap[64 * ee:64 * ee + 64, :, rq * 64:(rq + 1) * 64],
                    in0=po.ap[64 * ee:64 * ee + 64, 0:256].rearrange("p (c q) -> p c q", c=4),
                    in1=rz.ap[64 * ee:64 * ee + 64, ee * 256:(ee + 1) * 256].rearrange("p (c q) -> p c q", c=4),
                    op=ALU.mult), r=[po, rz], w=[ycT])

        part1(0)
        for rq in range(nrow):
            if rq + 1 < nrow:
                part1(rq + 1)
            part2(rq)

    def pass2_group(l, src, dst, dstkey, tok0, ntok, who, is_ctx, g):
        nt = ntok // 128
        load_uT(src, ntok, who)
        gu = av(0, [128, 4, 512]); sgA = av(4, [128, 4, 512]); vn = av(8, [128, 4, 512])
        vpre = av(12, [128, 512]); gv = av(13, [128, 512]); ta = av(14, [128, 512])
        w = wload(l, P_AV)
        wv = widev(w)
        for t in range(nt):
            p = ps()

            def f(e, t=t, p=p, wv=wv):
                for k in range(8):
                    ins = e.matmul(p.ap[:, 0:512], lhsT=uT[:, k, t * 128:(t + 1) * 128], rhs=wv[:, k, :], start=(k == 0), stop=(k == 7))
                return ins
            P.op("pe", f, r=[w, "uT"], w=[p])
            P.op("dve", lambda e, p=p: e.tensor_tensor(out=vpre.ap, in0=p.ap[:, 0:512], in1=bav_bc[:], op=ALU.add), r=[p, "bav_bc"], w=[vpre])
            P.op("act", lambda e: e.activation(out=gv.ap, in_=vpre.ap, func=AF.Gelu), r=[vpre], w=[gv])
            P.op("dve", lambda e: e.bn_stats(out=st6[:, 0:6], in_=gv.ap), r=[gv], w=["st6"])
            P.op("dve", lambda e: e.bn_aggr(out=mv2[:], in_=st6[:, 0:6]), r=["st6"], w=["mv2"])
            P.op("dve", lambda e: e.tensor_scalar(out=rs1[:], in0=mv2[:, 1:2], scalar1=LN_EPS, scalar2=None, op0=ALU.add), r=["mv2"], w=["rs1"])
            P.op("act", lambda e: e.activation(out=rs1[:], in_=rs1[:], func=AF.Sqrt), r=["rs1"], w=["rs1"])
            P.op("dve", lambda e: e.reciprocal(out=rs1[:], in_=rs1[:]), r=["rs1"], w=["rs1"])
            P.op("dve", lambda e, t=t: e.tensor_scalar(out=vn.ap[:, t, :], in0=gv.ap, scalar1=mv2[:, 0:1], scalar2=rs1[:, 0:1],
                                                       op0=ALU.subtract, op1=ALU.mult), r=[gv, "mv2", "rs1"], w=[vn])
            P.op("pool", lambda e, t=t: e.tensor_tensor(out=vn.ap[:, t, :], in0=vn.ap[:, t, :], in1=sg_bc[:], op=ALU.mult), r=[vn, "sg_bc"], w=[vn])
            P.op("pool", lambda e, t=t: e.tensor_tensor(out=vn.ap[:, t, :], in0=vn.ap[:, t, :], in1=sb_bc[:], op=ALU.add), r=[vn, "sb_bc"], w=[vn])
        for pc, dstv, fn in ((P_AU, gu, AF.Gelu), (P_AG, sgA, AF.Silu)):
            w = wload(l, pc)
            for b in range(4):
                p = mm_fm(w, b, ntok)
                P.op("act", lambda e, p=p, b=b, dstv=dstv, fn=fn, pc=pc: e.activation(
                    out=dstv.ap[:, b, 0:ntok], in_=p.ap[:, 0:ntok], func=fn, bias=bcol[:, pc * 4 + b:pc * 4 + b + 1], scale=1.0),
                    r=[p, "bcol"], w=[dstv])
        for g4 in range(4):
            p = ps()

            def f(e, g4=g4, p=p):
                for t in range(nt):
                    ins = e.matmul(p.ap[:, t * 128:(t + 1) * 128], lhsT=vn.ap[:, t, g4 * 128:(g4 + 1) * 128], rhs=wsT[:, g4, :], start=True, stop=True)
                return ins
            P.op("pe", f, r=[vn, "wsT"], w=[p])
            P.op("dve", lambda e, g4=g4, p=p: e.tensor_tensor(
                out=ta.ap[:, 0:ntok].rearrange("p (t q) -> p t q", t=nt), in0=p.ap[:, 0:ntok].rearrange("p (t q) -> p t q", t=nt),
                in1=bs_bc[:, g4 * 128:(g4 + 1) * 128].unsqueeze(1).broadcast_to([128, nt, 128]), op=ALU.add), r=[p, "bs_bc"], w=[ta])
            P.op("dve", lambda e, g4=g4: e.tensor_tensor(out=ta.ap[:, 0:ntok], in0=ta.ap[:, 0:ntok], in1=gu.ap[:, g4, 0:ntok], op=ALU.mult),
                 r=[ta, gu], w=[ta])
            P.op("pool", lambda e, g4=g4: e.tensor_tensor(out=yaT[:, g4, 0:ntok], in0=ta.ap[:, 0:ntok], in1=sgA.ap[:, g4, 0:ntok], op=ALU.mult),
                 r=[ta, sgA], w=["yaT"])
        if stop == "p2a":
            raise _Stop()
        ybl = av(16, [128, 4, 512], BF16)
        tmpb = av(12, [128, 512])
        ycol = SEQ if is_ctx else tok0
        for b in range(4):
            P.dma("sp", ybl.ap[:, b, 0:ntok], ybT[b * 128:(b + 1) * 128, ycol:ycol + ntok],
                  r=[("ybT", b, "c") if is_ctx else ("ybT", b, (tok0 // 1024) * 1024)], w=[ybl])
        w = wload(l, P_BG)
        for b in range(4):
            p = mm_fm(w, b, ntok)
            P.op("act", lambda e, p=p, b=b: e.activation(out=tmpb.ap[:, 0:ntok], in_=p.ap[:, 0:ntok], func=AF.Silu,
                                                          bias=bcol[:, P_BG * 4 + b:P_BG * 4 + b + 1], scale=1.0), r=[p, "bcol"], w=[tmpb])
            P.op("dve", lambda e, b=b: e.tensor_tensor(out=ybs[:, b, 0:ntok], in0=tmpb.ap[:, 0:ntok], in1=ybl.ap[:, b, 0:ntok], op=ALU.mult),
                 r=[tmpb, ybl], w=["ybs"])
        if stop == "p2b":
            raise _Stop()
        cosv, sinv = av(0, [128, 512]), av(1, [128, 512])
        P.dma("sp", cosv.ap[:, 0:ntok], cosT_in[:, tok0:tok0 + ntok], w=[cosv])
        P.dma("sp", sinv.ap[:, 0:ntok], sinT_in[:, tok0:tok0 + ntok], w=[sinv])
        qT = av(23, [128, 4, 512], BF16)
        ycT = av(18, [128, 4, 512])
        rope_proj(l, P_Q, P_QP, ntok, cosv, sinv, qT.ap, [qT], [av(2, [128, 512])], [av(3, [128, 512])])
        if stop == "p2q":
            raise _Stop()
        attention(l, g, tok0, ntok, is_ctx, qT, ycT)
        if stop == "p2c":
            raise _Stop()
        if "dbgy" in dbg and l == 0 and (is_ctx or g == 0):
            nm = "c" if is_ctx else "l"
            dya = nc.dram_tensor("dbg_ya_" + nm, [512, ntok], F32, kind="ExternalOutput").ap()
            dyc = nc.dram_tensor("dbg_yc_" + nm, [512, ntok], F32, kind="ExternalOutput").ap()
            dyb = nc.dram_tensor("dbg_yb_" + nm, [512, ntok], F32, kind="ExternalOutput").ap()
            for b in range(4):
                P.dma("sp", dya[b * 128:(b + 1) * 128, :], yaT[:, b, 0:ntok], r=["yaT"], w=[("dbg", nm, 0, b)])
                P.dma("sp", dyc[b * 128:(b + 1) * 128, :], ycT.ap[:, b, 0:ntok], r=[ycT], w=[("dbg", nm, 1, b)])
                P.dma("sp", dyb[b * 128:(b + 1) * 128, :], ybs[:, b, 0:ntok], r=["ybs"], w=[("dbg", nm, 2, b)])
        tmpc = av(22, [128, 512])
        w = wload(l, P_CG)
        for b in range(4):
            p = mm_fm(w, b, ntok)
            P.op("act", lambda e, p=p, b=b: e.activation(out=tmpc.ap[:, 0:ntok], in_=p.ap[:, 0:ntok], func=AF.Silu,
                                                          bias=bcol[:, P_CG * 4 + b:P_CG * 4 + b + 1], scale=1.0), r=[p, "bcol"], w=[tmpc])
            P.op("dve", lambda e, b=b: e.tensor_tensor(out=ycs[:, b, 0:ntok], in0=tmpc.ap[:, 0:ntok], in1=ycT.ap[:, b, 0:ntok], op=ALU.mult),
                 r=[tmpc, ycT], w=["ycs"])
        sgs = [av(0, [128, 512]), av(1, [128, 512])]
        tms = [av(2, [128, 512]), av(3, [128, 512])]
        macc = av(4, [128, 4, 512])
        ys = [(yaT, "yaT"), (ybs, "ybs"), (ycs, "ycs")]
        cnt = 0
        for hh_ in range(2):
            for n in range(3):
                pcg = P_GM + hh_ * 3 + n
                wg = wload(l, pcg)
                wbr = wload(l, P_WBR + hh_ * 3 + n, 2048)
                wbv = widev(wbr, 4)
                yt, yk = ys[n]
                for jj in range(4):
                    j = hh_ * 4 + jj
                    pg = mm_fm(wg, jj, ntok)
                    pb = ps()

                    def f(e, pb=pb, jj=jj, yt=yt, wbv=wbv):
                        for k in range(4):
                            ins = e.matmul(pb.ap[:, 0:ntok], lhsT=wbv[:, k, jj * 128:(jj + 1) * 128], rhs=yt[:, k, 0:ntok], start=(k == 0), stop=(k == 3))
                        return ins
                    P.op("pe", f, r=[wbr, yk], w=[pb])
                    sg = sgs[cnt % 2]
                    tm = tms[cnt % 2]
                    cnt += 1
                    P.op("act", lambda e, pg=pg, sg=sg, jj=jj, pcg=pcg: e.activation(
                        out=sg.ap[:, 0:ntok], in_=pg.ap[:, 0:ntok], func=AF.Sigmoid, bias=bcol[:, pcg * 4 + jj:pcg * 4 + jj + 1], scale=1.0),
                        r=[pg, "bcol"], w=[sg])
                    if n == 0:
                        P.op("dve", lambda e, sg=sg, pb=pb, jj=jj: e.tensor_tensor(out=macc.ap[:, jj, 0:ntok], in0=sg.ap[:, 0:ntok], in1=pb.ap[:, 0:ntok], op=ALU.mult),
                             r=[sg, pb], w=[macc])
                    else:
                        P.op("dve", lambda e, sg=sg, pb=pb, tm=tm: e.tensor_tensor(out=tm.ap[:, 0:ntok], in0=sg.ap[:, 0:ntok], in1=pb.ap[:, 0:ntok], op=ALU.mult),
                             r=[sg, pb], w=[tm])
                        if n == 1:
                            P.op("pool", lambda e, tm=tm, jj=jj: e.tensor_tensor(out=macc.ap[:, jj, 0:ntok], in0=macc.ap[:, jj, 0:ntok], in1=tm.ap[:, 0:ntok], op=ALU.add),
                                 r=[tm, macc], w=[macc])
                        else:
                            P.op("pool", lambda e, tm=tm, jj=jj, j=j: e.tensor_tensor(out=mT[:, j, 0:ntok], in0=macc.ap[:, jj, 0:ntok], in1=tm.ap[:, 0:ntok], op=ALU.add),
                                 r=[tm, macc], w=["mT"])
        if stop == "p2d":
            raise _Stop()
        xres = av(8, [128, 4, 1024])
        tmo = [av(16, [128, 512]), av(17, [128, 512])]
        xn = [av(18, [128, 1024]), av(20, [128, 1024])]
        for t in range(nt):
            P.dma("sp", xres.ap[:, t, :], src[t * 128:(t + 1) * 128, :], r=[("src", id(src.tensor), t)], w=[xres])
        cnt = 0
        for half in range(2):
            w = wload(l, P_WOUT + half)
            wv = widev(w)
            for t in range(nt):
                p = ps()

                def f(e, t=t, p=p, wv=wv):
                    for k in range(8):
                        ins = e.matmul(p.ap[:, 0:512], lhsT=mT[:, k, t * 128:(t + 1) * 128], rhs=wv[:, k, :], start=(k == 0), stop=(k == 7))
                    return ins
                P.op("pe", f, r=[w, "mT"], w=[p])
                tm = tmo[cnt % 2]
                cnt += 1
                P.op("dve", lambda e, p=p, tm=tm, half=half: e.tensor_tensor(out=tm.ap, in0=p.ap[:, 0:512], in1=gate_bc[:, who, half * 512:(half + 1) * 512], op=ALU.mult),
                     r=[p, "gate_bc"], w=[tm])
                P.op("dve", lambda e, tm=tm, t=t, half=half: e.scalar_tensor_tensor(
                    out=xres.ap[:, t, half * 512:(half + 1) * 512], in0=xres.ap[:, t, half * 512:(half + 1) * 512], scalar=float(ALPHA), in1=tm.ap,
                    op0=ALU.mult, op1=ALU.add), r=[tm, xres], w=[xres])
        for t in range(nt):
            xo = xn[t % 2]
            P.op("dve", lambda e, t=t: e.bn_stats(out=st6[:, 0:6], in_=xres.ap[:, t, 0:512]), r=[xres], w=["st6"])
            P.op("dve", lambda e, t=t: e.bn_stats(out=st6[:, 6:12], in_=xres.ap[:, t, 512:1024]), r=[xres, "st6"], w=["st6"])
            P.op("dve", lambda e: e.bn_aggr(out=mv2[:], in_=st6[:, 0:12]), r=["st6"], w=["mv2"])
            P.op("dve", lambda e: e.tensor_scalar(out=rs1[:], in0=mv2[:, 1:2], scalar1=LN_EPS, scalar2=None, op0=ALU.add), r=["mv2"], w=["rs1"])
            P.op("act", lambda e: e.activation(out=rs1[:], in_=rs1[:], func=AF.Sqrt), r=["rs1"], w=["rs1"])
            P.op("dve", lambda e: e.reciprocal(out=rs1[:], in_=rs1[:]), r=["rs1"], w=["rs1"])
            P.op("dve", lambda e, t=t, xo=xo: e.tensor_scalar(out=xo.ap, in0=xres.ap[:, t, :], scalar1=mv2[:, 0:1], scalar2=rs1[:, 0:1],
                                                              op0=ALU.subtract, op1=ALU.mult), r=[xres, "mv2", "rs1"], w=[xo])
            P.op("pool", lambda e, xo=xo: e.tensor_tensor(out=xo.ap, in0=xo.ap, in1=lng_bc[:], op=ALU.mult), r=[xo, "lng_bc"], w=[xo])
            P.op("pool", lambda e, xo=xo: e.tensor_tensor(out=xo.ap, in0=xo.ap, in1=lnb_bc[:], op=ALU.add), r=[xo, "lnb_bc"], w=[xo])
            P.dma("pool", dst[t * 128:(t + 1) * 128, :], xo.ap, r=[xo], w=[(dstkey, id(dst.tensor), t)])
        if stop == "p2e" or (stop == "p2g" and not is_ctx):
            raise _Stop()

    nl = len(layers)
    try:
      for li, l in enumerate(layers):
          last = (l == DEPTH - 1)
          with_ctx = not last
          xs = x_in if l == 0 else x1
          cs = ctx_in if l == 0 else xc1
          xd = out_d if last else x1
          layer_setup(l)
          if stop == "setup":
              break
          pass1_group(l, cs, SEQ, CTX, 1, 16)
          for g in range(16):
              pass1_group(l, xs[g * 512:(g + 1) * 512, :], g * 512, 512, 0, g)
          if stop == "pass1":
              break
          lru_pass(l, with_ctx)
          if stop == "lru":
              break
          P.dma("sp", kctx[:], KT.rearrange("(b p) t -> p b t", p=128)[:, :, SEQ:NTOK], r=[("KT", 16)], w=["kctx"])
          P.dma("sp", vctx[:], Vs[SEQ:NTOK, :].rearrange("(j p) c -> p j c", p=128), r=[("V", 16)], w=["vctx"])
          if with_ctx:
              pass2_group(l, cs, xc1, "src", SEQ, CTX, 1, True, 16)
          for g in range(16):
              pass2_group(l, xs[g * 512:(g + 1) * 512, :], xd[g * 512:(g + 1) * 512, :], "src", g * 512, 512, 0, False, g)
    except _Stop:
        pass
    P.finish()
    P.emit()
    return nc, P


_CACHE = {}


def kernel(**inputs):
    inp = {k: np.asarray(v) for k, v in inputs.items()}
    sh = prep_shared(inp)

    def fl(a, n):
        return np.ascontiguousarray(a.reshape(a.shape[:n] + (-1,)))

    shared = dict(wcat=sh["wcat"], wada=sh["wada"], bcol=sh["bcol"], rows=sh["rows"], wsT=fl(sh["wsT"], 2),
                  lruW=fl(sh["lruW"], 2), lcol=fl(sh["lcol"], 2), ccol=fl(sh["ccol"], 2), adac=sh["adac"],
                  btab=fl(sh["btab"], 2), cosT=sh["cosT"], sinT=sh["sinT"], ident=sh["ident"])
    in_maps = []
    for b in range(NB):
        m = prep_core(inp, b)
        m.update(shared)
        in_maps.append(m)
    if "nc" not in _CACHE:
        _CACHE["nc"] = build()[0]
    res = run_bass_kernel_spmd(_CACHE["nc"], in_maps, core_ids=list(range(NB)))
    return np.stack([np.asarray(r["out"], dtype=np.float32) for r in res.results], axis=0)
```

```python
import numpy as np
import concourse.bass as bass
import concourse.mybir as mybir
from concourse.bass_utils import run_bass_kernel_spmd

F32 = mybir.dt.float32
BF16 = mybir.dt.bfloat16
AF = mybir.ActivationFunctionType
ALU = mybir.AluOpType

D = 1024
SEQ = 8192
CTX = 256
NTOK = SEQ + CTX
DEPTH = 2
NB = 8
ALPHA = (2 * DEPTH) ** 0.25
LN_EPS = 1e-5
NEG = -30000.0
NPIECE = 25
PSZ = 4096

ENGS = ("pe", "act", "dve", "pool", "sp")


class V:
    def __init__(self, ap, keys):
        self.ap = ap
        self.keys = list(keys)


def _keys(items):
    out = []
    for it in items:
        if isinstance(it, V):
            out.extend(it.keys)
        else:
            out.append(it)
    return out


class Prog:
    def __init__(self, nc):
        self.nc = nc
        self.q = {e: [] for e in ENGS}
        self.esem = {e: nc.alloc_semaphore("s_" + e) for e in ("pe", "act", "dve", "pool")}
        self.ecnt = {e: 0 for e in self.esem}
        self.dsem = {}
        self.last_w = {}
        self.readers = {}
        self.waited = {e: {} for e in ENGS}
        self.n_ops = 0
        self.ps_i = 0

    def _deps(self, eng, reads, writes):
        toks = {}

        def add(t):
            if t is None:
                return
            s, v = t
            k = id(s)
            if k not in toks or toks[k][1] < v:
                toks[k] = (s, v)

        for r in reads:
            add(self.last_w.get(r))
        for w in writes:
            add(self.last_w.get(w))
            for t in self.readers.get(w, {}).values():
                add(t)
        out = []
        wd = self.waited[eng]
        for k, (s, v) in toks.items():
            if wd.get(k, 0) >= v:
                continue
            wd[k] = v
            out.append((s, v))
        return out

    def _commit(self, tok, reads, writes):
        for w in writes:
            self.last_w[w] = tok
            self.readers[w] = {}
        for r in reads:
            self.readers.setdefault(r, {})[id(tok[0])] = tok

    def op(self, eng, fn, r=(), w=()):
        reads, writes = _keys(r), _keys(w)
        waits = self._deps(eng, reads, writes)
        sem = self.esem[eng]
        self.ecnt[eng] += 1
        tok = (sem, self.ecnt[eng])

        def run(e, waits=waits, fn=fn, sem=sem):
            for s, v in waits:
                e.wait_ge(s, v)
            fn(e).then_inc(sem, 1)

        self.q[eng].append(run)
        self._commit(tok, reads, writes)
        self.n_ops += 1
        return tok

    def dma(self, eng, out, in_, r=(), w=(), semkey=None):
        reads, writes = _keys(r), _keys(w)
        if semkey is None:
            semkey = next(k for k in (writes + reads) if isinstance(k, str) or k[0] == "pg")
        if semkey not in self.dsem:
            self.dsem[semkey] = [self.nc.alloc_semaphore("d%d" % len(self.dsem)), 0]
        ent = self.dsem[semkey]
        waits = self._deps(eng, reads, writes)
        ent[1] += 16
        tok = (ent[0], ent[1])

        def run(e, waits=waits, sem=ent[0], out=out, in_=in_):
            for s, v in waits:
                e.wait_ge(s, v)
            e.dma_start(out=out, in_=in_).then_inc(sem, 16)

        self.q[eng].append(run)
        self._commit(tok, reads, writes)
        self.n_ops += 1
        return tok

    def finish(self, eng="sp"):
        ents = [tuple(v) for v in self.dsem.values()]
        ecs = [(self.esem[e], self.ecnt[e]) for e in self.esem if self.ecnt[e] > 0]

        def run(e):
            for sem, cnt in ents:
                e.wait_ge(sem, cnt)
            for sem, cnt in ecs:
                e.wait_ge(sem, cnt)

        self.q[eng].append(run)

    def emit(self):
        with self.nc.Block() as block:
            @block.tensor
            def _(e):
                for f in self.q["pe"]:
                    f(e)

            @block.scalar
            def _(e):
                for f in self.q["act"]:
                    f(e)

            @block.vector
            def _(e):
                for f in self.q["dve"]:
                    f(e)

            @block.gpsimd
            def _(e):
                for f in self.q["pool"]:
                    f(e)

            @block.sync
            def _(e):
                for f in self.q["sp"]:
                    f(e)


def _partner():
    d = np.arange(64)
    return np.where(d < 16, d + 16, np.where(d < 32, d - 16, np.where(d < 48, d + 16, d - 16)))


def _piece_cols():
    ar = np.arange(512)
    permcols = (np.arange(8)[:, None] * 64 + _partner()[None, :]).reshape(-1)
    pcs = [(1536 + ar, "fm"), (3072 + ar, "fm"), (3072 + permcols, "fm"), (3584 + ar, "wide"),
           (0 + ar, "fm"), (512 + ar, "wide"), (1024 + ar, "fm"), (2048 + ar, "fm"),
           (2560 + ar, "fm"), (2560 + permcols, "fm"), (4096 + ar, "fm")]
    for hh in range(2):
        for n in range(3):
            pcs.append((4608 + n * 1024 + hh * 512 + ar, "fm"))
    return pcs


P_BX, P_K, P_KP, P_V, P_AU, P_AV, P_AG, P_BG, P_Q, P_QP, P_CG = range(11)
P_GM = 11
P_WBR = 17
P_WOUT = 23


def _fm(Wc):
    return np.ascontiguousarray(Wc.reshape(8, 128, 4, 128).transpose(1, 2, 0, 3)).reshape(128, PSZ)


def _wide(Wc):
    return np.ascontiguousarray(Wc.reshape(8, 128, 512).transpose(1, 0, 2)).reshape(128, PSZ)


def _col(v, n):
    return np.ascontiguousarray(np.asarray(v).reshape(n, 128).T)


def prep_shared(inp):
    f32 = np.float32
    pcs = _piece_cols()
    sh = {}
    wcat = np.zeros((DEPTH, NPIECE, 128, PSZ), f32)
    wada = np.zeros((DEPTH, 6, 128, PSZ), f32)
    bcol = np.zeros((DEPTH, 128, 17 * 4), f32)
    rows = np.zeros((DEPTH, 9, 1024), f32)
    wsT = np.zeros((DEPTH, 128, 4, 128), f32)
    lruW = np.zeros((DEPTH, 128, 2, 4, 2, 128), f32)
    lcol = np.zeros((DEPTH, 128, 2, 4, 3), f32)
    ccol = np.zeros((DEPTH, 128, 4, 5), f32)
    adac = np.zeros((DEPTH, 128, 16), f32)
    btab = np.full((DEPTH, 128, 14, 512), NEG, f32)
    qc = np.arange(64)
    c0 = np.clip(qc - 8, 0, 48)
    kc = np.arange(64)
    valid = (kc[:, None] >= c0[None, :]) & (kc[:, None] < c0[None, :] + 16)
    bidx = np.clip(kc[:, None] - qc[None, :] + 15, 0, 30)
    for l in range(DEPTH):
        w_in, b_in = inp["w_in"][l], inp["b_in"][l]
        for i, (cols, kind) in enumerate(pcs):
            wcat[l, i] = _fm(w_in[:, cols]) if kind == "fm" else _wide(w_in[:, cols])
            if kind == "fm":
                bcol[l, :, i * 4:(i + 1) * 4] = _col(b_in[cols], 4)
        for hh in range(2):
            for n in range(3):
                wb = inp["w_br"][l][n][:, hh * 512:(hh + 1) * 512]
                wcat[l, P_WBR + hh * 3 + n, :, :2048] = np.ascontiguousarray(
                    wb.reshape(4, 128, 512).transpose(1, 0, 2)).reshape(128, 2048)
        for half in range(2):
            wcat[l, P_WOUT + half] = _wide(inp["w_out"][l][:, half * 512:(half + 1) * 512])
        wa = inp["w_ada"][l]
        for i in range(4):
            wada[l, i] = _fm(wa[:, i * 512:(i + 1) * 512])
        for i in range(2):
            wada[l, 4 + i] = _wide(wa[:, 2048 + i * 512:2048 + (i + 1) * 512])
        adac[l] = _col(inp["b_ada"][l][:2048], 16)
        rows[l, 0] = inp["b_ada"][l][2048:]
        rows[l, 1] = inp["ln_g"][l]
        rows[l, 2] = inp["ln_b"][l]
        rows[l, 3, :512] = b_in[512:1024]
        rows[l, 4, :512] = b_in[3584:4096]
        rows[l, 5, :512] = inp["sgu_ln_g"][l]
        rows[l, 6, :512] = inp["sgu_ln_b"][l]
        rows[l, 7, :512] = inp["b_s"][l].reshape(-1)
        wsT[l] = inp["w_s"][l].transpose(2, 0, 1)
        for d in range(2):
            for ct in range(4):
                for wi, nm in enumerate(("lru_wa", "lru_wx")):
                    for e in range(2):
                        lruW[l, e * 64:(e + 1) * 64, d, ct, wi, e * 64:(e + 1) * 64] = inp[nm][l][d][2 * ct + e]
            lcol[l, :, d, :, 0] = _col(inp["lru_ba"][l][d], 4)
            lcol[l, :, d, :, 1] = _col(inp["lru_bx"][l][d], 4)
            lcol[l, :, d, :, 2] = _col(inp["lru_lam"][l][d], 4)
        for j in range(4):
            ccol[l, :, :, j] = _col(inp["conv_w"][l][j], 4)
        ccol[l, :, :, 4] = _col(inp["conv_b"][l], 4)
        rpb = inp["rpb"][l]
        for m in range(14):
            for e in range(2):
                g = rpb[:, m + e][:, bidx]
                g = np.where(valid[None], g, f32(NEG))
                g = g.reshape(4, 2, 64, 64).transpose(2, 1, 0, 3)
                btab[l, e * 64:(e + 1) * 64, m] = g.reshape(64, 512)
    inv_freq = (f32(10000.0) ** (-np.arange(16, dtype=f32) / f32(16))).astype(f32)
    pos = np.arange(SEQ)
    rw, cl = (pos // 64).astype(f32), (pos % 64).astype(f32)
    cosT = np.ones((128, NTOK), f32)
    sinT = np.zeros((128, NTOK), f32)
    for d in range(64):
        p = rw if d < 32 else cl
        ang = (p * inv_freq[d % 16]).astype(f32)
        sgn = -1.0 if (d % 32) < 16 else 1.0
        for e in range(2):
            cosT[e * 64 + d, :SEQ] = np.cos(ang)
            sinT[e * 64 + d, :SEQ] = sgn * np.sin(ang)
    sh.update(wcat=wcat, wada=wada, bcol=bcol, rows=rows, wsT=wsT, lruW=lruW, lcol=lcol, ccol=ccol,
              adac=adac, btab=btab, cosT=cosT, sinT=sinT, ident=np.eye(128, dtype=f32))
    return sh


def prep_core(inp, b):
    cvec = np.zeros((128, 8, 2), np.float32)
    cvec[:, :, 0] = _col(inp["c"][b], 8)
    cvec[:, :, 1] = _col(inp["c_ctx"], 8)
    return {"x": np.ascontiguousarray(inp["x"][b]), "ctx": np.ascontiguousarray(inp["ctx"][b]), "cvec": cvec}


ARF = 22528
PG = 512


class _Stop(Exception):
    pass


BF = True
NWB = 3


def build(layers=(0, 1), dbg=(), stop=None):
    WDT = BF16 if BF else F32
    nc = bass.Bass("TRN2", target_bir_lowering=False)
    P = Prog(nc)

    def din(name, shape, dt=F32):
        return nc.dram_tensor(name, list(shape), dt, kind="ExternalInput").ap()

    def dscr(name, shape, dt, out=False):
        kind = "ExternalOutput" if (out or name in dbg) else "Internal"
        return nc.dram_tensor(name, list(shape), dt, kind=kind).ap()

    x_in = din("x", [SEQ, D]); ctx_in = din("ctx", [CTX, D]); cvec_in = din("cvec", [128, 8, 2])
    wcat = din("wcat", [DEPTH, NPIECE, 128, PSZ]); wada = din("wada", [DEPTH, 6, 128, PSZ])
    bcol_in = din("bcol", [DEPTH, 128, 68]); rows_in = din("rows", [DEPTH, 9, 1024])
    wsT_in = din("wsT", [DEPTH, 128, 512]); lruW_in = din("lruW", [DEPTH, 128, 2048])
    lcol_in = din("lcol", [DEPTH, 128, 24]); ccol_in = din("ccol", [DEPTH, 128, 20])
    adac_in = din("adac", [DEPTH, 128, 16]); btab_in = din("btab", [DEPTH, 128, 14 * 512])
    cosT_in = din("cosT", [128, NTOK]); sinT_in = din("sinT", [128, NTOK]); ident_in = din("ident", [128, 128])

    out_d = dscr("out", [SEQ, D], F32, out=True)
    wbf = dscr("wbf", [DEPTH, NPIECE, 128, PSZ], BF16) if BF else None
    bxT = dscr("bxT", [512, NTOK], F32)
    KT = dscr("KT", [512, NTOK], BF16)
    Vs = dscr("Vs", [NTOK, 512], BF16)
    hfs = dscr("hfs", [512, SEQ], F32)
    ybT = dscr("ybT", [512, NTOK], BF16)
    x1 = dscr("x1", [SEQ, D], F32)
    xc1 = dscr("xc1", [CTX, D], F32)

    def S(name, shape, dt=F32):
        return nc.alloc_sbuf_tensor("sb_" + name, list(shape), dt)

    AR = S("AR", [128, ARF])
    bcol = S("bcol", [128, 68]); adac = S("adac", [128, 16]); lcol = S("lcol", [128, 2, 4, 3]); ccol = S("ccol", [128, 4, 5])
    wsT = S("wsT", [128, 4, 128]); lruW = S("lruW", [128, 2, 4, 2, 128]); ident = S("ident", [128, 128])
    ones_bf = S("ones_bf", [128, 128], BF16)
    gate_bc = S("gate_bc", [128, 2, 1024]); lng_bc = S("lng_bc", [128, 1024]); lnb_bc = S("lnb_bc", [128, 1024])
    bav_bc = S("bav_bc", [128, 512]); bv_bc = S("bv_bc", [128, 512]); sg_bc = S("sg_bc", [128, 512])
    sb_bc = S("sb_bc", [128, 512]); bs_bc = S("bs_bc", [128, 512])
    btab = S("btab", [128, 14, 512], BF16)
    kctx = S("kctx", [128, 4, 256], BF16); vctx = S("vctx", [128, 2, 512], BF16)
    cv = S("cv", [128, 8, 2]); scv = S("scv", [128, 8, 2]); modc = S("modc", [128, 16, 2]); lcc = S("lcc", [128, 2, 4]); lcc2 = S("lcc2", [128, 2, 4])
    ltmp = S("ltmp", [128, 2, 4])
    uT = S("uT", [128, 8, 512], WDT)
    wb = [S("wb%d" % i, [128, PSZ], WDT) for i in range(NWB)]
    yaT = S("yaT", [128, 4, 512], WDT); ybs = S("ybs", [128, 4, 512], WDT); ycs = S("ycs", [128, 4, 512], WDT)
    mT = S("mT", [128, 8, 512], WDT)
    carry = S("carry", [128, 1]); hcf = S("hcf", [128, 256])
    st6 = S("st6", [128, 12]); mv2 = S("mv2", [128, 2]); rs1 = S("rs1", [128, 1])
    psum2 = [nc.alloc_psum_tensor("ps%d" % i, [128, 1024], F32) for i in range(4)]

    def ps():
        i = P.ps_i % 8
        P.ps_i += 1
        return V(psum2[i // 2][:, (i % 2) * 512:(i % 2 + 1) * 512], ["ps%d" % i])

    def psp():
        if P.ps_i % 2:
            P.ps_i += 1
        i = P.ps_i % 8
        P.ps_i += 2
        return V(psum2[i // 2][:, :], ["ps%d" % i, "ps%d" % (i + 1)])

    def av(off_pg, shape, dt=F32, off=0):
        n = int(np.prod(shape[1:]))
        nf = n if dt == F32 else n // 2
        o = off_pg * PG + off
        ap = AR[:, o:o + nf]
        if dt != F32:
            ap = ap.bitcast(BF16)
        if len(shape) == 3:
            ap = ap.rearrange("p (a b) -> p a b", a=shape[1])
        elif len(shape) == 4:
            ap = ap.rearrange("p (a b c) -> p a b c", a=shape[1], b=shape[2])
        v = V(ap, [("pg", i) for i in range(o // PG, (o + nf - 1) // PG + 1)])
        v.o = o
        v.nf = nf
        return v

    def rev_ap(v, n):
        return bass.AP(AR, v.o + n - 1, [[ARF, 128], [-1, n]])

    UTB = [V(uT, ["uT0"]), av(36, [128, 8, 512], WDT)]
    cur = {"ut": UTB[0]}
    wslot = [0]

    def wload(l, piece, nelem=PSZ):
        i = wslot[0] % NWB
        wslot[0] += 1
        if BF:
            P.dma("sp", wb[i][:, 0:nelem], wbf[l, piece, :, 0:nelem], r=[("wbf", l, piece)], w=["wb%d" % i])
        else:
            P.dma("sp", wb[i][:, 0:nelem], wcat[l, piece, :, 0:nelem], w=["wb%d" % i])
        return V(wb[i], ["wb%d" % i])

    def fmv(w):
        return w.ap[:, :].rearrange("p (b k c) -> p b k c", b=4, k=8)

    def widev(w, k=8):
        return w.ap[:, 0:k * 512].rearrange("p (k c) -> p k c", k=k)

    def mm_fm(w, b, ntok):
        p = ps()
        wv = fmv(w)
        ut = cur["ut"]

        def f(e):
            for k in range(8):
                ins = e.matmul(p.ap[:, 0:ntok], lhsT=wv[:, b, k, :], rhs=ut.ap[:, k, 0:ntok], start=(k == 0), stop=(k == 7))
            return ins
        P.op("pe", f, r=[w, ut], w=[p])
        return p

    def cast_weights(l):
        st32 = [av(0, [128, PSZ]), av(8, [128, PSZ])]
        st16 = [av(16, [128, PSZ], BF16), av(20, [128, PSZ], BF16)]
        for i in range(NPIECE):
            s_ = i % 2
            P.dma("sp", st32[s_].ap, wcat[l, i], w=[st32[s_]])
            P.op("dve", lambda e, s_=s_: e.tensor_copy(out=st16[s_].ap, in_=st32[s_].ap), r=[st32[s_]], w=[st16[s_]])
            P.dma("pool", wbf[l, i], st16[s_].ap, r=[st16[s_]], w=[("wbf", l, i)])

    if BF:
        for l in layers:
            cast_weights(l)
    P.dma("sp", ident[:], ident_in, w=["ident"])
    P.op("pool", lambda e: e.memset(ones_bf[:], 1.0), w=["ones_bf"])
    P.dma("sp", cv[:], cvec_in, w=["cv"])
    P.op("act", lambda e: e.activation(out=scv[:], in_=cv[:], func=AF.Silu), r=["cv"], w=["scv"])

    def load_uT(src, ntok, who, ub):
        xts = [av(24, [128, 1024]), av(26, [128, 1024])]
        for t in range(ntok // 128):
            xv = xts[t % 2]
            P.dma("sp", xv.ap, src[t * 128:(t + 1) * 128, :], r=[("src", id(src.tensor), t)], w=[xv])
            pa, pb = ps(), ps()

            def f(e, xv=xv, pa=pa, pb=pb):
                for k in range(8):
                    pp = pa if k < 4 else pb
                    ins = e.transpose(pp.ap[:, (k % 4) * 128:(k % 4 + 1) * 128], xv.ap[:, k * 128:(k + 1) * 128], ident[:])
                return ins
            P.op("pe", f, r=[xv, "ident"], w=[pa, pb])
            for k in range(8):
                pp = pa if k < 4 else pb
                P.op("act", lambda e, k=k, pp=pp, t=t: e.activation(
                    out=ub.ap[:, k, t * 128:(t + 1) * 128], in_=pp.ap[:, (k % 4) * 128:(k % 4 + 1) * 128],
                    func=AF.Identity, bias=modc[:, k, who:who + 1], scale=modc[:, 8 + k, who:who + 1]),
                    r=[pp, "modc"], w=[ub])

    def rope_proj(l, pc, pcp, ntok, cosv, sinv, dst_ap, dstkeys, t1s, t2s):
        w1 = wload(l, pc)
        w2 = wload(l, pcp)
        for b in range(4):
            p1 = mm_fm(w1, b, ntok)
            p2 = mm_fm(w2, b, ntok)
            t1, t2 = t1s[b % len(t1s)], t2s[b % len(t2s)]
            P.op("dve", lambda e, p1=p1, t1=t1, b=b: e.scalar_tensor_tensor(
                out=t1.ap[:, 0:ntok], in0=p1.ap[:, 0:ntok], scalar=bcol[:, pc * 4 + b:pc * 4 + b + 1], in1=cosv.ap[:, 0:ntok],
                op0=ALU.add, op1=ALU.mult), r=[p1, cosv, "bcol"], w=[t1])
            P.op("dve", lambda e, p2=p2, t2=t2, b=b: e.scalar_tensor_tensor(
                out=t2.ap[:, 0:ntok], in0=p2.ap[:, 0:ntok], scalar=bcol[:, pcp * 4 + b:pcp * 4 + b + 1], in1=sinv.ap[:, 0:ntok],
                op0=ALU.add, op1=ALU.mult), r=[p2, sinv, "bcol"], w=[t2])
            P.op("pool", lambda e, t1=t1, t2=t2, b=b: e.tensor_tensor(
                out=dst_ap[:, b, 0:ntok], in0=t1.ap[:, 0:ntok], in1=t2.ap[:, 0:ntok], op=ALU.add),
                r=[t1, t2], w=dstkeys)

    def layer_setup(l):
        P.dma("sp", bcol[:], bcol_in[l], w=["bcol"])
        P.dma("sp", adac[:], adac_in[l], w=["adac"])
        P.dma("sp", lcol[:].rearrange("p a b c -> p (a b c)"), lcol_in[l], w=["lcol"])
        P.dma("sp", ccol[:].rearrange("p a b -> p (a b)"), ccol_in[l], w=["ccol"])
        P.dma("sp", wsT[:].rearrange("p a b -> p (a b)"), wsT_in[l], w=["wsT"])
        P.dma("sp", lruW[:].rearrange("p a b c d -> p (a b c d)"), lruW_in[l], w=["lruW"])
        for hf_ in range(2):
            bt32 = av(hf_ * 7, [128, 7 * 512])
            P.dma("sp", bt32.ap, btab_in[l][:, hf_ * 3584:(hf_ + 1) * 3584], w=[bt32])
            P.op("pool", lambda e, hf_=hf_, bt32=bt32: e.tensor_copy(out=btab[:, hf_ * 7:(hf_ + 1) * 7, :].rearrange("p a b -> p (a b)"), in_=bt32.ap),
                 r=[bt32], w=["btab"])
        P.dma("sp", lng_bc[:], rows_in[l, 1, :].partition_broadcast(128), w=["lng_bc"])
        P.dma("sp", lnb_bc[:], rows_in[l, 2, :].partition_broadcast(128), w=["lnb_bc"])
        for i, (t, nm) in enumerate([(bav_bc, "bav_bc"), (bv_bc, "bv_bc"), (sg_bc, "sg_bc"), (sb_bc, "sb_bc"), (bs_bc, "bs_bc")]):
            P.dma("sp", t[:], rows_in[l, 3 + i, 0:512].partition_broadcast(128), w=[nm])
        badag = av(24, [128, 1024])
        P.dma("sp", badag.ap, rows_in[l, 0, :].partition_broadcast(128), w=[badag])
        P.op("act", lambda e: e.activation(out=ltmp[:], in_=lcol[:, :, :, 2], func=AF.Exp, scale=-1.0), r=["lcol"], w=["ltmp"])
        P.op("act", lambda e: e.activation(out=ltmp[:], in_=ltmp[:], func=AF.Ln, bias=1.0), r=["ltmp"], w=["ltmp"])
        P.op("dve", lambda e: e.tensor_scalar(out=lcc[:], in0=ltmp[:], scalar1=-8.0, scalar2=None, op0=ALU.mult), r=["ltmp"], w=["lcc"])
        P.op("dve", lambda e: e.tensor_scalar(out=lcc2[:], in0=ltmp[:], scalar1=-16.0, scalar2=None, op0=ALU.mult), r=["ltmp"], w=["lcc2"])
        screp = av(20, [128, 16, 128])
        P.op("dve", lambda e: e.tensor_copy(out=screp.ap, in_=scv[:].rearrange("p a b -> p (a b)").unsqueeze(2).broadcast_to([128, 16, 128])),
             r=["scv"], w=[screp])
        for i in range(4):
            wv = av((i % 2) * 8, [128, 4, 8, 128])
            P.dma("sp", wv.ap.rearrange("p a b c -> p (a b c)"), wada[l, i], w=[wv])
            for b in range(4):
                blk = i * 4 + b
                p = ps()

                def f(e, wv=wv, b=b, p=p):
                    for k in range(8):
                        ins = e.matmul(p.ap[:, 0:2], lhsT=wv.ap[:, b, k, :], rhs=scv[:, k, :], start=(k == 0), stop=(k == 7))
                    return ins
                P.op("pe", f, r=[wv, "scv"], w=[p])
                P.op("dve", lambda e, p=p, blk=blk: e.tensor_scalar(
                    out=modc[:, blk, :], in0=p.ap[:, 0:2], scalar1=adac[:, blk:blk + 1], scalar2=(1.0 if blk >= 8 else 0.0),
                    op0=ALU.add, op1=ALU.add), r=[p, "adac"], w=["modc"])
        for i in range(2):
            wv = av((i % 2) * 8, [128, 8, 512])
            P.dma("sp", wv.ap.rearrange("p a b -> p (a b)"), wada[l, 4 + i], w=[wv])
            for who in range(2):
                p = ps()

                def f(e, wv=wv, who=who, p=p):
                    for k in range(8):
                        ins = e.matmul(p.ap[:, 0:512], lhsT=screp.ap[:, k * 2 + who, :], rhs=wv.ap[:, k, :], start=(k == 0), stop=(k == 7))
                    return ins
                P.op("pe", f, r=[wv, screp], w=[p])
                P.op("dve", lambda e, p=p, who=who, i=i: e.tensor_tensor(
                    out=gate_bc[:, who, i * 512:(i + 1) * 512], in0=p.ap[:, 0:512], in1=badag.ap[:, i * 512:(i + 1) * 512], op=ALU.add),
                    r=[p, badag], w=["gate_bc"])

    def pass1_group(l, src, tok0, ntok, who, g, ub, hook):
        nt = ntok // 128
        cur["ut"] = ub
        if stop == "p1a":
            raise _Stop()
        cosv, sinv = av(0, [128, 512]), av(1, [128, 512])
        P.dma("sp", cosv.ap[:, 0:ntok], cosT_in[:, tok0:tok0 + ntok], w=[cosv])
        P.dma("sp", sinv.ap[:, 0:ntok], sinT_in[:, tok0:tok0 + ntok], w=[sinv])
        bxs = av(2, [128, 4, 512])
        w = wload(l, P_BX)
        for b in range(4):
            p = mm_fm(w, b, ntok)
            P.op("act", lambda e, p=p, b=b: e.activation(out=bxs.ap[:, b, 0:ntok], in_=p.ap[:, 0:ntok], func=AF.Identity,
                                                          bias=bcol[:, P_BX * 4 + b:P_BX * 4 + b + 1], scale=1.0),
                 r=[p, "bcol"], w=[bxs])
        if stop == "p1a2":
            raise _Stop()
        for b in range(4):
            P.dma("pool", bxT[b * 128:(b + 1) * 128, tok0:tok0 + ntok], bxs.ap[:, b, 0:ntok], r=[bxs], w=[("bxT", g)])
        if stop == "p1b":
            raise _Stop()
        hook()
        kTs = av(6, [128, 4, 512], BF16)
        t1s = [av(8, [128, 512]), av(9, [128, 512])]
        t2s = [av(10, [128, 512]), av(11, [128, 512])]
        rope_proj(l, P_K, P_KP, ntok, cosv, sinv, kTs.ap, [kTs], t1s, t2s)
        for b in range(4):
            P.dma("pool", KT[b * 128:(b + 1) * 128, tok0:tok0 + ntok], kTs.ap[:, b, 0:ntok], r=[kTs], w=[("KT", g)])
        if stop == "p1c":
            raise _Stop()
        vts = av(12, [128, 4, 512], BF16)
        w = wload(l, P_V)
        wv = widev(w)
        ut = cur["ut"]
        for t in range(nt):
            p = ps()

            def f(e, t=t, p=p):
                for k in range(8):
                    ins = e.matmul(p.ap[:, 0:512], lhsT=ut.ap[:, k, t * 128:(t + 1) * 128], rhs=wv[:, k, :], start=(k == 0), stop=(k == 7))
                return ins
            P.op("pe", f, r=[w, ut], w=[p])
            P.op("dve", lambda e, t=t, p=p: e.tensor_tensor(out=vts.ap[:, t, :], in0=p.ap[:, 0:512], in1=bv_bc[:], op=ALU.add),
                 r=[p, "bv_bc"], w=[vts])
        for t in range(nt):
            P.dma("pool", Vs[tok0 + t * 128:tok0 + (t + 1) * 128, :], vts.ap[:, t, :], r=[vts], w=[("V", g)])
        if stop == "p1d":
            raise _Stop()

    def lru_pass(l, with_ctx):
        NS = 1024
        sets = []
        for si in range(2):
            o = si * 18
            sets.append(dict(bxh=av(o, [128, 1536]), xc=av(o + 3, [128, NS]), rr=av(o + 5, [128, NS]), ii=av(o + 7, [128, NS]),
                             aa=av(o + 9, [128, NS]), bq=av(o + 11, [128, NS]), hh=av(o + 13, [128, NS]), hfl=av(o + 15, [128, NS]),
                             yo=av(o + 17, [128, NS], BF16)))
        allbx = [("bxT", g) for g in range(17)]

        def front(sg):
            B_ = sets[sg["i"] % 2]
            ct, d, colbase, seqlen, s0, n, is_ctx = sg["ct"], sg["d"], sg["colbase"], sg["seqlen"], sg["s0"], sg["n"], sg["is_ctx"]
            b_, xc, rr, ii, aa, bq = B_["bxh"], B_["xc"], B_["rr"], B_["ii"], B_["aa"], B_["bq"]
            lo, hi = max(s0 - 2, 0), min(s0 + n + 1, seqlen)
            if lo > s0 - 2 or hi < s0 + n + 1:
                P.op("pool", lambda e: e.memset(b_.ap[:, 0:n + 3], 0.0), w=[b_])
            P.dma("sp", b_.ap[:, lo - (s0 - 2):hi - (s0 - 2)], bxT[ct * 128:(ct + 1) * 128, colbase + lo:colbase + hi],
                  r=allbx, w=[b_])
            if d == 1 and not is_ctx:
                P.dma("sp", B_["hfl"].ap[:, 0:n], hfs[ct * 128:(ct + 1) * 128, s0:s0 + n], r=[("hfs", ct, s0)], w=[B_["hfl"]])
            P.op("dve", lambda e: e.tensor_scalar(out=xc.ap[:, 0:n], in0=b_.ap[:, 0:n], scalar1=ccol[:, ct, 0:1], scalar2=ccol[:, ct, 4:5],
                                                   op0=ALU.mult, op1=ALU.add), r=[b_, "ccol"], w=[xc])
            for j in range(1, 4):
                P.op("dve", lambda e, j=j: e.scalar_tensor_tensor(out=xc.ap[:, 0:n], in0=b_.ap[:, j:j + n], scalar=ccol[:, ct, j:j + 1],
                                                                   in1=xc.ap[:, 0:n], op0=ALU.mult, op1=ALU.add), r=[b_, xc, "ccol"], w=[xc])
            for c in range((n + 511) // 512):
                cn = min(512, n - c * 512)
                for wi, dst in ((0, rr), (1, ii)):
                    p = ps()
                    P.op("pe", lambda e, p=p, c=c, cn=cn, wi=wi: e.matmul(p.ap[:, 0:cn], lhsT=lruW[:, d, ct, wi, :], rhs=xc.ap[:, c * 512:c * 512 + cn],
                                                                           start=True, stop=True), r=[xc, "lruW"], w=[p])
                    P.op("act", lambda e, p=p, c=c, cn=cn, wi=wi, dst=dst: e.activation(
                        out=dst.ap[:, c * 512:c * 512 + cn], in_=p.ap[:, 0:cn], func=AF.Sigmoid, bias=lcol[:, d, ct, wi:wi + 1], scale=1.0),
                        r=[p, "lcol"], w=[dst])
            P.op("act", lambda e: e.activation(out=aa.ap[:, 0:n], in_=rr.ap[:, 0:n], func=AF.Exp, scale=lcc[:, d, ct:ct + 1]), r=[rr, "lcc"], w=[aa])
            P.op("act", lambda e: e.activation(out=bq.ap[:, 0:n], in_=rr.ap[:, 0:n], func=AF.Exp, scale=lcc2[:, d, ct:ct + 1]), r=[rr, "lcc2"], w=[bq])
            P.op("act", lambda e: e.activation(out=bq.ap[:, 0:n], in_=bq.ap[:, 0:n], func=AF.Sqrt, bias=1.0, scale=-1.0), r=[bq], w=[bq])
            P.op("pool", lambda e: e.tensor_tensor(out=ii.ap[:, 0:n], in0=ii.ap[:, 0:n], in1=xc.ap[:, 0:n], op=ALU.mult), r=[ii, xc], w=[ii])

        def back(sg):
            B_ = sets[sg["i"] % 2]
            ct, d, s0, n, is_ctx, first = sg["ct"], sg["d"], sg["s0"], sg["n"], sg["is_ctx"], sg["first"]
            ii, aa, bq, hh, yo, hf_ = B_["ii"], B_["aa"], B_["bq"], B_["hh"], B_["yo"], B_["hfl"]
            P.op("dve", lambda e: e.tensor_tensor(out=bq.ap[:, 0:n], in0=bq.ap[:, 0:n], in1=ii.ap[:, 0:n], op=ALU.mult), r=[bq, ii], w=[bq])
            init = 0.0 if first else carry[:, 0:1]
            if d == 0:
                P.op("dve", lambda e: e.tensor_tensor_scan(out=hh.ap[:, 0:n], data0=aa.ap[:, 0:n], data1=bq.ap[:, 0:n], initial=init,
                                                            op0=ALU.mult, op1=ALU.add), r=[aa, bq, "carry"], w=[hh])
                P.op("dve", lambda e: e.tensor_copy(out=carry[:], in_=hh.ap[:, n - 1:n]), r=[hh], w=["carry"])
            else:
                P.op("dve", lambda e: e.tensor_tensor_scan(out=rev_ap(hh, n), data0=rev_ap(aa, n), data1=rev_ap(bq, n), initial=init,
                                                            op0=ALU.mult, op1=ALU.add), r=[aa, bq, "carry"], w=[hh])
                P.op("dve", lambda e: e.tensor_copy(out=carry[:], in_=hh.ap[:, 0:1]), r=[hh], w=["carry"])
            rows = slice(ct * 128, (ct + 1) * 128)
            if d == 0:
                if is_ctx:
                    P.op("pool", lambda e: e.tensor_copy(out=hcf[:], in_=hh.ap[:, 0:n]), r=[hh], w=["hcf"])
                else:
                    P.dma("pool", hfs[rows, s0:s0 + n], hh.ap[:, 0:n], r=[hh], w=[("hfs", ct, s0)])
            else:
                if is_ctx:
                    if with_ctx:
                        P.op("pool", lambda e: e.tensor_tensor(out=yo.ap[:, 0:n], in0=hh.ap[:, 0:n], in1=hcf[:], op=ALU.add), r=[hh, "hcf"], w=[yo])
                        P.dma("pool", ybT[rows, SEQ:SEQ + n], yo.ap[:, 0:n], r=[yo], w=[("ybT", ct, "c")])
                else:
                    P.op("pool", lambda e: e.tensor_tensor(out=yo.ap[:, 0:n], in0=hh.ap[:, 0:n], in1=hf_.ap[:, 0:n], op=ALU.add), r=[hh, hf_], w=[yo])
                    P.dma("pool", ybT[rows, s0:s0 + n], yo.ap[:, 0:n], r=[yo], w=[("ybT", ct, s0)])

        segs = []
        for ct in range(4):
            for d in range(2):
                segs.append(dict(ct=ct, d=d, colbase=SEQ, seqlen=CTX, s0=0, n=CTX, first=True, is_ctx=True))
                order = range(0, SEQ, NS) if d == 0 else range(SEQ - NS, -1, -NS)
                for s0 in order:
                    segs.append(dict(ct=ct, d=d, colbase=0, seqlen=SEQ, s0=s0, n=NS, first=False, is_ctx=False))
        for i, sg in enumerate(segs):
            sg["i"] = i
        front(segs[0])
        for i in range(len(segs)):
            if i + 1 < len(segs):
                front(segs[i + 1])
            back(segs[i])

    def attention(l, g, tok0, ntok, is_ctx, qT, ycT):
        nrow = ntok // 64
        kwin = av(4, [128, 4, 1024], BF16)
        vwin = [av(8, [128, 4, 512], BF16), av(10, [128, 4, 512], BF16)]
        pTs = [av(12, [128, 6, 512], BF16), av(25, [128, 6, 512], BF16)]
        sps = [av(15, [128, 512]), av(16, [128, 512])]
        rz = av(17, [128, 512])
        if not is_ctx:
            t_lo, t_hi = max(0, tok0 - 256), min(SEQ, tok0 + 512 + 256)
            gs = [gg for gg in (g - 1, g, g + 1) if 0 <= gg < 16]
            for b in range(4):
                P.dma("sp", kwin.ap[:, b, 0:t_hi - t_lo], KT[b * 128:(b + 1) * 128, t_lo:t_hi],
                      r=[("KT", gg) for gg in gs], w=[kwin])
        state = {}

        def part1(rq):
            pT = pTs[rq % 2]
            koff = jv = 0
            vw = None
            tiles = []
            if not is_ctx:
                r = (tok0 // 64) + rq
                r0 = min(max(r - 4, 0), 120)
                jv = r0 - r + 7
                koff = r0 * 64 - t_lo
                vw = vwin[rq % 2]
                for jj_ in range(4):
                    P.dma("sp", vw.ap[:, jj_, :], Vs[r0 * 64 + jj_ * 128:r0 * 64 + (jj_ + 1) * 128, :],
                          r=[("V", gg) for gg in gs], w=[vw])
                tiles += [("loc", j) for j in range(4)]
            tiles += [("ctx", 0), ("ctx", 1)]
            ntile = len(tiles)
            for ti, (kind, j) in enumerate(tiles):
                p = psp()
                p3 = p.ap.rearrange("p (e y) -> p e y", e=2)[:, :, 0:256]

                def f(e, kind=kind, j=j, p=p, rq=rq, koff=(koff if not is_ctx else 0)):
                    for h in range(8):
                        c, ee = h // 2, h % 2
                        if kind == "loc":
                            lt = kwin.ap[64 * ee:64 * ee + 64, c, koff + j * 128:koff + (j + 1) * 128]
                        else:
                            lt = kctx[64 * ee:64 * ee + 64, c, j * 128:(j + 1) * 128]
                        ins = e.matmul(p.ap[:, ee * 512 + c * 64:ee * 512 + (c + 1) * 64], lhsT=lt,
                                       rhs=qT.ap[64 * ee:64 * ee + 64, c, rq * 64:(rq + 1) * 64], start=True, stop=True)
                    return ins
                P.op("pe", f, r=[kwin if kind == "loc" else "kctx", qT], w=[p])
                if kind == "loc":
                    sp_ = sps[ti % 2]
                    P.op("dve", lambda e, p3=p3, sp_=sp_, j=j, jv=jv: e.scalar_tensor_tensor(
                        out=sp_.ap.rearrange("p (e x) -> p e x", e=2), in0=p3, scalar=0.125,
                        in1=btab[:, 2 * j + jv, :].rearrange("p (e x) -> p e x", e=2), op0=ALU.mult, op1=ALU.add),
                        r=[p, "btab"], w=[sp_])
                    P.op("act", lambda e, sp_=sp_, ti=ti, pT=pT: e.activation(out=pT.ap[:, ti, :], in_=sp_.ap, func=AF.Exp),
                         r=[sp_], w=[pT])
                else:
                    P.op("act", lambda e, p3=p3, ti=ti, pT=pT: e.activation(out=pT.ap[:, ti, :].rearrange("p (e x) -> p e x", e=2), in_=p3,
                                                                              func=AF.Exp, scale=0.125), r=[p], w=[pT])
            state[rq] = (pT, tiles, vw)

        def part2(rq):
            pT, tiles, vw = state.pop(rq)
            ntile = len(tiles)
            po, pz = ps(), ps()

            def fo(e, tiles=tiles, po=po, pT=pT, vw=(vw if not is_ctx else None)):
                for h in range(8):
                    c, ee = h // 2, h % 2
                    for ti, (kind, j) in enumerate(tiles):
                        lt = vw.ap[:, j, h * 64:(h + 1) * 64] if kind == "loc" else vctx[:, j, h * 64:(h + 1) * 64]
                        ins = e.matmul(po.ap[64 * ee:64 * ee + 64, c * 64:(c + 1) * 64], lhsT=lt, rhs=pT.ap[:, ti, ee * 256 + c * 64:ee * 256 + (c + 1) * 64],
                                       start=(ti == 0), stop=(ti == len(tiles) - 1), tile_position=(0, 64 * ee))
                return ins
            P.op("pe", fo, r=[pT, "vctx"] + ([vw] if not is_ctx else []), w=[po])

            def fz(e, pz=pz, pT=pT, ntile=ntile):
                for ti in range(ntile):
                    ins = e.matmul(pz.ap[:, 0:512], lhsT=ones_bf[:], rhs=pT.ap[:, ti, :], start=(ti == 0), stop=(ti == ntile - 1))
                return ins
            P.op("pe", fz, r=[pT, "ones_bf"], w=[pz])

            P.op("dve", lambda e, pz=pz: e.reciprocal(out=rz.ap, in_=pz.ap[:, 0:512]), r=[pz], w=[rz])
            for ee in range(2):
                P.op("dve", lambda e, ee=ee, po=po, rq=rq: e.tensor_tensor(
                    out=ycT.ap[64 * ee:64 * ee + 64, :, rq * 64:(rq + 1) * 64],
                    in0=po.ap[64 * ee:64 * ee + 64, 0:256].rearrange("p (c q) -> p c q", c=4),
                    in1=rz.ap[64 * ee:64 * ee + 64, ee * 256:(ee + 1) * 256].rearrange("p (c q) -> p c q", c=4),
                    op=ALU.mult), r=[po, rz], w=[ycT])

        part1(0)
        for rq in range(nrow):
            if rq + 1 < nrow:
                part1(rq + 1)
            part2(rq)

    def pass2_group(l, src, dst, dstkey, tok0, ntok, who, is_ctx, g, ub, hook):
        nt = ntok // 128
        cur["ut"] = ub
        ut = ub
        gu = av(0, [128, 4, 512]); sgA = av(4, [128, 4, 512]); vn = av(8, [128, 4, 512])
        vpre = av(12, [128, 512]); gv = av(13, [128, 512]); ta = av(14, [128, 512])
        w = wload(l, P_AV)
        wv = widev(w)
        for t in range(nt):
            p = ps()

            def f(e, t=t, p=p, wv=wv):
                for k in range(8):
                    ins = e.matmul(p.ap[:, 0:512], lhsT=ut.ap[:, k, t * 128:(t + 1) * 128], rhs=wv[:, k, :], start=(k == 0), stop=(k == 7))
                return ins
            P.op("pe", f, r=[w, ut], w=[p])
            P.op("dve", lambda e, p=p: e.tensor_tensor(out=vpre.ap, in0=p.ap[:, 0:512], in1=bav_bc[:], op=ALU.add), r=[p, "bav_bc"], w=[vpre])
            P.op("act", lambda e: e.activation(out=gv.ap, in_=vpre.ap, func=AF.Gelu), r=[vpre], w=[gv])
            P.op("dve", lambda e: e.bn_stats(out=st6[:, 0:6], in_=gv.ap), r=[gv], w=["st6"])
            P.op("dve", lambda e: e.bn_aggr(out=mv2[:], in_=st6[:, 0:6]), r=["st6"], w=["mv2"])
            P.op("dve", lambda e: e.tensor_scalar(out=rs1[:], in0=mv2[:, 1:2], scalar1=LN_EPS, scalar2=None, op0=ALU.add), r=["mv2"], w=["rs1"])
            P.op("act", lambda e: e.activation(out=rs1[:], in_=rs1[:], func=AF.Sqrt), r=["rs1"], w=["rs1"])
            P.op("dve", lambda e: e.reciprocal(out=rs1[:], in_=rs1[:]), r=["rs1"], w=["rs1"])
            P.op("dve", lambda e, t=t: e.tensor_scalar(out=vn.ap[:, t, :], in0=gv.ap, scalar1=mv2[:, 0:1], scalar2=rs1[:, 0:1],
                                                       op0=ALU.subtract, op1=ALU.mult), r=[gv, "mv2", "rs1"], w=[vn])
            P.op("pool", lambda e, t=t: e.tensor_tensor(out=vn.ap[:, t, :], in0=vn.ap[:, t, :], in1=sg_bc[:], op=ALU.mult), r=[vn, "sg_bc"], w=[vn])
            P.op("pool", lambda e, t=t: e.tensor_tensor(out=vn.ap[:, t, :], in0=vn.ap[:, t, :], in1=sb_bc[:], op=ALU.add), r=[vn, "sb_bc"], w=[vn])
        for pc, dstv, fn in ((P_AU, gu, AF.Gelu), (P_AG, sgA, AF.Silu)):
            w = wload(l, pc)
            for b in range(4):
                p = mm_fm(w, b, ntok)
                P.op("act", lambda e, p=p, b=b, dstv=dstv, fn=fn, pc=pc: e.activation(
                    out=dstv.ap[:, b, 0:ntok], in_=p.ap[:, 0:ntok], func=fn, bias=bcol[:, pc * 4 + b:pc * 4 + b + 1], scale=1.0),
                    r=[p, "bcol"], w=[dstv])
        for g4 in range(4):
            p = ps()

            def f(e, g4=g4, p=p):
                for t in range(nt):
                    ins = e.matmul(p.ap[:, t * 128:(t + 1) * 128], lhsT=vn.ap[:, t, g4 * 128:(g4 + 1) * 128], rhs=wsT[:, g4, :], start=True, stop=True)
                return ins
            P.op("pe", f, r=[vn, "wsT"], w=[p])
            P.op("dve", lambda e, g4=g4, p=p: e.tensor_tensor(
                out=ta.ap[:, 0:ntok].rearrange("p (t q) -> p t q", t=nt), in0=p.ap[:, 0:ntok].rearrange("p (t q) -> p t q", t=nt),
                in1=bs_bc[:, g4 * 128:(g4 + 1) * 128].unsqueeze(1).broadcast_to([128, nt, 128]), op=ALU.add), r=[p, "bs_bc"], w=[ta])
            P.op("dve", lambda e, g4=g4: e.tensor_tensor(out=ta.ap[:, 0:ntok], in0=ta.ap[:, 0:ntok], in1=gu.ap[:, g4, 0:ntok], op=ALU.mult),
                 r=[ta, gu], w=[ta])
            P.op("pool", lambda e, g4=g4: e.tensor_tensor(out=yaT[:, g4, 0:ntok], in0=ta.ap[:, 0:ntok], in1=sgA.ap[:, g4, 0:ntok], op=ALU.mult),
                 r=[ta, sgA], w=["yaT"])
        if stop == "p2a":
            raise _Stop()
        ybl = av(16, [128, 4, 512], BF16)
        tmpb = av(12, [128, 512])
        ycol = SEQ if is_ctx else tok0
        for b in range(4):
            P.dma("sp", ybl.ap[:, b, 0:ntok], ybT[b * 128:(b + 1) * 128, ycol:ycol + ntok],
                  r=[("ybT", b, "c") if is_ctx else ("ybT", b, (tok0 // 1024) * 1024)], w=[ybl])
        w = wload(l, P_BG)
        for b in range(4):
            p = mm_fm(w, b, ntok)
            P.op("act", lambda e, p=p, b=b: e.activation(out=tmpb.ap[:, 0:ntok], in_=p.ap[:, 0:ntok], func=AF.Silu,
                                                          bias=bcol[:, P_BG * 4 + b:P_BG * 4 + b + 1], scale=1.0), r=[p, "bcol"], w=[tmpb])
            P.op("dve", lambda e, b=b: e.tensor_tensor(out=ybs[:, b, 0:ntok], in0=tmpb.ap[:, 0:ntok], in1=ybl.ap[:, b, 0:ntok], op=ALU.mult),
                 r=[tmpb, ybl], w=["ybs"])
        if stop == "p2b":
            raise _Stop()
        cosv, sinv = av(0, [128, 512]), av(1, [128, 512])
        P.dma("sp", cosv.ap[:, 0:ntok], cosT_in[:, tok0:tok0 + ntok], w=[cosv])
        P.dma("sp", sinv.ap[:, 0:ntok], sinT_in[:, tok0:tok0 + ntok], w=[sinv])
        qT = av(23, [128, 4, 512], BF16)
        ycT = av(18, [128, 4, 512])
        rope_proj(l, P_Q, P_QP, ntok, cosv, sinv, qT.ap, [qT], [av(2, [128, 512])], [av(3, [128, 512])])
        if stop == "p2q":
            raise _Stop()
        attention(l, g, tok0, ntok, is_ctx, qT, ycT)
        if stop == "p2c":
            raise _Stop()
        if "dbgy" in dbg and l == 0 and (is_ctx or g == 0):
            nm = "c" if is_ctx else "l"
            dya = nc.dram_tensor("dbg_ya_" + nm, [512, ntok], F32, kind="ExternalOutput").ap()
            dyc = nc.dram_tensor("dbg_yc_" + nm, [512, ntok], F32, kind="ExternalOutput").ap()
            dyb = nc.dram_tensor("dbg_yb_" + nm, [512, ntok], F32, kind="ExternalOutput").ap()
            for b in range(4):
                P.dma("sp", dya[b * 128:(b + 1) * 128, :], yaT[:, b, 0:ntok], r=["yaT"], w=[("dbg", nm, 0, b)])
                P.dma("sp", dyc[b * 128:(b + 1) * 128, :], ycT.ap[:, b, 0:ntok], r=[ycT], w=[("dbg", nm, 1, b)])
                P.dma("sp", dyb[b * 128:(b + 1) * 128, :], ybs[:, b, 0:ntok], r=["ybs"], w=[("dbg", nm, 2, b)])
        tmpc = av(22, [128, 512])
        w = wload(l, P_CG)
        for b in range(4):
            p = mm_fm(w, b, ntok)
            P.op("act", lambda e, p=p, b=b: e.activation(out=tmpc.ap[:, 0:ntok], in_=p.ap[:, 0:ntok], func=AF.Silu,
                                                          bias=bcol[:, P_CG * 4 + b:P_CG * 4 + b + 1], scale=1.0), r=[p, "bcol"], w=[tmpc])
            P.op("dve", lambda e, b=b: e.tensor_tensor(out=ycs[:, b, 0:ntok], in0=tmpc.ap[:, 0:ntok], in1=ycT.ap[:, b, 0:ntok], op=ALU.mult),
                 r=[tmpc, ycT], w=["ycs"])
        hook()
        cur["ut"] = ub
        sgs = [av(0, [128, 512]), av(1, [128, 512])]
        tms = [av(2, [128, 512]), av(3, [128, 512])]
        macc = av(4, [128, 4, 512])
        ys = [(yaT, "yaT"), (ybs, "ybs"), (ycs, "ycs")]
        cnt = 0
        for hh_ in range(2):
            for n in range(3):
                pcg = P_GM + hh_ * 3 + n
                wg = wload(l, pcg)
                wbr = wload(l, P_WBR + hh_ * 3 + n, 2048)
                wbv = widev(wbr, 4)
                yt, yk = ys[n]
                for jj in range(4):
                    j = hh_ * 4 + jj
                    pg = mm_fm(wg, jj, ntok)
                    pb = ps()

                    def f(e, pb=pb, jj=jj, yt=yt, wbv=wbv):
                        for k in range(4):
                            ins = e.matmul(pb.ap[:, 0:ntok], lhsT=wbv[:, k, jj * 128:(jj + 1) * 128], rhs=yt[:, k, 0:ntok], start=(k == 0), stop=(k == 3))
                        return ins
                    P.op("pe", f, r=[wbr, yk], w=[pb])
                    sg = sgs[cnt % 2]
                    tm = tms[cnt % 2]
                    cnt += 1
                    P.op("act", lambda e, pg=pg, sg=sg, jj=jj, pcg=pcg: e.activation(
                        out=sg.ap[:, 0:ntok], in_=pg.ap[:, 0:ntok], func=AF.Sigmoid, bias=bcol[:, pcg * 4 + jj:pcg * 4 + jj + 1], scale=1.0),
                        r=[pg, "bcol"], w=[sg])
                    if n == 0:
                        P.op("dve", lambda e, sg=sg, pb=pb, jj=jj: e.tensor_tensor(out=macc.ap[:, jj, 0:ntok], in0=sg.ap[:, 0:ntok], in1=pb.ap[:, 0:ntok], op=ALU.mult),
                             r=[sg, pb], w=[macc])
                    else:
                        P.op("dve", lambda e, sg=sg, pb=pb, tm=tm: e.tensor_tensor(out=tm.ap[:, 0:ntok], in0=sg.ap[:, 0:ntok], in1=pb.ap[:, 0:ntok], op=ALU.mult),
                             r=[sg, pb], w=[tm])
                        if n == 1:
                            P.op("pool", lambda e, tm=tm, jj=jj: e.tensor_tensor(out=macc.ap[:, jj, 0:ntok], in0=macc.ap[:, jj, 0:ntok], in1=tm.ap[:, 0:ntok], op=ALU.add),
                                 r=[tm, macc], w=[macc])
                        else:
                            P.op("pool", lambda e, tm=tm, jj=jj, j=j: e.tensor_tensor(out=mT[:, j, 0:ntok], in0=macc.ap[:, jj, 0:ntok], in1=tm.ap[:, 0:ntok], op=ALU.add),
                                 r=[tm, macc], w=["mT"])
        if stop == "p2d":
            raise _Stop()
        xres = av(8, [128, 4, 1024])
        tmo = [av(16, [128, 512]), av(17, [128, 512])]
        xn = [av(18, [128, 1024]), av(20, [128, 1024])]
        for t in range(nt):
            P.dma("sp", xres.ap[:, t, :], src[t * 128:(t + 1) * 128, :], r=[("src", id(src.tensor), t)], w=[xres])
        cnt = 0
        for half in range(2):
            w = wload(l, P_WOUT + half)
            wv = widev(w)
            for t in range(nt):
                p = ps()

                def f(e, t=t, p=p, wv=wv):
                    for k in range(8):
                        ins = e.matmul(p.ap[:, 0:512], lhsT=mT[:, k, t * 128:(t + 1) * 128], rhs=wv[:, k, :], start=(k == 0), stop=(k == 7))
                    return ins
                P.op("pe", f, r=[w, "mT"], w=[p])
                tm = tmo[cnt % 2]
                cnt += 1
                P.op("dve", lambda e, p=p, tm=tm, half=half: e.tensor_tensor(out=tm.ap, in0=p.ap[:, 0:512], in1=gate_bc[:, who, half * 512:(half + 1) * 512], op=ALU.mult),
                     r=[p, "gate_bc"], w=[tm])
                P.op("dve", lambda e, tm=tm, t=t, half=half: e.scalar_tensor_tensor(
                    out=xres.ap[:, t, half * 512:(half + 1) * 512], in0=xres.ap[:, t, half * 512:(half + 1) * 512], scalar=float(ALPHA), in1=tm.ap,
                    op0=ALU.mult, op1=ALU.add), r=[tm, xres], w=[xres])
        for t in range(nt):
            xo = xn[t % 2]
            P.op("dve", lambda e, t=t: e.bn_stats(out=st6[:, 0:6], in_=xres.ap[:, t, 0:512]), r=[xres], w=["st6"])
            P.op("dve", lambda e, t=t: e.bn_stats(out=st6[:, 6:12], in_=xres.ap[:, t, 512:1024]), r=[xres, "st6"], w=["st6"])
            P.op("dve", lambda e: e.bn_aggr(out=mv2[:], in_=st6[:, 0:12]), r=["st6"], w=["mv2"])
            P.op("dve", lambda e: e.tensor_scalar(out=rs1[:], in0=mv2[:, 1:2], scalar1=LN_EPS, scalar2=None, op0=ALU.add), r=["mv2"], w=["rs1"])
            P.op("act", lambda e: e.activation(out=rs1[:], in_=rs1[:], func=AF.Sqrt), r=["rs1"], w=["rs1"])
            P.op("dve", lambda e: e.reciprocal(out=rs1[:], in_=rs1[:]), r=["rs1"], w=["rs1"])
            P.op("dve", lambda e, t=t, xo=xo: e.tensor_scalar(out=xo.ap, in0=xres.ap[:, t, :], scalar1=mv2[:, 0:1], scalar2=rs1[:, 0:1],
                                                              op0=ALU.subtract, op1=ALU.mult), r=[xres, "mv2", "rs1"], w=[xo])
            P.op("pool", lambda e, xo=xo: e.tensor_tensor(out=xo.ap, in0=xo.ap, in1=lng_bc[:], op=ALU.mult), r=[xo, "lng_bc"], w=[xo])
            P.op("pool", lambda e, xo=xo: e.tensor_tensor(out=xo.ap, in0=xo.ap, in1=lnb_bc[:], op=ALU.add), r=[xo, "lnb_bc"], w=[xo])
            P.dma("pool", dst[t * 128:(t + 1) * 128, :], xo.ap, r=[xo], w=[(dstkey, id(dst.tensor), t)])
        if stop == "p2e" or (stop == "p2g" and not is_ctx):
            raise _Stop()

    nl = len(layers)
    try:
      for li, l in enumerate(layers):
          last = (l == DEPTH - 1)
          with_ctx = not last
          xs = x_in if l == 0 else x1
          cs = ctx_in if l == 0 else xc1
          xd = out_d if last else x1
          layer_setup(l)
          if stop == "setup":
              break
          g1 = [(cs, SEQ, CTX, 1, 16)] + [(xs[g * 512:(g + 1) * 512, :], g * 512, 512, 0, g) for g in range(16)]
          load_uT(g1[0][0], g1[0][2], g1[0][3], UTB[0])
          for i, (src_, tok0_, ntok_, who_, g_) in enumerate(g1):
              def hook1(i=i):
                  if i + 1 < len(g1):
                      load_uT(g1[i + 1][0], g1[i + 1][2], g1[i + 1][3], UTB[(i + 1) % 2])
              pass1_group(l, src_, tok0_, ntok_, who_, g_, UTB[i % 2], hook1)
          if stop == "pass1":
              break
          lru_pass(l, with_ctx)
          if stop == "lru":
              break
          P.dma("sp", kctx[:], KT.rearrange("(b p) t -> p b t", p=128)[:, :, SEQ:NTOK], r=[("KT", 16)], w=["kctx"])
          P.dma("sp", vctx[:], Vs[SEQ:NTOK, :].rearrange("(j p) c -> p j c", p=128), r=[("V", 16)], w=["vctx"])
          g2 = ([(cs, xc1, SEQ, CTX, 1, True, 16)] if with_ctx else []) + \
               [(xs[g * 512:(g + 1) * 512, :], xd[g * 512:(g + 1) * 512, :], g * 512, 512, 0, False, g) for g in range(16)]
          load_uT(g2[0][0], g2[0][3], g2[0][4], UTB[0])
          for i, (src_, dst_, tok0_, ntok_, who_, isc_, g_) in enumerate(g2):
              def hook2(i=i):
                  if i + 1 < len(g2):
                      load_uT(g2[i + 1][0], g2[i + 1][3], g2[i + 1][4], UTB[(i + 1) % 2])
              pass2_group(l, src_, dst_, "src", tok0_, ntok_, who_, isc_, g_, UTB[i % 2], hook2)
    except _Stop:
        pass
    P.finish()
    P.emit()
    return nc, P


_CACHE = {}


def kernel(**inputs):
    inp = {k: np.asarray(v) for k, v in inputs.items()}
    sh = prep_shared(inp)

    def fl(a, n):
        return np.ascontiguousarray(a.reshape(a.shape[:n] + (-1,)))

    shared = dict(wcat=sh["wcat"], wada=sh["wada"], bcol=sh["bcol"], rows=sh["rows"], wsT=fl(sh["wsT"], 2),
                  lruW=fl(sh["lruW"], 2), lcol=fl(sh["lcol"], 2), ccol=fl(sh["ccol"], 2), adac=sh["adac"],
                  btab=fl(sh["btab"], 2), cosT=sh["cosT"], sinT=sh["sinT"], ident=sh["ident"])
    in_maps = []
    for b in range(NB):
        m = prep_core(inp, b)
        m.update(shared)
        in_maps.append(m)
    if "nc" not in _CACHE:
        _CACHE["nc"] = build()[0]
    res = run_bass_kernel_spmd(_CACHE["nc"], in_maps, core_ids=list(range(NB)))
    return np.stack([np.asarray(r["out"], dtype=np.float32) for r in res.results], axis=0)
```

```python
import numpy as np
import concourse.bass as bass
import concourse.mybir as mybir
from concourse.bass_utils import run_bass_kernel_spmd

F32 = mybir.dt.float32
BF16 = mybir.dt.bfloat16
AF = mybir.ActivationFunctionType
ALU = mybir.AluOpType

D = 1024
SEQ = 8192
CTX = 256
NTOK = SEQ + CTX
DEPTH = 2
NB = 8
ALPHA = (2 * DEPTH) ** 0.25
LN_EPS = 1e-5
NEG = -30000.0
NPIECE = 25
PSZ = 4096

ENGS = ("pe", "act", "dve", "pool", "sp")


class V:
    def __init__(self, ap, keys):
        self.ap = ap
        self.keys = list(keys)


def _keys(items):
    out = []
    for it in items:
        if isinstance(it, V):
            out.extend(it.keys)
        else:
            out.append(it)
    return out


class Prog:
    def __init__(self, nc):
        self.nc = nc
        self.q = {e: [] for e in ENGS}
        self.esem = {e: nc.alloc_semaphore("s_" + e) for e in ("pe", "act", "dve", "pool")}
        self.ecnt = {e: 0 for e in self.esem}
        self.dsem = {}
        self.last_w = {}
        self.readers = {}
        self.waited = {e: {} for e in ENGS}
        self.n_ops = 0
        self.ps_i = 0

    def _deps(self, eng, reads, writes):
        toks = {}

        def add(t):
            if t is None:
                return
            s, v = t
            k = id(s)
            if k not in toks or toks[k][1] < v:
                toks[k] = (s, v)

        for r in reads:
            add(self.last_w.get(r))
        for w in writes:
            add(self.last_w.get(w))
            for t in self.readers.get(w, {}).values():
                add(t)
        out = []
        wd = self.waited[eng]
        for k, (s, v) in toks.items():
            if wd.get(k, 0) >= v:
                continue
            wd[k] = v
            out.append((s, v))
        return out

    def _commit(self, tok, reads, writes):
        for w in writes:
            self.last_w[w] = tok
            self.readers[w] = {}
        for r in reads:
            self.readers.setdefault(r, {})[id(tok[0])] = tok

    def op(self, eng, fn, r=(), w=()):
        reads, writes = _keys(r), _keys(w)
        waits = self._deps(eng, reads, writes)
        sem = self.esem[eng]
        self.ecnt[eng] += 1
        tok = (sem, self.ecnt[eng])

        def run(e, waits=waits, fn=fn, sem=sem):
            for s, v in waits:
                e.wait_ge(s, v)
            fn(e).then_inc(sem, 1)

        self.q[eng].append(run)
        self._commit(tok, reads, writes)
        self.n_ops += 1
        return tok

    def dma(self, eng, out, in_, r=(), w=(), semkey=None):
        reads, writes = _keys(r), _keys(w)
        if semkey is None:
            semkey = next(k for k in (writes + reads) if isinstance(k, str) or k[0] == "pg")
        if semkey not in self.dsem:
            self.dsem[semkey] = [self.nc.alloc_semaphore("d%d" % len(self.dsem)), 0]
        ent = self.dsem[semkey]
        waits = self._deps(eng, reads, writes)
        ent[1] += 16
        tok = (ent[0], ent[1])

        def run(e, waits=waits, sem=ent[0], out=out, in_=in_):
            for s, v in waits:
                e.wait_ge(s, v)
            e.dma_start(out=out, in_=in_).then_inc(sem, 16)

        self.q[eng].append(run)
        self._commit(tok, reads, writes)
        self.n_ops += 1
        return tok

    def finish(self, eng="sp"):
        ents = [tuple(v) for v in self.dsem.values()]
        ecs = [(self.esem[e], self.ecnt[e]) for e in self.esem if self.ecnt[e] > 0]

        def run(e):
            for sem, cnt in ents:
                e.wait_ge(sem, cnt)
            for sem, cnt in ecs:
                e.wait_ge(sem, cnt)

        self.q[eng].append(run)

    def emit(self):
        with self.nc.Block() as block:
            @block.tensor
            def _(e):
                for f in self.q["pe"]:
                    f(e)

            @block.scalar
            def _(e):
                for f in self.q["act"]:
                    f(e)

            @block.vector
            def _(e):
                for f in self.q["dve"]:
                    f(e)

            @block.gpsimd
            def _(e):
                for f in self.q["pool"]:
                    f(e)

            @block.sync
            def _(e):
                for f in self.q["sp"]:
                    f(e)


def _partner():
    d = np.arange(64)
    return np.where(d < 16, d + 16, np.where(d < 32, d - 16, np.where(d < 48, d + 16, d - 16)))


def _piece_cols():
    ar = np.arange(512)
    permcols = (np.arange(8)[:, None] * 64 + _partner()[None, :]).reshape(-1)
    pcs = [(1536 + ar, "fm"), (3072 + ar, "fm"), (3072 + permcols, "fm"), (3584 + ar, "wide"),
           (0 + ar, "fm"), (512 + ar, "wide"), (1024 + ar, "fm"), (2048 + ar, "fm"),
           (2560 + ar, "fm"), (2560 + permcols, "fm"), (4096 + ar, "fm")]
    for hh in range(2):
        for n in range(3):
            pcs.append((4608 + n * 1024 + hh * 512 + ar, "fm"))
    return pcs


P_BX, P_K, P_KP, P_V, P_AU, P_AV, P_AG, P_BG, P_Q, P_QP, P_CG = range(11)
P_GM = 11
P_WBR = 17
P_WOUT = 23


def _fm(Wc):
    return np.ascontiguousarray(Wc.reshape(8, 128, 4, 128).transpose(1, 2, 0, 3)).reshape(128, PSZ)


def _wide(Wc):
    return np.ascontiguousarray(Wc.reshape(8, 128, 512).transpose(1, 0, 2)).reshape(128, PSZ)


def _col(v, n):
    return np.ascontiguousarray(np.asarray(v).reshape(n, 128).T)


def prep_shared(inp):
    f32 = np.float32
    pcs = _piece_cols()
    sh = {}
    wcat = np.zeros((DEPTH, NPIECE, 128, PSZ), f32)
    wada = np.zeros((DEPTH, 6, 128, PSZ), f32)
    bcol = np.zeros((DEPTH, 128, 17 * 4), f32)
    rows = np.zeros((DEPTH, 9, 1024), f32)
    wsT = np.zeros((DEPTH, 128, 4, 128), f32)
    lruW = np.zeros((DEPTH, 128, 2, 4, 2, 128), f32)
    lcol = np.zeros((DEPTH, 128, 2, 4, 3), f32)
    ccol = np.zeros((DEPTH, 128, 4, 5), f32)
    adac = np.zeros((DEPTH, 128, 16), f32)
    btab = np.full((DEPTH, 128, 14, 512), NEG, f32)
    qc = np.arange(64)
    c0 = np.clip(qc - 8, 0, 48)
    kc = np.arange(64)
    valid = (kc[:, None] >= c0[None, :]) & (kc[:, None] < c0[None, :] + 16)
    bidx = np.clip(kc[:, None] - qc[None, :] + 15, 0, 30)
    for l in range(DEPTH):
        w_in, b_in = inp["w_in"][l], inp["b_in"][l]
        for i, (cols, kind) in enumerate(pcs):
            wcat[l, i] = _fm(w_in[:, cols]) if kind == "fm" else _wide(w_in[:, cols])
            if kind == "fm":
                bcol[l, :, i * 4:(i + 1) * 4] = _col(b_in[cols], 4)
        for hh in range(2):
            for n in range(3):
                wb = inp["w_br"][l][n][:, hh * 512:(hh + 1) * 512]
                wcat[l, P_WBR + hh * 3 + n, :, :2048] = np.ascontiguousarray(
                    wb.reshape(4, 128, 512).transpose(1, 0, 2)).reshape(128, 2048)
        for half in range(2):
            wcat[l, P_WOUT + half] = _wide(inp["w_out"][l][:, half * 512:(half + 1) * 512])
        wa = inp["w_ada"][l]
        for i in range(4):
            wada[l, i] = _fm(wa[:, i * 512:(i + 1) * 512])
        for i in range(2):
            wada[l, 4 + i] = _wide(wa[:, 2048 + i * 512:2048 + (i + 1) * 512])
        adac[l] = _col(inp["b_ada"][l][:2048], 16)
        rows[l, 0] = inp["b_ada"][l][2048:]
        rows[l, 1] = inp["ln_g"][l]
        rows[l, 2] = inp["ln_b"][l]
        rows[l, 3, :512] = b_in[512:1024]
        rows[l, 4, :512] = b_in[3584:4096]
        rows[l, 5, :512] = inp["sgu_ln_g"][l]
        rows[l, 6, :512] = inp["sgu_ln_b"][l]
        rows[l, 7, :512] = inp["b_s"][l].reshape(-1)
        wsT[l] = inp["w_s"][l].transpose(2, 0, 1)
        for d in range(2):
            for ct in range(4):
                for wi, nm in enumerate(("lru_wa", "lru_wx")):
                    for e in range(2):
                        lruW[l, e * 64:(e + 1) * 64, d, ct, wi, e * 64:(e + 1) * 64] = inp[nm][l][d][2 * ct + e]
            lcol[l, :, d, :, 0] = _col(inp["lru_ba"][l][d], 4)
            lcol[l, :, d, :, 1] = _col(inp["lru_bx"][l][d], 4)
            lcol[l, :, d, :, 2] = _col(inp["lru_lam"][l][d], 4)
        for j in range(4):
            ccol[l, :, :, j] = _col(inp["conv_w"][l][j], 4)
        ccol[l, :, :, 4] = _col(inp["conv_b"][l], 4)
        rpb = inp["rpb"][l]
        for m in range(14):
            for e in range(2):
                g = rpb[:, m + e][:, bidx]
                g = np.where(valid[None], g, f32(NEG))
                g = g.reshape(4, 2, 64, 64).transpose(2, 1, 0, 3)
                btab[l, e * 64:(e + 1) * 64, m] = g.reshape(64, 512)
    inv_freq = (f32(10000.0) ** (-np.arange(16, dtype=f32) / f32(16))).astype(f32)
    pos = np.arange(SEQ)
    rw, cl = (pos // 64).astype(f32), (pos % 64).astype(f32)
    cosT = np.ones((128, NTOK), f32)
    sinT = np.zeros((128, NTOK), f32)
    for d in range(64):
        p = rw if d < 32 else cl
        ang = (p * inv_freq[d % 16]).astype(f32)
        sgn = -1.0 if (d % 32) < 16 else 1.0
        for e in range(2):
            cosT[e * 64 + d, :SEQ] = np.cos(ang)
            sinT[e * 64 + d, :SEQ] = sgn * np.sin(ang)
    sh.update(wcat=wcat, wada=wada, bcol=bcol, rows=rows, wsT=wsT, lruW=lruW, lcol=lcol, ccol=ccol,
              adac=adac, btab=btab, cosT=cosT, sinT=sinT, ident=np.eye(128, dtype=f32))
    return sh


def prep_core(inp, b):
    cvec = np.zeros((128, 8, 2), np.float32)
    cvec[:, :, 0] = _col(inp["c"][b], 8)
    cvec[:, :, 1] = _col(inp["c_ctx"], 8)
    return {"x": np.ascontiguousarray(inp["x"][b]), "ctx": np.ascontiguousarray(inp["ctx"][b]), "cvec": cvec}


ARF = 22528
PG = 512


class _Stop(Exception):
    pass


BF = True
NWB = 3


def build(layers=(0, 1), dbg=(), stop=None):
    WDT = BF16 if BF else F32
    nc = bass.Bass("TRN2", target_bir_lowering=False)
    P = Prog(nc)

    def din(name, shape, dt=F32):
        return nc.dram_tensor(name, list(shape), dt, kind="ExternalInput").ap()

    def dscr(name, shape, dt, out=False):
        kind = "ExternalOutput" if (out or name in dbg) else "Internal"
        return nc.dram_tensor(name, list(shape), dt, kind=kind).ap()

    x_in = din("x", [SEQ, D]); ctx_in = din("ctx", [CTX, D]); cvec_in = din("cvec", [128, 8, 2])
    wcat = din("wcat", [DEPTH, NPIECE, 128, PSZ]); wada = din("wada", [DEPTH, 6, 128, PSZ])
    bcol_in = din("bcol", [DEPTH, 128, 68]); rows_in = din("rows", [DEPTH, 9, 1024])
    wsT_in = din("wsT", [DEPTH, 128, 512]); lruW_in = din("lruW", [DEPTH, 128, 2048])
    lcol_in = din("lcol", [DEPTH, 128, 24]); ccol_in = din("ccol", [DEPTH, 128, 20])
    adac_in = din("adac", [DEPTH, 128, 16]); btab_in = din("btab", [DEPTH, 128, 14 * 512])
    cosT_in = din("cosT", [128, NTOK]); sinT_in = din("sinT", [128, NTOK]); ident_in = din("ident", [128, 128])

    out_d = dscr("out", [SEQ, D], F32, out=True)
    wbf = dscr("wbf", [DEPTH, NPIECE, 128, PSZ], BF16) if BF else None
    bxT = dscr("bxT", [512, NTOK], F32)
    KT = dscr("KT", [512, NTOK], BF16)
    Vs = dscr("Vs", [NTOK, 512], BF16)
    hfs = dscr("hfs", [512, SEQ], F32)
    ybT = dscr("ybT", [512, NTOK], BF16)
    x1 = dscr("x1", [SEQ, D], F32)
    xc1 = dscr("xc1", [CTX, D], F32)

    def S(name, shape, dt=F32):
        return nc.alloc_sbuf_tensor("sb_" + name, list(shape), dt)

    AR = S("AR", [128, ARF])
    bcol = S("bcol", [128, 68]); adac = S("adac", [128, 16]); lcol = S("lcol", [128, 2, 4, 3]); ccol = S("ccol", [128, 4, 5])
    wsT = S("wsT", [128, 4, 128]); lruW = S("lruW", [128, 2, 4, 2, 128]); ident = S("ident", [128, 128])
    ones_bf = S("ones_bf", [128, 128], BF16)
    gate_bc = S("gate_bc", [128, 2, 1024]); lng_bc = S("lng_bc", [128, 1024]); lnb_bc = S("lnb_bc", [128, 1024])
    bav_bc = S("bav_bc", [128, 512]); bv_bc = S("bv_bc", [128, 512]); sg_bc = S("sg_bc", [128, 512])
    sb_bc = S("sb_bc", [128, 512]); bs_bc = S("bs_bc", [128, 512])
    btab = S("btab", [128, 14, 512], BF16)
    kctx = S("kctx", [128, 4, 256], BF16); vctx = S("vctx", [128, 2, 512], BF16)
    cv = S("cv", [128, 8, 2]); scv = S("scv", [128, 8, 2]); modc = S("modc", [128, 16, 2]); lcc = S("lcc", [128, 2, 4]); lcc2 = S("lcc2", [128, 2, 4])
    ltmp = S("ltmp", [128, 2, 4])
    uT = S("uT", [128, 8, 512], WDT)
    wb = [S("wb%d" % i, [128, PSZ], WDT) for i in range(NWB)]
    yaT = S("yaT", [128, 4, 512], WDT); ybs = S("ybs", [128, 4, 512], WDT); ycs = S("ycs", [128, 4, 512], WDT)
    mT = S("mT", [128, 8, 512], WDT)
    carry = S("carry", [128, 1]); hcf = S("hcf", [128, 256])
    st6 = S("st6", [128, 12]); mv2 = S("mv2", [128, 2]); rs1 = S("rs1", [128, 1])
    psum2 = [nc.alloc_psum_tensor("ps%d" % i, [128, 1024], F32) for i in range(4)]

    def ps():
        i = P.ps_i % 8
        P.ps_i += 1
        return V(psum2[i // 2][:, (i % 2) * 512:(i % 2 + 1) * 512], ["ps%d" % i])

    def psp():
        if P.ps_i % 2:
            P.ps_i += 1
        i = P.ps_i % 8
        P.ps_i += 2
        return V(psum2[i // 2][:, :], ["ps%d" % i, "ps%d" % (i + 1)])

    def av(off_pg, shape, dt=F32, off=0):
        n = int(np.prod(shape[1:]))
        nf = n if dt == F32 else n // 2
        o = off_pg * PG + off
        ap = AR[:, o:o + nf]
        if dt != F32:
            ap = ap.bitcast(BF16)
        if len(shape) == 3:
            ap = ap.rearrange("p (a b) -> p a b", a=shape[1])
        elif len(shape) == 4:
            ap = ap.rearrange("p (a b c) -> p a b c", a=shape[1], b=shape[2])
        v = V(ap, [("pg", i) for i in range(o // PG, (o + nf - 1) // PG + 1)])
        v.o = o
        v.nf = nf
        return v

    def rev_ap(v, n):
        return bass.AP(AR, v.o + n - 1, [[ARF, 128], [-1, n]])

    UTB = [V(uT, ["uT0"]), av(36, [128, 8, 512], WDT)]
    cur = {"ut": UTB[0]}
    wslot = [0]

    def wload(l, piece, nelem=PSZ):
        i = wslot[0] % NWB
        wslot[0] += 1
        if BF:
            P.dma("sp", wb[i][:, 0:nelem], wbf[l, piece, :, 0:nelem], r=[("wbf", l, piece)], w=["wb%d" % i])
        else:
            P.dma("sp", wb[i][:, 0:nelem], wcat[l, piece, :, 0:nelem], w=["wb%d" % i])
        return V(wb[i], ["wb%d" % i])

    def fmv(w):
        return w.ap[:, :].rearrange("p (b k c) -> p b k c", b=4, k=8)

    def widev(w, k=8):
        return w.ap[:, 0:k * 512].rearrange("p (k c) -> p k c", k=k)

    def mm_fm(w, b, ntok):
        p = ps()
        wv = fmv(w)
        ut = cur["ut"]

        def f(e):
            for k in range(8):
                ins = e.matmul(p.ap[:, 0:ntok], lhsT=wv[:, b, k, :], rhs=ut.ap[:, k, 0:ntok], start=(k == 0), stop=(k == 7))
            return ins
        P.op("pe", f, r=[w, ut], w=[p])
        return p

    def cast_weights(l):
        st32 = [av(0, [128, PSZ]), av(8, [128, PSZ])]
        st16 = [av(16, [128, PSZ], BF16), av(20, [128, PSZ], BF16)]
        for i in range(NPIECE):
            s_ = i % 2
            P.dma("sp", st32[s_].ap, wcat[l, i], w=[st32[s_]])
            P.op("dve", lambda e, s_=s_: e.tensor_copy(out=st16[s_].ap, in_=st32[s_].ap), r=[st32[s_]], w=[st16[s_]])
            P.dma("pool", wbf[l, i], st16[s_].ap, r=[st16[s_]], w=[("wbf", l, i)])

    if BF:
        for l in layers:
            cast_weights(l)
    P.dma("sp", ident[:], ident_in, w=["ident"])
    P.op("pool", lambda e: e.memset(ones_bf[:], 1.0), w=["ones_bf"])
    P.dma("sp", cv[:], cvec_in, w=["cv"])
    P.op("act", lambda e: e.activation(out=scv[:], in_=cv[:], func=AF.Silu), r=["cv"], w=["scv"])

    def load_uT(src, ntok, who, ub):
        xts = [av(24, [128, 1024]), av(26, [128, 1024])]
        for t in range(ntok // 128):
            xv = xts[t % 2]
            P.dma("sp", xv.ap, src[t * 128:(t + 1) * 128, :], r=[("src", id(src.tensor), t)], w=[xv])
            pa, pb = ps(), ps()

            def f(e, xv=xv, pa=pa, pb=pb):
                for k in range(8):
                    pp = pa if k < 4 else pb
                    ins = e.transpose(pp.ap[:, (k % 4) * 128:(k % 4 + 1) * 128], xv.ap[:, k * 128:(k + 1) * 128], ident[:])
                return ins
            P.op("pe", f, r=[xv, "ident"], w=[pa, pb])
            for k in range(8):
                pp = pa if k < 4 else pb
                P.op("act", lambda e, k=k, pp=pp, t=t: e.activation(
                    out=ub.ap[:, k, t * 128:(t + 1) * 128], in_=pp.ap[:, (k % 4) * 128:(k % 4 + 1) * 128],
                    func=AF.Identity, bias=modc[:, k, who:who + 1], scale=modc[:, 8 + k, who:who + 1]),
                    r=[pp, "modc"], w=[ub])

    def rope_proj(l, pc, pcp, ntok, cosv, sinv, dst_ap, dstkeys, t1s, t2s):
        w1 = wload(l, pc)
        w2 = wload(l, pcp)
        for b in range(4):
            p1 = mm_fm(w1, b, ntok)
            p2 = mm_fm(w2, b, ntok)
            t1, t2 = t1s[b % len(t1s)], t2s[b % len(t2s)]
            P.op("dve", lambda e, p1=p1, t1=t1, b=b: e.scalar_tensor_tensor(
                out=t1.ap[:, 0:ntok], in0=p1.ap[:, 0:ntok], scalar=bcol[:, pc * 4 + b:pc * 4 + b + 1], in1=cosv.ap[:, 0:ntok],
                op0=ALU.add, op1=ALU.mult), r=[p1, cosv, "bcol"], w=[t1])
            P.op("dve", lambda e, p2=p2, t2=t2, b=b: e.scalar_tensor_tensor(
                out=t2.ap[:, 0:ntok], in0=p2.ap[:, 0:ntok], scalar=bcol[:, pcp * 4 + b:pcp * 4 + b + 1], in1=sinv.ap[:, 0:ntok],
                op0=ALU.add, op1=ALU.mult), r=[p2, sinv, "bcol"], w=[t2])
            P.op("pool", lambda e, t1=t1, t2=t2, b=b: e.tensor_tensor(
                out=dst_ap[:, b, 0:ntok], in0=t1.ap[:, 0:ntok], in1=t2.ap[:, 0:ntok], op=ALU.add),
                r=[t1, t2], w=dstkeys)

    def layer_setup(l):
        P.dma("sp", bcol[:], bcol_in[l], w=["bcol"])
        P.dma("sp", adac[:], adac_in[l], w=["adac"])
        P.dma("sp", lcol[:].rearrange("p a b c -> p (a b c)"), lcol_in[l], w=["lcol"])
        P.dma("sp", ccol[:].rearrange("p a b -> p (a b)"), ccol_in[l], w=["ccol"])
        P.dma("sp", wsT[:].rearrange("p a b -> p (a b)"), wsT_in[l], w=["wsT"])
        P.dma("sp", lruW[:].rearrange("p a b c d -> p (a b c d)"), lruW_in[l], w=["lruW"])
        for hf_ in range(2):
            bt32 = av(hf_ * 7, [128, 7 * 512])
            P.dma("sp", bt32.ap, btab_in[l][:, hf_ * 3584:(hf_ + 1) * 3584], w=[bt32])
            P.op("pool", lambda e, hf_=hf_, bt32=bt32: e.tensor_copy(out=btab[:, hf_ * 7:(hf_ + 1) * 7, :].rearrange("p a b -> p (a b)"), in_=bt32.ap),
                 r=[bt32], w=["btab"])
        P.dma("sp", lng_bc[:], rows_in[l, 1, :].partition_broadcast(128), w=["lng_bc"])
        P.dma("sp", lnb_bc[:], rows_in[l, 2, :].partition_broadcast(128), w=["lnb_bc"])
        for i, (t, nm) in enumerate([(bav_bc, "bav_bc"), (bv_bc, "bv_bc"), (sg_bc, "sg_bc"), (sb_bc, "sb_bc"), (bs_bc, "bs_bc")]):
            P.dma("sp", t[:], rows_in[l, 3 + i, 0:512].partition_broadcast(128), w=[nm])
        badag = av(24, [128, 1024])
        P.dma("sp", badag.ap, rows_in[l, 0, :].partition_broadcast(128), w=[badag])
        P.op("act", lambda e: e.activation(out=ltmp[:], in_=lcol[:, :, :, 2], func=AF.Exp, scale=-1.0), r=["lcol"], w=["ltmp"])
        P.op("act", lambda e: e.activation(out=ltmp[:], in_=ltmp[:], func=AF.Ln, bias=1.0), r=["ltmp"], w=["ltmp"])
        P.op("dve", lambda e: e.tensor_scalar(out=lcc[:], in0=ltmp[:], scalar1=-8.0, scalar2=None, op0=ALU.mult), r=["ltmp"], w=["lcc"])
        P.op("dve", lambda e: e.tensor_scalar(out=lcc2[:], in0=ltmp[:], scalar1=-16.0, scalar2=None, op0=ALU.mult), r=["ltmp"], w=["lcc2"])
        screp = av(20, [128, 16, 128])
        P.op("dve", lambda e: e.tensor_copy(out=screp.ap, in_=scv[:].rearrange("p a b -> p (a b)").unsqueeze(2).broadcast_to([128, 16, 128])),
             r=["scv"], w=[screp])
        for i in range(4):
            wv = av((i % 2) * 8, [128, 4, 8, 128])
            P.dma("sp", wv.ap.rearrange("p a b c -> p (a b c)"), wada[l, i], w=[wv])
            for b in range(4):
                blk = i * 4 + b
                p = ps()

                def f(e, wv=wv, b=b, p=p):
                    for k in range(8):
                        ins = e.matmul(p.ap[:, 0:2], lhsT=wv.ap[:, b, k, :], rhs=scv[:, k, :], start=(k == 0), stop=(k == 7))
                    return ins
                P.op("pe", f, r=[wv, "scv"], w=[p])
                P.op("dve", lambda e, p=p, blk=blk: e.tensor_scalar(
                    out=modc[:, blk, :], in0=p.ap[:, 0:2], scalar1=adac[:, blk:blk + 1], scalar2=(1.0 if blk >= 8 else 0.0),
                    op0=ALU.add, op1=ALU.add), r=[p, "adac"], w=["modc"])
        for i in range(2):
            wv = av((i % 2) * 8, [128, 8, 512])
            P.dma("sp", wv.ap.rearrange("p a b -> p (a b)"), wada[l, 4 + i], w=[wv])
            for who in range(2):
                p = ps()

                def f(e, wv=wv, who=who, p=p):
                    for k in range(8):
                        ins = e.matmul(p.ap[:, 0:512], lhsT=screp.ap[:, k * 2 + who, :], rhs=wv.ap[:, k, :], start=(k == 0), stop=(k == 7))
                    return ins
                P.op("pe", f, r=[wv, screp], w=[p])
                P.op("dve", lambda e, p=p, who=who, i=i: e.tensor_tensor(
                    out=gate_bc[:, who, i * 512:(i + 1) * 512], in0=p.ap[:, 0:512], in1=badag.ap[:, i * 512:(i + 1) * 512], op=ALU.add),
                    r=[p, badag], w=["gate_bc"])

    def pass1_group(l, src, tok0, ntok, who, g, ub, hook):
        nt = ntok // 128
        cur["ut"] = ub
        if stop == "p1a":
            raise _Stop()
        cosv, sinv = av(0, [128, 512]), av(1, [128, 512])
        P.dma("sp", cosv.ap[:, 0:ntok], cosT_in[:, tok0:tok0 + ntok], w=[cosv])
        P.dma("sp", sinv.ap[:, 0:ntok], sinT_in[:, tok0:tok0 + ntok], w=[sinv])
        bxs = av(2, [128, 4, 512])
        w = wload(l, P_BX)
        for b in range(4):
            p = mm_fm(w, b, ntok)
            P.op("act", lambda e, p=p, b=b: e.activation(out=bxs.ap[:, b, 0:ntok], in_=p.ap[:, 0:ntok], func=AF.Identity,
                                                          bias=bcol[:, P_BX * 4 + b:P_BX * 4 + b + 1], scale=1.0),
                 r=[p, "bcol"], w=[bxs])
        if stop == "p1a2":
            raise _Stop()
        for b in range(4):
            P.dma("pool", bxT[b * 128:(b + 1) * 128, tok0:tok0 + ntok], bxs.ap[:, b, 0:ntok], r=[bxs], w=[("bxT", g)])
        if stop == "p1b":
            raise _Stop()
        hook()
        kTs = av(6, [128, 4, 512], BF16)
        t1s = [av(8, [128, 512]), av(9, [128, 512])]
        t2s = [av(10, [128, 512]), av(11, [128, 512])]
        rope_proj(l, P_K, P_KP, ntok, cosv, sinv, kTs.ap, [kTs], t1s, t2s)
        for b in range(4):
            P.dma("pool", KT[b * 128:(b + 1) * 128, tok0:tok0 + ntok], kTs.ap[:, b, 0:ntok], r=[kTs], w=[("KT", g)])
        if stop == "p1c":
            raise _Stop()
        vts = av(12, [128, 4, 512], BF16)
        w = wload(l, P_V)
        wv = widev(w)
        ut = cur["ut"]
        for t in range(nt):
            p = ps()

            def f(e, t=t, p=p):
                for k in range(8):
                    ins = e.matmul(p.ap[:, 0:512], lhsT=ut.ap[:, k, t * 128:(t + 1) * 128], rhs=wv[:, k, :], start=(k == 0), stop=(k == 7))
                return ins
            P.op("pe", f, r=[w, ut], w=[p])
            P.op("dve", lambda e, t=t, p=p: e.tensor_tensor(out=vts.ap[:, t, :], in0=p.ap[:, 0:512], in1=bv_bc[:], op=ALU.add),
                 r=[p, "bv_bc"], w=[vts])
        for t in range(nt):
            P.dma("pool", Vs[tok0 + t * 128:tok0 + (t + 1) * 128, :], vts.ap[:, t, :], r=[vts], w=[("V", g)])
        if stop == "p1d":
            raise _Stop()

    def lru_pass(l, with_ctx):
        NS = 1024
        sets = []
        for si in range(2):
            o = si * 18
            sets.append(dict(bxh=av(o, [128, 1536]), xc=av(o + 3, [128, NS]), rr=av(o + 5, [128, NS]), ii=av(o + 7, [128, NS]),
                             aa=av(o + 9, [128, NS]), bq=av(o + 11, [128, NS]), hh=av(o + 13, [128, NS]), hfl=av(o + 15, [128, NS]),
                             yo=av(o + 17, [128, NS], BF16)))
        allbx = [("bxT", g) for g in range(17)]

        def front(sg):
            B_ = sets[sg["i"] % 2]
            ct, d, colbase, seqlen, s0, n, is_ctx = sg["ct"], sg["d"], sg["colbase"], sg["seqlen"], sg["s0"], sg["n"], sg["is_ctx"]
            b_, xc, rr, ii, aa, bq = B_["bxh"], B_["xc"], B_["rr"], B_["ii"], B_["aa"], B_["bq"]
            lo, hi = max(s0 - 2, 0), min(s0 + n + 1, seqlen)
            if lo > s0 - 2 or hi < s0 + n + 1:
                P.op("pool", lambda e: e.memset(b_.ap[:, 0:n + 3], 0.0), w=[b_])
            P.dma("sp", b_.ap[:, lo - (s0 - 2):hi - (s0 - 2)], bxT[ct * 128:(ct + 1) * 128, colbase + lo:colbase + hi],
                  r=allbx, w=[b_])
            if d == 1 and not is_ctx:
                P.dma("sp", B_["hfl"].ap[:, 0:n], hfs[ct * 128:(ct + 1) * 128, s0:s0 + n], r=[("hfs", ct, s0)], w=[B_["hfl"]])
            P.op("dve", lambda e: e.tensor_scalar(out=xc.ap[:, 0:n], in0=b_.ap[:, 0:n], scalar1=ccol[:, ct, 0:1], scalar2=ccol[:, ct, 4:5],
                                                   op0=ALU.mult, op1=ALU.add), r=[b_, "ccol"], w=[xc])
            for j in range(1, 4):
                P.op("dve", lambda e, j=j: e.scalar_tensor_tensor(out=xc.ap[:, 0:n], in0=b_.ap[:, j:j + n], scalar=ccol[:, ct, j:j + 1],
                                                                   in1=xc.ap[:, 0:n], op0=ALU.mult, op1=ALU.add), r=[b_, xc, "ccol"], w=[xc])
            for c in range((n + 511) // 512):
                cn = min(512, n - c * 512)
                for wi, dst in ((0, rr), (1, ii)):
                    p = ps()
                    P.op("pe", lambda e, p=p, c=c, cn=cn, wi=wi: e.matmul(p.ap[:, 0:cn], lhsT=lruW[:, d, ct, wi, :], rhs=xc.ap[:, c * 512:c * 512 + cn],
                                                                           start=True, stop=True), r=[xc, "lruW"], w=[p])
                    P.op("act", lambda e, p=p, c=c, cn=cn, wi=wi, dst=dst: e.activation(
                        out=dst.ap[:, c * 512:c * 512 + cn], in_=p.ap[:, 0:cn], func=AF.Sigmoid, bias=lcol[:, d, ct, wi:wi + 1], scale=1.0),
                        r=[p, "lcol"], w=[dst])
            P.op("act", lambda e: e.activation(out=aa.ap[:, 0:n], in_=rr.ap[:, 0:n], func=AF.Exp, scale=lcc[:, d, ct:ct + 1]), r=[rr, "lcc"], w=[aa])
            P.op("act", lambda e: e.activation(out=bq.ap[:, 0:n], in_=rr.ap[:, 0:n], func=AF.Exp, scale=lcc2[:, d, ct:ct + 1]), r=[rr, "lcc2"], w=[bq])
            P.op("act", lambda e: e.activation(out=bq.ap[:, 0:n], in_=bq.ap[:, 0:n], func=AF.Sqrt, bias=1.0, scale=-1.0), r=[bq], w=[bq])
            P.op("pool", lambda e: e.tensor_tensor(out=ii.ap[:, 0:n], in0=ii.ap[:, 0:n], in1=xc.ap[:, 0:n], op=ALU.mult), r=[ii, xc], w=[ii])

        def back(sg):
            B_ = sets[sg["i"] % 2]
            ct, d, s0, n, is_ctx, first = sg["ct"], sg["d"], sg["s0"], sg["n"], sg["is_ctx"], sg["first"]
            ii, aa, bq, hh, yo, hf_ = B_["ii"], B_["aa"], B_["bq"], B_["hh"], B_["yo"], B_["hfl"]
            P.op("dve", lambda e: e.tensor_tensor(out=bq.ap[:, 0:n], in0=bq.ap[:, 0:n], in1=ii.ap[:, 0:n], op=ALU.mult), r=[bq, ii], w=[bq])
            init = 0.0 if first else carry[:, 0:1]
            if d == 0:
                P.op("dve", lambda e: e.tensor_tensor_scan(out=hh.ap[:, 0:n], data0=aa.ap[:, 0:n], data1=bq.ap[:, 0:n], initial=init,
                                                            op0=ALU.mult, op1=ALU.add), r=[aa, bq, "carry"], w=[hh])
                P.op("dve", lambda e: e.tensor_copy(out=carry[:], in_=hh.ap[:, n - 1:n]), r=[hh], w=["carry"])
            else:
                P.op("dve", lambda e: e.tensor_tensor_scan(out=rev_ap(hh, n), data0=rev_ap(aa, n), data1=rev_ap(bq, n), initial=init,
                                                            op0=ALU.mult, op1=ALU.add), r=[aa, bq, "carry"], w=[hh])
                P.op("dve", lambda e: e.tensor_copy(out=carry[:], in_=hh.ap[:, 0:1]), r=[hh], w=["carry"])
            rows = slice(ct * 128, (ct + 1) * 128)
            if d == 0:
                if is_ctx:
                    P.op("pool", lambda e: e.tensor_copy(out=hcf[:], in_=hh.ap[:, 0:n]), r=[hh], w=["hcf"])
                else:
                    P.dma("pool", hfs[rows, s0:s0 + n], hh.ap[:, 0:n], r=[hh], w=[("hfs", ct, s0)])
            else:
                if is_ctx:
                    if with_ctx:
                        P.op("pool", lambda e: e.tensor_tensor(out=yo.ap[:, 0:n], in0=hh.ap[:, 0:n], in1=hcf[:], op=ALU.add), r=[hh, "hcf"], w=[yo])
                        P.dma("pool", ybT[rows, SEQ:SEQ + n], yo.ap[:, 0:n], r=[yo], w=[("ybT", ct, "c")])
                else:
                    P.op("pool", lambda e: e.tensor_tensor(out=yo.ap[:, 0:n], in0=hh.ap[:, 0:n], in1=hf_.ap[:, 0:n], op=ALU.add), r=[hh, hf_], w=[yo])
                    P.dma("pool", ybT[rows, s0:s0 + n], yo.ap[:, 0:n], r=[yo], w=[("ybT", ct, s0)])

        segs = []
        for ct in range(4):
            for d in range(2):
                segs.append(dict(ct=ct, d=d, colbase=SEQ, seqlen=CTX, s0=0, n=CTX, first=True, is_ctx=True))
                order = range(0, SEQ, NS) if d == 0 else range(SEQ - NS, -1, -NS)
                for s0 in order:
                    segs.append(dict(ct=ct, d=d, colbase=0, seqlen=SEQ, s0=s0, n=NS, first=False, is_ctx=False))
        for i, sg in enumerate(segs):
            sg["i"] = i
        front(segs[0])
        for i in range(len(segs)):
            if i + 1 < len(segs):
                front(segs[i + 1])
            back(segs[i])

    def attention(l, g, tok0, ntok, is_ctx, qT, ycT):
        nrow = ntok // 64
        kwin = av(4, [128, 4, 1024], BF16)
        vwin = [av(8, [128, 4, 512], BF16), av(10, [128, 4, 512], BF16)]
        pTs = [av(12, [128, 6, 512], BF16), av(25, [128, 6, 512], BF16)]
        sps = [av(15, [128, 512]), av(16, [128, 512])]
        rz = av(17, [128, 512])
        if not is_ctx:
            t_lo, t_hi = max(0, tok0 - 256), min(SEQ, tok0 + 512 + 256)
            gs = [gg for gg in (g - 1, g, g + 1) if 0 <= gg < 16]
            for b in range(4):
                P.dma("sp", kwin.ap[:, b, 0:t_hi - t_lo], KT[b * 128:(b + 1) * 128, t_lo:t_hi],
                      r=[("KT", gg) for gg in gs], w=[kwin])
        state = {}

        def part1(rq):
            pT = pTs[rq % 2]
            koff = jv = 0
            vw = None
            tiles = []
            if not is_ctx:
                r = (tok0 // 64) + rq
                r0 = min(max(r - 4, 0), 120)
                jv = r0 - r + 7
                koff = r0 * 64 - t_lo
                vw = vwin[rq % 2]
                for jj_ in range(4):
                    P.dma("sp", vw.ap[:, jj_, :], Vs[r0 * 64 + jj_ * 128:r0 * 64 + (jj_ + 1) * 128, :],
                          r=[("V", gg) for gg in gs], w=[vw])
                tiles += [("loc", j) for j in range(4)]
            tiles += [("ctx", 0), ("ctx", 1)]
            ntile = len(tiles)
            for ti, (kind, j) in enumerate(tiles):
                p = psp()
                p3 = p.ap.rearrange("p (e y) -> p e y", e=2)[:, :, 0:256]

                def f(e, kind=kind, j=j, p=p, rq=rq, koff=(koff if not is_ctx else 0)):
                    for h in range(8):
                        c, ee = h // 2, h % 2
                        if kind == "loc":
                            lt = kwin.ap[64 * ee:64 * ee + 64, c, koff + j * 128:koff + (j + 1) * 128]
                        else:
                            lt = kctx[64 * ee:64 * ee + 64, c, j * 128:(j + 1) * 128]
                        ins = e.matmul(p.ap[:, ee * 512 + c * 64:ee * 512 + (c + 1) * 64], lhsT=lt,
                                       rhs=qT.ap[64 * ee:64 * ee + 64, c, rq * 64:(rq + 1) * 64], start=True, stop=True)
                    return ins
                P.op("pe", f, r=[kwin if kind == "loc" else "kctx", qT], w=[p])
                if kind == "loc":
                    sp_ = sps[ti % 2]
                    P.op("dve", lambda e, p3=p3, sp_=sp_, j=j, jv=jv: e.scalar_tensor_tensor(
                        out=sp_.ap.rearrange("p (e x) -> p e x", e=2), in0=p3, scalar=0.125,
                        in1=btab[:, 2 * j + jv, :].rearrange("p (e x) -> p e x", e=2), op0=ALU.mult, op1=ALU.add),
                        r=[p, "btab"], w=[sp_])
                    P.op("act", lambda e, sp_=sp_, ti=ti, pT=pT: e.activation(out=pT.ap[:, ti, :], in_=sp_.ap, func=AF.Exp),
                         r=[sp_], w=[pT])
                else:
                    P.op("act", lambda e, p3=p3, ti=ti, pT=pT: e.activation(out=pT.ap[:, ti, :].rearrange("p (e x) -> p e x", e=2), in_=p3,
                                                                              func=AF.Exp, scale=0.125), r=[p], w=[pT])
            state[rq] = (pT, tiles, vw)

        def part2(rq):
            pT, tiles, vw = state.pop(rq)
            ntile = len(tiles)
            po, pz = ps(), ps()

            def fo(e, tiles=tiles, po=po, pT=pT, vw=(vw if not is_ctx else None)):
                for h in range(8):
                    c, ee = h // 2, h % 2
                    for ti, (kind, j) in enumerate(tiles):
                        lt = vw.ap[:, j, h * 64:(h + 1) * 64] if kind == "loc" else vctx[:, j, h * 64:(h + 1) * 64]
                        ins = e.matmul(po.ap[64 * ee:64 * ee + 64, c * 64:(c + 1) * 64], lhsT=lt, rhs=pT.ap[:, ti, ee * 256 + c * 64:ee * 256 + (c + 1) * 64],
                                       start=(ti == 0), stop=(ti == len(tiles) - 1), tile_position=(0, 64 * ee))
                return ins
            P.op("pe", fo, r=[pT, "vctx"] + ([vw] if not is_ctx else []), w=[po])

            def fz(e, pz=pz, pT=pT, ntile=ntile):
                for ti in range(ntile):
                    ins = e.matmul(pz.ap[:, 0:512], lhsT=ones_bf[:], rhs=pT.ap[:, ti, :], start=(ti == 0), stop=(ti == ntile - 1))
                return ins
            P.op("pe", fz, r=[pT, "ones_bf"], w=[pz])

            P.op("dve", lambda e, pz=pz: e.reciprocal(out=rz.ap, in_=pz.ap[:, 0:512]), r=[pz], w=[rz])
            for ee in range(2):
                P.op("dve", lambda e, ee=ee, po=po, rq=rq: e.tensor_tensor(
                    out=ycT.ap[64 * ee:64 * ee + 64, :, rq * 64:(rq + 1) * 64],
                    in0=po.ap[64 * ee:64 * ee + 64, 0:256].rearrange("p (c q) -> p c q", c=4),
                    in1=rz.ap[64 * ee:64 * ee + 64, ee * 256:(ee + 1) * 256].rearrange("p (c q) -> p c q", c=4),
                    op=ALU.mult), r=[po, rz], w=[ycT])

        part1(0)
        for rq in range(nrow):
            if rq + 1 < nrow:
                part1(rq + 1)
            part2(rq)

    def pass2_group(l, src, dst, dstkey, tok0, ntok, who, is_ctx, g, ub, hook):
        nt = ntok // 128
        cur["ut"] = ub
        ut = ub
        gu = av(0, [128, 4, 512]); sgA = av(4, [128, 4, 512]); vn = av(8, [128, 4, 512])
        vpre = av(12, [128, 512]); gv = av(13, [128, 512]); ta = av(14, [128, 512])
        w = wload(l, P_AV)
        wv = widev(w)
        for t in range(nt):
            p = ps()

            def f(e, t=t, p=p, wv=wv):
                for k in range(8):
                    ins = e.matmul(p.ap[:, 0:512], lhsT=ut.ap[:, k, t * 128:(t + 1) * 128], rhs=wv[:, k, :], start=(k == 0), stop=(k == 7))
                return ins
            P.op("pe", f, r=[w, ut], w=[p])
            P.op("dve", lambda e, p=p: e.tensor_tensor(out=vpre.ap, in0=p.ap[:, 0:512], in1=bav_bc[:], op=ALU.add), r=[p, "bav_bc"], w=[vpre])
            P.op("act", lambda e: e.activation(out=gv.ap, in_=vpre.ap, func=AF.Gelu), r=[vpre], w=[gv])
            P.op("dve", lambda e: e.bn_stats(out=st6[:, 0:6], in_=gv.ap), r=[gv], w=["st6"])
            P.op("dve", lambda e: e.bn_aggr(out=mv2[:], in_=st6[:, 0:6]), r=["st6"], w=["mv2"])
            P.op("dve", lambda e: e.tensor_scalar(out=rs1[:], in0=mv2[:, 1:2], scalar1=LN_EPS, scalar2=None, op0=ALU.add), r=["mv2"], w=["rs1"])
            P.op("act", lambda e: e.activation(out=rs1[:], in_=rs1[:], func=AF.Sqrt), r=["rs1"], w=["rs1"])
            P.op("dve", lambda e: e.reciprocal(out=rs1[:], in_=rs1[:]), r=["rs1"], w=["rs1"])
            P.op("dve", lambda e, t=t: e.tensor_scalar(out=vn.ap[:, t, :], in0=gv.ap, scalar1=mv2[:, 0:1], scalar2=rs1[:, 0:1],
                                                       op0=ALU.subtract, op1=ALU.mult), r=[gv, "mv2", "rs1"], w=[vn])
            P.op("pool", lambda e, t=t: e.tensor_tensor(out=vn.ap[:, t, :], in0=vn.ap[:, t, :], in1=sg_bc[:], op=ALU.mult), r=[vn, "sg_bc"], w=[vn])
            P.op("pool", lambda e, t=t: e.tensor_tensor(out=vn.ap[:, t, :], in0=vn.ap[:, t, :], in1=sb_bc[:], op=ALU.add), r=[vn, "sb_bc"], w=[vn])
        for pc, dstv, fn in ((P_AU, gu, AF.Gelu), (P_AG, sgA, AF.Silu)):
            w = wload(l, pc)
            for b in range(4):
                p = mm_fm(w, b, ntok)
                P.op("act", lambda e, p=p, b=b, dstv=dstv, fn=fn, pc=pc: e.activation(
                    out=dstv.ap[:, b, 0:ntok], in_=p.ap[:, 0:ntok], func=fn, bias=bcol[:, pc * 4 + b:pc * 4 + b + 1], scale=1.0),
                    r=[p, "bcol"], w=[dstv])
        for g4 in range(4):
            p = ps()

            def f(e, g4=g4, p=p):
                for t in range(nt):
                    ins = e.matmul(p.ap[:, t * 128:(t + 1) * 128], lhsT=vn.ap[:, t, g4 * 128:(g4 + 1) * 128], rhs=wsT[:, g4, :], start=True, stop=True)
                return ins
            P.op("pe", f, r=[vn, "wsT"], w=[p])
            P.op("dve", lambda e, g4=g4, p=p: e.tensor_tensor(
                out=ta.ap[:, 0:ntok].rearrange("p (t q) -> p t q", t=nt), in0=p.ap[:, 0:ntok].rearrange("p (t q) -> p t q", t=nt),
                in1=bs_bc[:, g4 * 128:(g4 + 1) * 128].unsqueeze(1).broadcast_to([128, nt, 128]), op=ALU.add), r=[p, "bs_bc"], w=[ta])
            P.op("dve", lambda e, g4=g4: e.tensor_tensor(out=ta.ap[:, 0:ntok], in0=ta.ap[:, 0:ntok], in1=gu.ap[:, g4, 0:ntok], op=ALU.mult),
                 r=[ta, gu], w=[ta])
            P.op("pool", lambda e, g4=g4: e.tensor_tensor(out=yaT[:, g4, 0:ntok], in0=ta.ap[:, 0:ntok], in1=sgA.ap[:, g4, 0:ntok], op=ALU.mult),
                 r=[ta, sgA], w=["yaT"])
        if stop == "p2a":
            raise _Stop()
        ybl = av(16, [128, 4, 512], BF16)
        tmpbs = [av(12, [128, 512]), av(42, [128, 512])]
        ycol = SEQ if is_ctx else tok0
        for b in range(4):
            P.dma("sp", ybl.ap[:, b, 0:ntok], ybT[b * 128:(b + 1) * 128, ycol:ycol + ntok],
                  r=[("ybT", b, "c") if is_ctx else ("ybT", b, (tok0 // 1024) * 1024)], w=[ybl])
        w = wload(l, P_BG)
        for b in range(4):
            p = mm_fm(w, b, ntok)
            tmpb = tmpbs[b % 2]
            P.op("act", lambda e, p=p, b=b, tmpb=tmpb: e.activation(out=tmpb.ap[:, 0:ntok], in_=p.ap[:, 0:ntok], func=AF.Silu,
                                                          bias=bcol[:, P_BG * 4 + b:P_BG * 4 + b + 1], scale=1.0), r=[p, "bcol"], w=[tmpb])
            P.op("dve", lambda e, b=b, tmpb=tmpb: e.tensor_tensor(out=ybs[:, b, 0:ntok], in0=tmpb.ap[:, 0:ntok], in1=ybl.ap[:, b, 0:ntok], op=ALU.mult),
                 r=[tmpb, ybl], w=["ybs"])
        if stop == "p2b":
            raise _Stop()
        cosv, sinv = av(0, [128, 512]), av(1, [128, 512])
        P.dma("sp", cosv.ap[:, 0:ntok], cosT_in[:, tok0:tok0 + ntok], w=[cosv])
        P.dma("sp", sinv.ap[:, 0:ntok], sinT_in[:, tok0:tok0 + ntok], w=[sinv])
        qT = av(23, [128, 4, 512], BF16)
        ycT = av(18, [128, 4, 512])
        rope_proj(l, P_Q, P_QP, ntok, cosv, sinv, qT.ap, [qT], [av(2, [128, 512]), av(40, [128, 512])], [av(3, [128, 512]), av(41, [128, 512])])
        if stop == "p2q":
            raise _Stop()
        attention(l, g, tok0, ntok, is_ctx, qT, ycT)
        if stop == "p2c":
            raise _Stop()
        if "dbgy" in dbg and l == 0 and (is_ctx or g == 0):
            nm = "c" if is_ctx else "l"
            dya = nc.dram_tensor("dbg_ya_" + nm, [512, ntok], F32, kind="ExternalOutput").ap()
            dyc = nc.dram_tensor("dbg_yc_" + nm, [512, ntok], F32, kind="ExternalOutput").ap()
            dyb = nc.dram_tensor("dbg_yb_" + nm, [512, ntok], F32, kind="ExternalOutput").ap()
            for b in range(4):
                P.dma("sp", dya[b * 128:(b + 1) * 128, :], yaT[:, b, 0:ntok], r=["yaT"], w=[("dbg", nm, 0, b)])
                P.dma("sp", dyc[b * 128:(b + 1) * 128, :], ycT.ap[:, b, 0:ntok], r=[ycT], w=[("dbg", nm, 1, b)])
                P.dma("sp", dyb[b * 128:(b + 1) * 128, :], ybs[:, b, 0:ntok], r=["ybs"], w=[("dbg", nm, 2, b)])
        tmpcs = [av(22, [128, 512]), av(43, [128, 512])]
        w = wload(l, P_CG)
        for b in range(4):
            p = mm_fm(w, b, ntok)
            tmpc = tmpcs[b % 2]
            P.op("act", lambda e, p=p, b=b, tmpc=tmpc: e.activation(out=tmpc.ap[:, 0:ntok], in_=p.ap[:, 0:ntok], func=AF.Silu,
                                                          bias=bcol[:, P_CG * 4 + b:P_CG * 4 + b + 1], scale=1.0), r=[p, "bcol"], w=[tmpc])
            P.op("dve", lambda e, b=b, tmpc=tmpc: e.tensor_tensor(out=ycs[:, b, 0:ntok], in0=tmpc.ap[:, 0:ntok], in1=ycT.ap[:, b, 0:ntok], op=ALU.mult),
                 r=[tmpc, ycT], w=["ycs"])
        hook()
        cur["ut"] = ub
        sgs = [av(0, [128, 512]), av(1, [128, 512])]
        tms = [av(2, [128, 512]), av(3, [128, 512])]
        macc = av(4, [128, 4, 512])
        ys = [(yaT, "yaT"), (ybs, "ybs"), (ycs, "ycs")]
        cnt = 0
        for hh_ in range(2):
            for n in range(3):
                pcg = P_GM + hh_ * 3 + n
                wg = wload(l, pcg)
                wbr = wload(l, P_WBR + hh_ * 3 + n, 2048)
                wbv = widev(wbr, 4)
                yt, yk = ys[n]
                for jj in range(4):
                    j = hh_ * 4 + jj
                    pg = mm_fm(wg, jj, ntok)
                    pb = ps()

                    def f(e, pb=pb, jj=jj, yt=yt, wbv=wbv):
                        for k in range(4):
                            ins = e.matmul(pb.ap[:, 0:ntok], lhsT=wbv[:, k, jj * 128:(jj + 1) * 128], rhs=yt[:, k, 0:ntok], start=(k == 0), stop=(k == 3))
                        return ins
                    P.op("pe", f, r=[wbr, yk], w=[pb])
                    sg = sgs[cnt % 2]
                    tm = tms[cnt % 2]
                    cnt += 1
                    P.op("act", lambda e, pg=pg, sg=sg, jj=jj, pcg=pcg: e.activation(
                        out=sg.ap[:, 0:ntok], in_=pg.ap[:, 0:ntok], func=AF.Sigmoid, bias=bcol[:, pcg * 4 + jj:pcg * 4 + jj + 1], scale=1.0),
                        r=[pg, "bcol"], w=[sg])
                    if n == 0:
                        P.op("dve", lambda e, sg=sg, pb=pb, jj=jj: e.tensor_tensor(out=macc.ap[:, jj, 0:ntok], in0=sg.ap[:, 0:ntok], in1=pb.ap[:, 0:ntok], op=ALU.mult),
                             r=[sg, pb], w=[macc])
                    else:
                        P.op("dve", lambda e, sg=sg, pb=pb, tm=tm: e.tensor_tensor(out=tm.ap[:, 0:ntok], in0=sg.ap[:, 0:ntok], in1=pb.ap[:, 0:ntok], op=ALU.mult),
                             r=[sg, pb], w=[tm])
                        if n == 1:
                            P.op("pool", lambda e, tm=tm, jj=jj: e.tensor_tensor(out=macc.ap[:, jj, 0:ntok], in0=macc.ap[:, jj, 0:ntok], in1=tm.ap[:, 0:ntok], op=ALU.add),
                                 r=[tm, macc], w=[macc])
                        else:
                            P.op("pool", lambda e, tm=tm, jj=jj, j=j: e.tensor_tensor(out=mT[:, j, 0:ntok], in0=macc.ap[:, jj, 0:ntok], in1=tm.ap[:, 0:ntok], op=ALU.add),
                                 r=[tm, macc], w=["mT"])
        if stop == "p2d":
            raise _Stop()
        xres = av(8, [128, 4, 1024])
        tmo = [av(16, [128, 512]), av(17, [128, 512])]
        xn = [av(18, [128, 1024]), av(20, [128, 1024])]
        for t in range(nt):
            P.dma("sp", xres.ap[:, t, :], src[t * 128:(t + 1) * 128, :], r=[("src", id(src.tensor), t)], w=[xres])
        cnt = 0
        for half in range(2):
            w = wload(l, P_WOUT + half)
            wv = widev(w)
            for t in range(nt):
                p = ps()

                def f(e, t=t, p=p, wv=wv):
                    for k in range(8):
                        ins = e.matmul(p.ap[:, 0:512], lhsT=mT[:, k, t * 128:(t + 1) * 128], rhs=wv[:, k, :], start=(k == 0), stop=(k == 7))
                    return ins
                P.op("pe", f, r=[w, "mT"], w=[p])
                tm = tmo[cnt % 2]
                cnt += 1
                P.op("dve", lambda e, p=p, tm=tm, half=half: e.tensor_tensor(out=tm.ap, in0=p.ap[:, 0:512], in1=gate_bc[:, who, half * 512:(half + 1) * 512], op=ALU.mult),
                     r=[p, "gate_bc"], w=[tm])
                P.op("dve", lambda e, tm=tm, t=t, half=half: e.scalar_tensor_tensor(
                    out=xres.ap[:, t, half * 512:(half + 1) * 512], in0=xres.ap[:, t, half * 512:(half + 1) * 512], scalar=float(ALPHA), in1=tm.ap,
                    op0=ALU.mult, op1=ALU.add), r=[tm, xres], w=[xres])
        for t in range(nt):
            xo = xn[t % 2]
            P.op("dve", lambda e, t=t: e.bn_stats(out=st6[:, 0:6], in_=xres.ap[:, t, 0:512]), r=[xres], w=["st6"])
            P.op("dve", lambda e, t=t: e.bn_stats(out=st6[:, 6:12], in_=xres.ap[:, t, 512:1024]), r=[xres, "st6"], w=["st6"])
            P.op("dve", lambda e: e.bn_aggr(out=mv2[:], in_=st6[:, 0:12]), r=["st6"], w=["mv2"])
            P.op("dve", lambda e: e.tensor_scalar(out=rs1[:], in0=mv2[:, 1:2], scalar1=LN_EPS, scalar2=None, op0=ALU.add), r=["mv2"], w=["rs1"])
            P.op("act", lambda e: e.activation(out=rs1[:], in_=rs1[:], func=AF.Sqrt), r=["rs1"], w=["rs1"])
            P.op("dve", lambda e: e.reciprocal(out=rs1[:], in_=rs1[:]), r=["rs1"], w=["rs1"])
            P.op("dve", lambda e, t=t, xo=xo: e.tensor_scalar(out=xo.ap, in0=xres.ap[:, t, :], scalar1=mv2[:, 0:1], scalar2=rs1[:, 0:1],
                                                              op0=ALU.subtract, op1=ALU.mult), r=[xres, "mv2", "rs1"], w=[xo])
            P.op("pool", lambda e, xo=xo: e.tensor_tensor(out=xo.ap, in0=xo.ap, in1=lng_bc[:], op=ALU.mult), r=[xo, "lng_bc"], w=[xo])
            P.op("pool", lambda e, xo=xo: e.tensor_tensor(out=xo.ap, in0=xo.ap, in1=lnb_bc[:], op=ALU.add), r=[xo, "lnb_bc"], w=[xo])
            P.dma("pool", dst[t * 128:(t + 1) * 128, :], xo.ap, r=[xo], w=[(dstkey, id(dst.tensor), t)])
        if stop == "p2e" or (stop == "p2g" and not is_ctx):
            raise _Stop()

    nl = len(layers)
    try:
      for li, l in enumerate(layers):
          last = (l == DEPTH - 1)
          with_ctx = not last
          xs = x_in if l == 0 else x1
          cs = ctx_in if l == 0 else xc1
          xd = out_d if last else x1
          layer_setup(l)
          if stop == "setup":
              break
          g1 = [(cs, SEQ, CTX, 1, 16)] + [(xs[g * 512:(g + 1) * 512, :], g * 512, 512, 0, g) for g in range(16)]
          load_uT(g1[0][0], g1[0][2], g1[0][3], UTB[0])
          for i, (src_, tok0_, ntok_, who_, g_) in enumerate(g1):
              def hook1(i=i):
                  if i + 1 < len(g1):
                      load_uT(g1[i + 1][0], g1[i + 1][2], g1[i + 1][3], UTB[(i + 1) % 2])
              pass1_group(l, src_, tok0_, ntok_, who_, g_, UTB[i % 2], hook1)
          if stop == "pass1":
              break
          lru_pass(l, with_ctx)
          if stop == "lru":
              break
          P.dma("sp", kctx[:], KT.rearrange("(b p) t -> p b t", p=128)[:, :, SEQ:NTOK], r=[("KT", 16)], w=["kctx"])
          P.dma("sp", vctx[:], Vs[SEQ:NTOK, :].rearrange("(j p) c -> p j c", p=128), r=[("V", 16)], w=["vctx"])
          g2 = ([(cs, xc1, SEQ, CTX, 1, True, 16)] if with_ctx else []) + \
               [(xs[g * 512:(g + 1) * 512, :], xd[g * 512:(g + 1) * 512, :], g * 512, 512, 0, False, g) for g in range(16)]
          load_uT(g2[0][0], g2[0][3], g2[0][4], UTB[0])
          for i, (src_, dst_, tok0_, ntok_, who_, isc_, g_) in enumerate(g2):
              def hook2(i=i):
                  if i + 1 < len(g2):
                      load_uT(g2[i + 1][0], g2[i + 1][3], g2[i + 1][4], UTB[(i + 1) % 2])
              pass2_group(l, src_, dst_, "src", tok0_, ntok_, who_, isc_, g_, UTB[i % 2], hook2)
    except _Stop:
        pass
    P.finish()
    P.emit()
    return nc, P


_CACHE = {}


def kernel(**inputs):
    inp = {k: np.asarray(v) for k, v in inputs.items()}
    sh = prep_shared(inp)

    def fl(a, n):
        return np.ascontiguousarray(a.reshape(a.shape[:n] + (-1,)))

    shared = dict(wcat=sh["wcat"], wada=sh["wada"], bcol=sh["bcol"], rows=sh["rows"], wsT=fl(sh["wsT"], 2),
                  lruW=fl(sh["lruW"], 2), lcol=fl(sh["lcol"], 2), ccol=fl(sh["ccol"], 2), adac=sh["adac"],
                  btab=fl(sh["btab"], 2), cosT=sh["cosT"], sinT=sh["sinT"], ident=sh["ident"])
    in_maps = []
    for b in range(NB):
        m = prep_core(inp, b)
        m.update(shared)
        in_maps.append(m)
    if "nc" not in _CACHE:
        _CACHE["nc"] = build()[0]
    res = run_bass_kernel_spmd(_CACHE["nc"], in_maps, core_ids=list(range(NB)))
    return np.stack([np.asarray(r["out"], dtype=np.float32) for r in res.results], axis=0)
```
